# Optimizing a Trainium2 kernel written in Bass

```python
import math
import jax, jax.numpy as jnp
from jax import lax
import numpy as np

D_MODEL = 1024
BATCH = 4
SEQ = 4096
DEPTH = 4

N_MIXERS = 4
MIXER_WIDTH = D_MODEL // 4
D_MIX = N_MIXERS * MIXER_WIDTH
SGU_HEADS = 4
SGU_CHUNK = 128
S5_GROUP = 16
S5_GROUPS = MIXER_WIDTH // S5_GROUP
S5_STATE = 64
S5_DT_MIN = 1e-3
S5_DT_MAX = 1e-1
LRU_HEADS = 4
LRU_CONV = 4
LRU_C = 8.0
FOX_HEADS = 4
FOX_HEAD_DIM = MIXER_WIDTH // FOX_HEADS
ATTN_BLOCK = 128
D_FF = 4 * D_MODEL
RMS_EPS = 1e-6
D_IN_PROJ = 8 * MIXER_WIDTH + FOX_HEADS

kernel_name = "hybrid_parallel_heads_sgu_s5_rglru_fox"


def rms_norm(x, g):
    xf = x.astype(jnp.float32)
    y = xf * lax.rsqrt(jnp.mean(jnp.square(xf), axis=-1, keepdims=True) + RMS_EPS)
    return (y * g.astype(jnp.float32)).astype(x.dtype)


def group_rms_norm(y, g):
    bsz, seq, _ = y.shape
    yf = y.astype(jnp.float32).reshape(bsz, seq, N_MIXERS, MIXER_WIDTH)
    yf = yf * lax.rsqrt(jnp.mean(jnp.square(yf), axis=-1, keepdims=True) + RMS_EPS)
    return (yf.reshape(bsz, seq, D_MIX) * g.astype(jnp.float32)).astype(y.dtype)


def _linear_combine(e1, e2):
    a1, b1 = e1
    a2, b2 = e2
    return a1 * a2, a2 * b1 + b2


def _complex_linear_combine(e1, e2):
    a1r, a1i, b1r, b1i = e1
    a2r, a2i, b2r, b2i = e2
    ar = a2r * a1r - a2i * a1i
    ai = a2r * a1i + a2i * a1r
    br = a2r * b1r - a2i * b1i + b2r
    bi = a2r * b1i + a2i * b1r + b2i
    return ar, ai, br, bi


def sgu_mixer(u, v, norm_g, w_s, b_s):
    bsz, seq, _ = u.shape
    u = jax.nn.gelu(u)
    v = rms_norm(jax.nn.gelu(v), norm_g)
    mask = jnp.tril(jnp.ones((SGU_CHUNK, SGU_CHUNK), w_s.dtype))
    vh = v.reshape(bsz, seq // SGU_CHUNK, SGU_CHUNK, SGU_HEADS, MIXER_WIDTH // SGU_HEADS)
    mixed = jnp.einsum('hts,bcshd->bcthd', w_s * mask, vh) + b_s.T[None, None, :, :, None]
    return u * mixed.reshape(bsz, seq, MIXER_WIDTH)


def s5_mixer(u, lam_re, lam_im, log_dt, b_re, b_im, c_re, c_im, d, glu_w, glu_b):
    bsz, seq, _ = u.shape
    f32 = jnp.float32
    uf = u.astype(f32)
    lam_re = lam_re.astype(f32)
    lam_im = lam_im.astype(f32)
    dt = jnp.exp(log_dt.astype(f32))[:, None]
    mag = jnp.exp(lam_re * dt)
    abar_re = mag * jnp.cos(lam_im * dt)
    abar_im = mag * jnp.sin(lam_im * dt)
    denom = jnp.square(lam_re) + jnp.square(lam_im)
    num_re = abar_re - 1.0
    num_im = abar_im
    fac_re = (num_re * lam_re + num_im * lam_im) / denom
    fac_im = (num_im * lam_re - num_re * lam_im) / denom
    b_re = b_re.astype(f32)
    b_im = b_im.astype(f32)
    bbar_re = fac_re[..., None] * b_re - fac_im[..., None] * b_im
    bbar_im = fac_re[..., None] * b_im + fac_im[..., None] * b_re
    ug = uf.reshape(bsz, seq, S5_GROUPS, S5_GROUP)
    bu_re = jnp.einsum('blgh,gph->blgp', ug, bbar_re)
    bu_im = jnp.einsum('blgh,gph->blgp', ug, bbar_im)
    a_re = jnp.broadcast_to(abar_re, bu_re.shape)
    a_im = jnp.broadcast_to(abar_im, bu_im.shape)
    _, _, s_re, s_im = lax.associative_scan(
        _complex_linear_combine, (a_re, a_im, bu_re, bu_im), axis=1)
    y = (jnp.einsum('blgp,ghp->blgh', s_re, c_re.astype(f32))
         - jnp.einsum('blgp,ghp->blgh', s_im, c_im.astype(f32)))
    y = y.reshape(bsz, seq, MIXER_WIDTH) + d.astype(f32) * uf
    y = jax.nn.gelu(y)
    y = y * jax.nn.sigmoid(y @ glu_w.astype(f32) + glu_b.astype(f32))
    return y.astype(u.dtype)


def causal_depthwise_conv(x, w, b):
    out = lax.conv_general_dilated(
        x, w[:, None, :], window_strides=(1,), padding=[(LRU_CONV - 1, 0)],
        dimension_numbers=('NWC', 'WIO', 'NWC'), feature_group_count=x.shape[-1])
    return out + b


def rglru_mixer(x, gate, conv_w, conv_b, wa, ba, wx, bx, lam):
    bsz, seq, _ = x.shape
    f32 = jnp.float32
    xc = causal_depthwise_conv(x, conv_w, conv_b)
    xh = xc.reshape(bsz, seq, LRU_HEADS, MIXER_WIDTH // LRU_HEADS)
    r = jax.nn.sigmoid(jnp.einsum('blhi,hij->blhj', xh, wa) + ba).reshape(bsz, seq, MIXER_WIDTH)
    i = jax.nn.sigmoid(jnp.einsum('blhi,hij->blhj', xh, wx) + bx).reshape(bsz, seq, MIXER_WIDTH)
    log_a = -LRU_C * r.astype(f32) * jax.nn.softplus(-lam.astype(f32))
    a = jnp.exp(log_a)
    b = jnp.sqrt(-jnp.expm1(2.0 * log_a)) * (i * xc).astype(f32)
    _, h = lax.associative_scan(_linear_combine, (a, b), axis=1)
    return h.astype(x.dtype) * jax.nn.gelu(gate)


def forgetting_attention(q, k, v, log_f):
    bsz, seq, n_heads, hd = q.shape
    c = jnp.cumsum(log_f, axis=1).transpose(0, 2, 1)
    scale = hd ** -0.5
    neg = jnp.finfo(jnp.float32).min
    out_blocks = []
    for blk in range(seq // ATTN_BLOCK):
        q0, q1 = blk * ATTN_BLOCK, (blk + 1) * ATTN_BLOCK
        s = jnp.einsum('bqhd,bkhd->bhqk', q[:, q0:q1], k[:, :q1],
                       preferred_element_type=jnp.float32) * scale
        s = s + c[:, :, q0:q1, None] - c[:, :, None, :q1]
        q_pos = jnp.arange(q0, q1)[:, None]
        k_pos = jnp.arange(q1)[None, :]
        s = jnp.where(k_pos <= q_pos, s, neg)
        p = jax.nn.softmax(s, axis=-1).astype(v.dtype)
        out_blocks.append(jnp.einsum('bhqk,bkhd->bqhd', p, v[:, :q1]))
    return jnp.concatenate(out_blocks, axis=1)


def fox_mixer(q, k, v, f_logit, b_f):
    bsz, seq, _ = q.shape
    shp = (bsz, seq, FOX_HEADS, FOX_HEAD_DIM)
    log_f = jax.nn.log_sigmoid((f_logit + b_f).astype(jnp.float32))
    o = forgetting_attention(q.reshape(shp), k.reshape(shp), v.reshape(shp), log_f)
    return o.reshape(bsz, seq, MIXER_WIDTH)


def setup_inputs(seed: int = 0) -> dict:
    key = jax.random.key(seed)
    ks = iter(jax.random.split(key, 40))
    f32 = jnp.float32

    def nrm(shape, scale):
        return jax.random.normal(next(ks), shape, f32) * scale

    def gain(shape):
        return 1.0 + 0.02 * jax.random.normal(next(ks), shape, f32)

    L = DEPTH
    W = MIXER_WIDTH
    hd_lru = W // LRU_HEADS
    x = jax.random.normal(next(ks), (BATCH, SEQ, D_MODEL), f32)
    norm1_g = gain((L, D_MODEL))
    w_in = nrm((L, D_MODEL, D_IN_PROJ), D_MODEL ** -0.5)
    sgu_norm_g = gain((L, W))
    sgu_w = nrm((L, SGU_HEADS, SGU_CHUNK, SGU_CHUNK), SGU_CHUNK ** -0.5)
    sgu_b = 1.0 + 0.1 * jax.random.normal(next(ks), (L, SGU_HEADS, SGU_CHUNK), f32)
    s5_lambda_re = -0.5 + 0.01 * jax.random.normal(next(ks), (L, S5_GROUPS, S5_STATE), f32)
    s5_lambda_im = (jnp.pi * jnp.arange(S5_STATE, dtype=f32))[None, None, :] \
        + 0.01 * jax.random.normal(next(ks), (L, S5_GROUPS, S5_STATE), f32)
    s5_log_dt = jax.random.uniform(next(ks), (L, S5_GROUPS), f32,
                                   minval=math.log(S5_DT_MIN), maxval=math.log(S5_DT_MAX))
    s5_b_re = nrm((L, S5_GROUPS, S5_STATE, S5_GROUP), (2.0 * S5_GROUP) ** -0.5)
    s5_b_im = nrm((L, S5_GROUPS, S5_STATE, S5_GROUP), (2.0 * S5_GROUP) ** -0.5)
    s5_c_re = nrm((L, S5_GROUPS, S5_GROUP, S5_STATE), (2.0 * S5_STATE) ** -0.5)
    s5_c_im = nrm((L, S5_GROUPS, S5_GROUP, S5_STATE), (2.0 * S5_STATE) ** -0.5)
    s5_d = nrm((L, W), 0.5)
    s5_glu_w = nrm((L, W, W), W ** -0.5)
    s5_glu_b = nrm((L, W), 0.01)
    lru_conv_w = nrm((L, LRU_CONV, W), LRU_CONV ** -0.5)
    lru_conv_b = nrm((L, W), 0.01)
    lru_wa = nrm((L, LRU_HEADS, hd_lru, hd_lru), hd_lru ** -0.5)
    lru_ba = nrm((L, LRU_HEADS, hd_lru), 0.01)
    lru_wx = nrm((L, LRU_HEADS, hd_lru, hd_lru), hd_lru ** -0.5)
    lru_bx = nrm((L, LRU_HEADS, hd_lru), 0.01)
    a_pow = jax.random.uniform(next(ks), (L, W), f32, minval=0.9, maxval=0.999)
    sig = a_pow ** (1.0 / LRU_C)
    lru_lambda = jnp.log(sig) - jnp.log1p(-sig)
    fox_fgate_b = 2.0 + 0.1 * jax.random.normal(next(ks), (L, FOX_HEADS), f32)
    mix_norm_g = gain((L, D_MIX))
    w_out = nrm((L, D_MIX, D_MODEL), D_MIX ** -0.5)
    norm2_g = gain((L, D_MODEL))
    w_mlp_in = nrm((L, D_MODEL, D_FF), D_MODEL ** -0.5)
    w_mlp_out = nrm((L, D_FF, D_MODEL), D_FF ** -0.5)
    final_g = gain((D_MODEL,))
    return {
        "x": x, "norm1_g": norm1_g, "w_in": w_in,
        "sgu_norm_g": sgu_norm_g, "sgu_w": sgu_w, "sgu_b": sgu_b,
        "s5_lambda_re": s5_lambda_re, "s5_lambda_im": s5_lambda_im, "s5_log_dt": s5_log_dt,
        "s5_b_re": s5_b_re, "s5_b_im": s5_b_im, "s5_c_re": s5_c_re, "s5_c_im": s5_c_im,
        "s5_d": s5_d, "s5_glu_w": s5_glu_w, "s5_glu_b": s5_glu_b,
        "lru_conv_w": lru_conv_w, "lru_conv_b": lru_conv_b, "lru_wa": lru_wa, "lru_ba": lru_ba,
        "lru_wx": lru_wx, "lru_bx": lru_bx, "lru_lambda": lru_lambda,
        "fox_fgate_b": fox_fgate_b,
        "mix_norm_g": mix_norm_g, "w_out": w_out, "norm2_g": norm2_g,
        "w_mlp_in": w_mlp_in, "w_mlp_out": w_mlp_out, "final_g": final_g,
    }


def reference(x, norm1_g, w_in, sgu_norm_g, sgu_w, sgu_b,
              s5_lambda_re, s5_lambda_im, s5_log_dt, s5_b_re, s5_b_im, s5_c_re, s5_c_im,
              s5_d, s5_glu_w, s5_glu_b,
              lru_conv_w, lru_conv_b, lru_wa, lru_ba, lru_wx, lru_bx, lru_lambda,
              fox_fgate_b, mix_norm_g, w_out, norm2_g, w_mlp_in, w_mlp_out, final_g):
    split_points = [MIXER_WIDTH * i for i in range(1, 9)]
    for l in range(DEPTH):
        h = rms_norm(x, norm1_g[l])
        z = h @ w_in[l]
        a_u, a_v, b_in, c_x, c_gate, d_q, d_k, d_v, d_f = jnp.split(z, split_points, axis=-1)
        y_a = sgu_mixer(a_u, a_v, sgu_norm_g[l], sgu_w[l], sgu_b[l])
        y_b = s5_mixer(b_in, s5_lambda_re[l], s5_lambda_im[l], s5_log_dt[l],
                       s5_b_re[l], s5_b_im[l], s5_c_re[l], s5_c_im[l],
                       s5_d[l], s5_glu_w[l], s5_glu_b[l])
        y_c = rglru_mixer(c_x, c_gate, lru_conv_w[l], lru_conv_b[l],
                          lru_wa[l], lru_ba[l], lru_wx[l], lru_bx[l], lru_lambda[l])
        y_d = fox_mixer(d_q, d_k, d_v, d_f, fox_fgate_b[l])
        y = jnp.concatenate([y_a, y_b, y_c, y_d], axis=-1)
        y = group_rms_norm(y, mix_norm_g[l])
        x = x + y @ w_out[l]
        h = rms_norm(x, norm2_g[l])
        x = x + jnp.square(jax.nn.relu(h @ w_mlp_in[l])) @ w_mlp_out[l]
    return rms_norm(x, final_g)
```

```python
import math
from contextlib import ExitStack
import numpy as np
import concourse.bass as bass
import concourse.mybir as mybir
from concourse.bass_utils import run_bass_kernel_spmd

F32 = mybir.dt.float32
BF16 = mybir.dt.bfloat16
AF = mybir.ActivationFunctionType
ALU = mybir.AluOpType

D = 1024
S = 4096
L = 4
NCORE = 4
TB = 512
NB = S // TB
TF = 1024
NBF = S // TF
DIN = 2052
EPS = 1e-6
PI = math.pi
NEG = -30000.0
NDS = 24
GELU = AF.Gelu
NOSYNC_SAME = ()


class K:
    def __init__(s, nc, es):
        s.nc = nc
        s.es = es
        s.epoch = 0
        s.engs = {'pe': nc.tensor, 'act': nc.scalar, 'dve': nc.vector, 'pool': nc.gpsimd, 'sp': nc.sync}
        s.sem = {e: es.enter_context(nc.semaphore('s_' + e)) for e in s.engs}
        s.cnt = {e: 0 for e in s.engs}
        s.dsem = [es.enter_context(nc.semaphore('d%d' % i)) for i in range(NDS)]
        s.dcnt = [0] * NDS
        s.dnext = 0
        s.seen = {e: {} for e in s.engs}
        s.lastw = {}
        s.readers = {}
        s.psn = 0
        s.xsem = {}
        s.xcnt = {}
        s.xn = 0

    def _wait(s, e, sk, val):
        if e == sk and (e == 'pe' or e in NOSYNC_SAME):
            return
        if s.seen[e].get(sk, 0) >= val:
            return
        if isinstance(sk, str):
            sem = s.sem[sk]
        elif isinstance(sk, int):
            sem = s.dsem[sk]
        else:
            sem = s.xsem[sk]
        s.engs[e].wait_ge(sem, val)
        s.seen[e][sk] = val

    def deps(s, e, reads, writes):
        for k in reads:
            if k in s.lastw:
                s._wait(e, *s.lastw[k])
        for k in writes:
            if k in s.lastw:
                s._wait(e, *s.lastw[k])
            for sk, v in s.readers.get(k, {}).items():
                s._wait(e, sk, v)

    def _record(s, tok, reads, writes):
        for k in reads:
            d = s.readers.setdefault(k, {})
            d[tok[0]] = max(d.get(tok[0], 0), tok[1])
        for k in writes:
            s.lastw[k] = tok
            s.readers[k] = {}

    def op(s, e, reads, writes, fn):
        s.deps(e, reads, writes)
        ins = fn(s.engs[e])
        s.cnt[e] += 1
        ins.then_inc(s.sem[e], 1)
        s._record((e, s.cnt[e]), reads, writes)

    def dma(s, q, out, in_, reads, writes):
        i = s.dnext
        s.dnext = (i + 1) % NDS
        if s.dcnt[i] > 0:
            s._wait(q, i, s.dcnt[i])
        s.deps(q, reads, writes)
        s.dcnt[i] += 16
        s.engs[q].dma_start(out=out, in_=in_).then_inc(s.dsem[i], 16)
        s._record((i, s.dcnt[i]), reads, writes)

    def dma_sw(s, out, in_, reads, writes, **kw):
        key = ('x', s.xn % 4)
        s.xn += 1
        if key not in s.xsem:
            s.xsem[key] = s.es.enter_context(s.nc.semaphore('x%d' % key[1]))
            s.xcnt[key] = 0
        s.deps('pool', reads, writes)
        s.xcnt[key] += 16
        s.engs['pool'].dma_start(out=out, in_=in_, **kw).then_inc(s.xsem[key], 16)
        s._record((key, s.xcnt[key]), reads, writes)
        return (key, s.xcnt[key])

    def barrier(s):
        for e in s.engs:
            for f in s.engs:
                if f != e and s.cnt[f] > 0:
                    s._wait(e, f, s.cnt[f])
            for i in range(NDS):
                if s.dcnt[i] > 0:
                    s._wait(e, i, s.dcnt[i])
        s.lastw = {k_: v_ for k_, v_ in s.lastw.items() if isinstance(v_[0], tuple)}
        s.readers = {}
        s.epoch += 1
        for e in s.engs:
            s.sem[e] = s.es.enter_context(s.nc.semaphore('s_%s_%d' % (e, s.epoch)))
            s.cnt[e] = 0
        for e in s.engs:
            for f in s.engs:
                s.seen[e].pop(f, None)


def build(nl=L):
    nc = bass.Bass("TRN2", target_bir_lowering=False)

    def din(name, shape):
        return nc.dram_tensor(name, list(shape), F32, kind="ExternalInput").ap()

    xT_in = din("xT", [D, S])
    w_in_t = din("w_in_t", [L, 8, 128, 2048])
    wif_t = din("wif_t", [L, 128, 32])
    woa_t = din("woa_t", [L, 8, 128, 768])
    wod_t = din("wod_t", [L, 8, 64, 512])
    w1_t = din("w1_t", [L, 8, 128, 4096])
    w2_t = din("w2_t", [L, 8, 128, 4096])
    wbin = nc.dram_tensor("wbin", [L, 8, 128, 2048], BF16).ap()
    wbif = nc.dram_tensor("wbif", [L, 128, 32], BF16).ap()
    wboa = nc.dram_tensor("wboa", [L, 8, 128, 768], BF16).ap()
    wbod = nc.dram_tensor("wbod", [L, 8, 64, 512], BF16).ap()
    wb1 = nc.dram_tensor("wb1", [L, 8, 128, 4096], BF16).ap()
    wb2 = nc.dram_tensor("wb2", [L, 8, 128, 4096], BF16).ap()
    g1_d = din("g1", [128, L, 8])
    g2_d = din("g2", [128, L, 8])
    gma_d = din("gma", [128, L, 6])
    gmd_d = din("gmd", [64, L, 4])
    gf_d = din("gf", [128, 8])
    sgug_d = din("sgug", [128, L, 256])
    sguw_d = din("sguw", [L, 128, 4, 128])
    sgub_d = din("sgub", [128, L, 2, 128])
    lre_d = din("lre", [128, L, 8])
    lim_d = din("lim", [128, L, 8])
    ldt_d = din("ldt", [128, L, 8])
    bre_d = din("bre", [L, 128, 8, 128])
    bim_d = din("bim", [L, 128, 8, 128])
    cre_d = din("cre", [L, 128, 8, 128])
    cim_d = din("cim", [L, 128, 8, 128])
    s5d_d = din("s5d", [128, L, 2])
    gluw_d = din("gluw", [L, 256, 256])
    glub_d = din("glub", [128, L, 2])
    cw_d = din("cw", [128, L, 2, 4])
    cb_d = din("cb", [128, L, 2])
    wa_d = din("wa", [L, 128, 2, 128])
    wx_d = din("wx", [L, 128, 2, 128])
    ba_d = din("ba", [128, L, 2])
    bx_d = din("bx", [128, L, 2])
    llam_d = din("llam", [128, L, 2])
    bf_d = din("bfr", [128, L, 16])
    ident_d = din("ident", [128, 128])
    tri_d = din("tri", [128, 128])
    amask_d = din("amask", [128, 4, 512])
    iota_d = din("iota", [128, 512])
    outT = nc.dram_tensor("outT", [D, S], F32, kind="ExternalOutput").ap()
    xs = nc.dram_tensor("xs", [D, S], F32).ap()
    xs1 = nc.dram_tensor("xs1", [D, S], F32).ap()

    def fm(ap2d, t0, n):
        return ap2d[:, t0:t0 + n].rearrange("(c p) t -> p c t", p=128)

    with ExitStack() as es:
        E = es.enter_context
        k = K(nc, es)
        PS = [E(nc.psum_tensor("ps%d" % i, [128, 512], F32)) for i in range(8)]

        def psum():
            i = k.psn
            k.psn = (i + 1) % 6
            return PS[i], 'ps%d' % i

        def psfix(i):
            return PS[i], 'ps%d' % i

        uid = [0]

        def sb(stack, name, shape, dt=F32):
            uid[0] += 1
            return stack.enter_context(nc.sbuf_tensor("sb_%s_%d" % (name, uid[0]), list(shape), dt))

        g1 = sb(es, "g1", [128, L, 8]); g2 = sb(es, "g2", [128, L, 8])
        gma = sb(es, "gma", [128, L, 6]); gmd = sb(es, "gmd", [64, L, 4]); gf = sb(es, "gf", [128, 8])
        lre = sb(es, "lre", [128, L, 8]); lim = sb(es, "lim", [128, L, 8]); ldt = sb(es, "ldt", [128, L, 8])
        s5d = sb(es, "s5d", [128, L, 2]); glub = sb(es, "glub", [128, L, 2])
        cw = sb(es, "cw", [128, L, 2, 4]); cb = sb(es, "cb", [128, L, 2])
        ba = sb(es, "ba", [128, L, 2]); bx = sb(es, "bx", [128, L, 2]); llam = sb(es, "llam", [128, L, 2])
        bfr = sb(es, "bfr", [128, L, 16])
        ident_f = sb(es, "ident_f", [128, 128]); ident_b = sb(es, "ident_b", [128, 128], BF16)
        tri_f = sb(es, "tri_f", [128, 128])
        ones_f = sb(es, "ones_f", [128, 128]); ones_b = sb(es, "ones_b", [128, 128], BF16)
        for t, d_, nm in [(g1, g1_d, 'g1'), (g2, g2_d, 'g2'), (gma, gma_d, 'gma'), (gmd, gmd_d, 'gmd'),
                          (gf, gf_d, 'gf'), (lre, lre_d, 'lre'), (lim, lim_d, 'lim'), (ldt, ldt_d, 'ldt'),
                          (s5d, s5d_d, 's5d'), (glub, glub_d, 'glub'), (cw, cw_d, 'cw'), (cb, cb_d, 'cb'),
                          (ba, ba_d, 'ba'), (bx, bx_d, 'bx'), (llam, llam_d, 'llam'), (bfr, bf_d, 'bfr'),
                          (ident_f, ident_d, 'ident_f'), (tri_f, tri_d, 'tri_f')]:
            k.dma('sp', t[:], d_, [], [nm])
        conv_hist = []
        conv_hist.append(k.dma_sw(ident_b[:], ident_d, [], ['ident_b']))

        def f2k(ap):
            names = " ".join("d%d" % i for i in range(len(ap.shape)))
            return ap.rearrange("%s -> (%s)" % (names, names)).rearrange("(r c) -> r c", c=2048)
        def conv1(out, in_, key):
            if len(conv_hist) >= 2:
                k._wait('pool', *conv_hist[-2])
            conv_hist.append(k.dma_sw(out, in_, [], [key]))

        def conv(l_):
            conv1(f2k(wbin[l_]), f2k(w_in_t[l_]), ('wbin', l_))
            conv1(wbif[l_], wif_t[l_], ('wbif', l_))
            conv1(f2k(wboa[l_]), f2k(woa_t[l_]), ('wboa', l_))
            conv1(f2k(wbod[l_]), f2k(wod_t[l_]), ('wbod', l_))
            for hh_ in range(2):
                conv1(f2k(wb1[l_, 4 * hh_:4 * hh_ + 4]), f2k(w1_t[l_, 4 * hh_:4 * hh_ + 4]), ('wb1', l_, hh_))
            for hh_ in range(2):
                conv1(f2k(wb2[l_, 4 * hh_:4 * hh_ + 4]), f2k(w2_t[l_, 4 * hh_:4 * hh_ + 4]), ('wb2', l_, hh_))
        conv(0)
        epsc = sb(es, "epsc", [128, 1])
        k.op('dve', [], ['epsc'], lambda e: e.memset(epsc[:], EPS))
        k.op('dve', [], ['ones_f'], lambda e: e.memset(ones_f[:], 1.0))
        k.op('dve', [], ['ones_b'], lambda e: e.memset(ones_b[:], 1.0))

        def rmsnorm(stack_tmp, keys, src, srck, nch, ntok, gain, dst, dstk, dim, eng2='dve'):
            sq, rs = stack_tmp
            sqks, rsk = keys
            skf = srck if callable(srck) else (lambda c_: srck)
            dkf = dstk if callable(dstk) else (lambda c_: dstk)
            for s0 in range(0, ntok, 512):
                pss, pk = psum()
                for c in range(nch):
                    sqt, sqk = sq[c % 2].bitcast(BF16)[:, 0:512], sqks[c % 2]
                    if c % 2 == 0:
                        k.op('act', [skf(c)], [sqk], lambda e, c=c, sqt=sqt: e.activation(out=sqt[:], in_=src[:, c, s0:s0 + 512], func=AF.Square))
                    else:
                        k.op('dve', [skf(c)], [sqk], lambda e, c=c, sqt=sqt: e.tensor_tensor(out=sqt[:], in0=src[:, c, s0:s0 + 512], in1=src[:, c, s0:s0 + 512], op=ALU.mult))
                    k.op('pe', [sqk, 'ones_b'], [pk], lambda e, c=c, sqt=sqt: e.matmul(pss[:], lhsT=ones_b[:], rhs=sqt[:], start=(c == 0), stop=(c == nch - 1)))
                k.op('act', [pk], [rsk], lambda e: e.activation(out=rs[:], in_=pss[:], func=AF.Sqrt, scale=1.0 / dim, bias=epsc[:, 0:1]))
                k.op('dve', [rsk], [rsk], lambda e: e.reciprocal(out=rs[:], in_=rs[:]))
                for c in range(nch):
                    en = eng2 if c % 2 else 'dve'
                    k.op(en, [skf(c), rsk], [dkf(c)], lambda e, c=c: e.scalar_tensor_tensor(out=dst[:, c, s0:s0 + 512], in0=src[:, c, s0:s0 + 512], scalar=gain[:, c:c + 1], in1=rs[:], op0=ALU.mult, op1=ALU.mult))

        cosT = sb(es, "cosT", [128, 8, 512]); sinT = sb(es, "sinT", [128, 8, 512])
        bT = sb(es, "bT", [128, 2, 8, 128], BF16)
        cre = sb(es, "cre", [128, 8, 128], BF16); cimn = sb(es, "cimn", [128, 8, 128], BF16)
        wm = sb(es, "wm", [128, 4, 128], BF16); sgug = sb(es, "sgug", [128, 256]); sgub = sb(es, "sgub", [128, 2, 128])
        gluw = sb(es, "gluw", [128, 2, 256], BF16); gls = sb(es, "gls", [128, 2, 256])
        wab = sb(es, "wab", [128, 2, 128], BF16); wxb = sb(es, "wxb", [128, 2, 128], BF16)
        s5s = sb(es, "s5s", [128, 16, 8]); s5i = sb(es, "s5i", [128, 1, 8], mybir.dt.int32)
        tI = sb(es, "tI", [128, 512], mybir.dt.int32); lsc = sb(es, "lsc", [128, 8])
        tSall = sb(es, "tSall", [128, 4, 512])
        tS = [tSall[:, i, :] for i in range(4)]
        iota = tS[3]
        stg = tSall[:, 0:2, :].rearrange("p a (b c) -> p (a b) c", b=4)
        c_ = lambda i: s5s[:, i, :]
        DT, MAG, TH, SN, CS, ARE, AIM, DEN, FRE, FIM, T1, T2, C5, S5_, RR = range(15)

        def setup(l):
            k.dma('sp', gls[:], gluw_d[l].rearrange("(c p) n -> p c n", p=128), [], ['rl0', 'rl1'])
            k.op('dve', ['rl0', 'rl1'], ['gluw'], lambda e: e.tensor_copy(out=gluw[:], in_=gls[:]))
            k.dma('sp', stg[:, 0:2, :], wa_d[l], [], ['tS0', 'tS1'])
            k.op('dve', ['tS0', 'tS1'], ['wab'], lambda e: e.tensor_copy(out=wab[:], in_=stg[:, 0:2, :]))
            k.dma('sp', stg[:, 2:4, :], wx_d[l], [], ['tS0', 'tS1'])
            k.op('dve', ['tS0', 'tS1'], ['wxb'], lambda e: e.tensor_copy(out=wxb[:], in_=stg[:, 2:4, :]))
            k.dma('sp', stg[:], cre_d[l], ['tS0', 'tS1'], ['tS0', 'tS1'])
            k.op('dve', ['tS0', 'tS1'], ['cre'], lambda e: e.tensor_copy(out=cre[:], in_=stg[:]))
            k.dma('sp', iota[:], iota_d, [], ['tS3'])
            k.dma('sp', sgug[:], sgug_d[:, l, :], [], ['sgug'])
            k.dma('sp', sgub[:], sgub_d[:, l, :, :], [], ['sgub'])
            k.dma('sp', stg[:], cim_d[l], [], ['tS0', 'tS1'])
            k.op('act', ['tS0', 'tS1'], ['cimn'], lambda e: e.activation(out=cimn[:], in_=stg[:], func=AF.Copy, scale=-1.0))
            k.dma('sp', stg[:, 0:4, :], sguw_d[l], ['tS0', 'tS1'], ['tS0', 'tS1'])
            for h in range(4):
                k.op('dve', ['tS0', 'tS1', 'tri_f'], ['wm'], lambda e, h=h: e.tensor_tensor(out=wm[:, h, :], in0=stg[:, h, :], in1=tri_f[:], op=ALU.mult))
            c_ = lambda i: s5s[:, i, :]
            lr, li, ld = lre[:, l, :], lim[:, l, :], ldt[:, l, :]
            DT, MAG, TH, SN, CS, ARE, AIM, DEN, FRE, FIM, T1, T2, C5, S5_, RR = range(15)
            v = lambda fn: k.op('dve', ['s5s', 'lre', 'lim', 'ldt'], ['s5s'], fn)
            a_ = lambda fn: k.op('act', ['s5s', 'lre', 'lim', 'ldt'], ['s5s'], fn)
            a_(lambda e: e.activation(out=c_(DT), in_=ld, func=AF.Exp))
            v(lambda e: e.tensor_tensor(out=c_(T1), in0=lr, in1=c_(DT), op=ALU.mult))
            a_(lambda e: e.activation(out=c_(MAG), in_=c_(T1), func=AF.Exp))
            v(lambda e: e.tensor_tensor(out=c_(TH), in0=li, in1=c_(DT), op=ALU.mult))

            C1 = 6.28125
            C2 = 2 * PI - C1

            def red_sin(dst, X, KI, KF, M, rk, wk):
                o = lambda en, fn: k.op(en, rk + wk, wk, fn)
                o('dve', lambda e: e.tensor_scalar(out=KF, in0=X, scalar1=1.0 / (2 * PI), scalar2=None, op0=ALU.mult))
                o('dve', lambda e: e.tensor_copy(out=KI, in_=KF))
                o('dve', lambda e: e.tensor_copy(out=KF, in_=KI))
                o('dve', lambda e: e.scalar_tensor_tensor(out=X, in0=KF, scalar=-C1, in1=X, op0=ALU.mult, op1=ALU.add))
                o('dve', lambda e: e.scalar_tensor_tensor(out=X, in0=KF, scalar=-C2, in1=X, op0=ALU.mult, op1=ALU.add))
                o('dve', lambda e: e.tensor_scalar(out=M, in0=X, scalar1=PI, scalar2=2 * PI, op0=ALU.is_gt, op1=ALU.mult))
                o('dve', lambda e: e.tensor_tensor(out=X, in0=X, in1=M, op=ALU.subtract))
                o('dve', lambda e: e.tensor_scalar(out=M, in0=X, scalar1=-PI, scalar2=2 * PI, op0=ALU.is_lt, op1=ALU.mult))
                o('dve', lambda e: e.tensor_tensor(out=X, in0=X, in1=M, op=ALU.add))
                o('act', lambda e: e.activation(out=dst, in_=X, func=AF.Sin))

            def sincos(dst_s, dst_c, src, mul):
                kk_ = ['s5s', 'lre', 'lim', 'ldt']
                v(lambda e: e.tensor_scalar(out=c_(T1), in0=src, scalar1=float(mul), scalar2=16 * PI, op0=ALU.mult, op1=ALU.add))
                red_sin(dst_s, c_(T1), s5i[:, 0, :], c_(T2), c_(RR), kk_, ['s5s', 's5i'])
                v(lambda e: e.tensor_scalar(out=c_(T1), in0=src, scalar1=float(mul), scalar2=16.5 * PI, op0=ALU.mult, op1=ALU.add))
                red_sin(dst_c, c_(T1), s5i[:, 0, :], c_(T2), c_(RR), kk_, ['s5s', 's5i'])
            sincos(c_(SN), c_(CS), c_(TH), 1.0)
            sincos(c_(S5_), c_(C5), c_(TH), float(TB))
            v(lambda e: e.tensor_tensor(out=c_(ARE), in0=c_(MAG), in1=c_(CS), op=ALU.mult))
            v(lambda e: e.tensor_tensor(out=c_(AIM), in0=c_(MAG), in1=c_(SN), op=ALU.mult))
            v(lambda e: e.tensor_tensor(out=c_(T1), in0=lr, in1=lr, op=ALU.mult))
            v(lambda e: e.tensor_tensor(out=c_(T2), in0=li, in1=li, op=ALU.mult))
            v(lambda e: e.tensor_tensor(out=c_(DEN), in0=c_(T1), in1=c_(T2), op=ALU.add))
            v(lambda e: e.reciprocal(out=c_(DEN), in_=c_(DEN)))
            v(lambda e: e.tensor_scalar(out=c_(RR), in0=c_(ARE), scalar1=-1.0, scalar2=None, op0=ALU.add))
            v(lambda e: e.tensor_tensor(out=c_(T1), in0=c_(RR), in1=lr, op=ALU.mult))
            v(lambda e: e.tensor_tensor(out=c_(T2), in0=c_(AIM), in1=li, op=ALU.mult))
            v(lambda e: e.tensor_tensor(out=c_(T1), in0=c_(T1), in1=c_(T2), op=ALU.add))
            v(lambda e: e.tensor_tensor(out=c_(FRE), in0=c_(T1), in1=c_(DEN), op=ALU.mult))
            v(lambda e: e.tensor_tensor(out=c_(T1), in0=c_(AIM), in1=lr, op=ALU.mult))
            v(lambda e: e.tensor_tensor(out=c_(T2), in0=c_(RR), in1=li, op=ALU.mult))
            v(lambda e: e.tensor_tensor(out=c_(T1), in0=c_(T1), in1=c_(T2), op=ALU.subtract))
            v(lambda e: e.tensor_tensor(out=c_(FIM), in0=c_(T1), in1=c_(DEN), op=ALU.mult))
            Bre = cosT
            Bim = sinT
            k.dma('sp', Bre[:, :, 0:128], bre_d[l], [], ['cosT'])
            k.dma('sp', Bim[:, :, 0:128], bim_d[l], [], ['sinT'])
            for j in range(8):
                fr, fi = s5s[:, FRE, j:j + 1], s5s[:, FIM, j:j + 1]
                k.op('dve', ['sinT', 's5s'], ['tS0'], lambda e, j=j, fi=fi: e.tensor_scalar(out=tS[0][:, 0:128], in0=Bim[:, j, 0:128], scalar1=fi, scalar2=None, op0=ALU.mult))
                k.op('dve', ['cosT', 'tS0', 's5s'], ['tS1'], lambda e, j=j, fr=fr: e.scalar_tensor_tensor(out=tS[1][:, 0:128], in0=Bre[:, j, 0:128], scalar=fr, in1=tS[0][:, 0:128], op0=ALU.mult, op1=ALU.subtract))
                k.op('dve', ['cosT', 's5s'], ['tS0'], lambda e, j=j, fi=fi: e.tensor_scalar(out=tS[0][:, 0:128], in0=Bre[:, j, 0:128], scalar1=fi, scalar2=None, op0=ALU.mult))
                k.op('dve', ['sinT', 'tS0', 's5s'], ['tS2'], lambda e, j=j, fr=fr: e.scalar_tensor_tensor(out=tS[2][:, 0:128], in0=Bim[:, j, 0:128], scalar=fr, in1=tS[0][:, 0:128], op0=ALU.mult, op1=ALU.add))
                for ri, tt, tk in ((0, tS[1], 'tS1'), (1, tS[2], 'tS2')):
                    pst, pk = psum()
                    k.op('pe', [tk, 'ident_f'], [pk], lambda e, tt=tt, pst=pst: e.transpose(out=pst[:, 0:128], in_=tt[:, 0:128], identity=ident_f[:]))
                    k.op('act', [pk], ['bT'], lambda e, ri=ri, j=j, pst=pst: e.activation(out=bT[:, ri, j, :], in_=pst[:, 0:128], func=AF.Copy))
            for j in range(8):
                th = s5s[:, TH, j:j + 1]
                for tab, tkk, off in ((sinT, 'sinT', 16 * PI), (cosT, 'cosT', 16.5 * PI)):
                    k.op('dve', ['tS3', 's5s'], ['tS0'], lambda e, th=th, off=off: e.tensor_scalar(out=tS[0][:], in0=iota[:], scalar1=th, scalar2=off, op0=ALU.mult, op1=ALU.add))
                    red_sin(tab[:, j, :], tS[0][:], tI[:], tS[1][:], tS[2][:], [], ['tS0', 'tS1', 'tS2', 'tI', tkk])
            k.op('act', ['llam'], ['lsc'], lambda e: e.activation(out=lsc[:, 0:2], in_=llam[:, l, :], func=AF.Exp, scale=-1.0))
            k.op('act', ['lsc'], ['lsc'], lambda e: e.activation(out=lsc[:, 2:4], in_=lsc[:, 0:2], func=AF.Ln, bias=1.0))
            k.op('dve', ['lsc'], ['lsc'], lambda e: e.tensor_scalar(out=lsc[:, 4:6], in0=lsc[:, 2:4], scalar1=-8.0, scalar2=None, op0=ALU.mult))


        setup(0)
        k.barrier()
        for l in range(nl):
            x_src = xT_in if l == 0 else xs
            with ExitStack() as ea:
                A = lambda name, shape, dt=F32: sb(ea, name, shape, dt)
                wr = [A("wr%d" % i, [128, 8, 256], BF16) for i in range(3)]
                wif = A("wif", [128, 8, 4], BF16)
                woa = [A("woa%d" % i, [128, 6, 128], BF16) for i in range(2)]
                wod = [A("wod%d" % i, [64, 4, 128], BF16) for i in range(2)]
                Kc = A("Kc", [65, 4, S], BF16); Vc = A("Vc", [128, S // 128, 4, 65], BF16)
                pcK = A("pcK", [128, S // 128, 4])
                amask = A("amask", [128, 128], BF16)
                vA = A("vA", [128, 2, 2, 128], BF16)
                xb = A("xb", [128, 8, TB]); hT = A("hT", [128, 8, TB], BF16)
                ya = A("ya", [128, 6, TB], BF16); yd = A("yd", [64, 4, TB], BF16)
                yn = ya; ynd = yd
                ug = A("m2a", [128, 2, TB]); uf = A("uf", [128, 2, TB]); xc = ug; ub = A("ub", [128, 2, TB], BF16); xcb = A("b2a", [128, 2, TB], BF16); qT = A("qT", [65, 4, TB], BF16); pcx = A("pcx", [128, 4, 4, 65])
                fT = tSall[:, 2:4, :]
                tAall = A("tAall", [128, 6, 512])
                tA = [tAall[:, i, :] for i in range(6)]
                sq0, sq1, rs = tA[3], tA[4], tA[5]
                gv = tA[1][:, 0:256]; rd = tA[1]
                sS = A("sS", [128, 4, 2, TB], BF16)
                pT = [tI[:].bitcast(BF16)[:, 0:512], tI[:].bitcast(BF16)[:, 512:1024], A("pT2", [128, 512], BF16)]
                sm = A("sm", [128, 12])
                wend = A("wend", [128, 2, 8]); winit = A("winit", [128, 2, 8])
                hcar = A("hcar", [128, 2])
                xbuf = A("xbuf", [128, 2, TB + 3])
                lft = A("lft", [128, 16]); car = [A("car%d" % i, [128, 4]) for i in range(2)]
                ygf = tSall[:, 0:2, :]; ygb = gls[:].bitcast(BF16)

                k.dma('sp', wif[:].rearrange("p a b -> p (a b)"), wbif[l], [('wbif', l)], ['wif'])
                k.dma('sp', tA[0][:, 0:128], amask_d[:, 0, 0:128], [], ['tA0'])
                k.op('pool', ['tA0'], ['amask'], lambda e: e.tensor_copy(out=amask[:], in_=tA[0][:, 0:128]))
                wslots = [None, None, None]
                wuse = [0, 0, 0]
                wtick = [0]

                def wget(grp):
                    wtick[0] += 1
                    for i in (0, 1, 2):
                        if wslots[i] == grp:
                            wuse[i] = wtick[0]
                            return wr[i], 'wr%d' % i
                    i = min((0, 1, 2), key=lambda i_: wuse[i_])
                    wslots[i] = grp
                    wuse[i] = wtick[0]
                    k.dma('sp', wr[i][:].rearrange("p a b -> p (a b)"), wbin[l, grp], [('wbin', l)], ['wr%d' % i])
                    return wr[i], 'wr%d' % i
                k.op('dve', [], ['vA'], lambda e: e.memset(vA[:], 0.0))
                k.op('pool', [], ['Vc'], lambda e: e.memset(Vc[:], 1.0))
                k.op('dve', [], ['xbuf'], lambda e: e.memset(xbuf[:], 0.0))
                k.op('dve', [], ['pcx'], lambda e: e.memset(pcx[:], 0.0))
                k.op('pool', [], ['Kc'], lambda e: e.memset(Kc[64:65, :, :], 1.0))
                k.op('dve', [], ['hcar'], lambda e: e.memset(hcar[:], 0.0))
                k.op('dve', [], ['winit'], lambda e: e.memset(winit[:], 0.0))
                k.op('dve', [], ['car0'], lambda e: e.memset(car[0][:], 0.0))

                wcur = {}

                def inproj_fm(col0, ncol, evac):
                    base = (col0 // 256) * 256
                    wt, wk = wget(base // 256)
                    o = col0 - base
                    ps, pk = psum()
                    for kc in range(8):
                        k.op('pe', [wk, 'hT%d' % kc], [pk], lambda e, kc=kc: e.matmul(ps[0:ncol, :], lhsT=wt[:, kc, o:o + ncol], rhs=hT[:, kc, :], start=(kc == 0), stop=(kc == 7)))
                    evac(ps, pk)

                def inproj_tm(q, col0, ncol):
                    wt, wk = wget(col0 // 256)
                    ps, pk = psum()
                    for kc in range(8):
                        k.op('pe', [wk, 'hT%d' % kc], [pk], lambda e, kc=kc: e.matmul(ps[:, 0:ncol], lhsT=hT[:, kc, q * 128:(q + 1) * 128], rhs=wt[:, kc, 0:ncol], start=(kc == 0), stop=(kc == 7)))
                    return ps, pk

                for tb in range(NB):
                    t0 = tb * TB
                    for c8 in range(8):
                        k.dma('sp', xb[:, c8, :], x_src[c8 * 128:(c8 + 1) * 128, t0:t0 + TB], [], ['xb%d' % c8])
                    rmsnorm(((sq0, sq1), rs), (('tA3', 'tA4'), 'tA5'), xb, (lambda c_: 'xb%d' % c_), 8, TB, g1[:, l, :], hT, (lambda c_: 'hT%d' % c_), D, eng2='dve')

                    fox_st = {}
                    def fox_f1():
                        pf, pfk = psfix(7)
                        for q in range(4):
                            for kc in range(8):
                                k.op('pe', ['wif', 'hT%d' % kc], [pfk], lambda e, kc=kc, q=q: e.matmul(pf[:, 4 * q:4 * q + 4], lhsT=hT[:, kc, q * 128:(q + 1) * 128], rhs=wif[:, kc, 0:4], start=(kc == 0), stop=(kc == 7)))
                        k.op('dve', [pfk, 'bfr'], ['lft'], lambda e: e.tensor_tensor(out=lft[:], in0=pf[:, 0:16], in1=bfr[:, l, :], op=ALU.add))
                        k.op('act', ['lft'], ['lft'], lambda e: e.activation(out=lft[:], in_=lft[:], func=AF.Exp, scale=-1.0))
                        k.op('act', ['lft'], ['lft'], lambda e: e.activation(out=lft[:], in_=lft[:], func=AF.Ln, bias=1.0))

                    def fox_f2():
                        pcs, pcsk = psum()
                        fox_st['pcs'] = (pcs, pcsk)
                        for q in range(4):
                            k.op('pe', ['tri_f', 'lft'], [pcsk], lambda e, q=q: e.matmul(pcs[:, 4 * q:4 * q + 4], lhsT=tri_f[:], rhs=lft[:, 4 * q:4 * q + 4], start=True, stop=True))
                        k.op('pe', ['ones_f', 'lft'], [pcsk], lambda e: e.matmul(pcs[:, 16:32], lhsT=ones_f[:], rhs=lft[:, 0:16], start=True, stop=True))
                        for q in range(4):
                            ci, co = car[(tb * 4 + q) % 2], car[(tb * 4 + q + 1) % 2]
                            cik, cok = 'car%d' % ((tb * 4 + q) % 2), 'car%d' % ((tb * 4 + q + 1) % 2)
                            k.op('dve', [pcsk, cik], ['pcK'], lambda e, q=q, ci=ci: e.tensor_tensor(out=pcK[:, tb * 4 + q, :], in0=pcs[:, 4 * q:4 * q + 4], in1=ci[:], op=ALU.add))
                            k.op('dve', ['pcK'], ['pcx'], lambda e, q=q: e.tensor_copy(out=pcx[:, q, :, 64], in_=pcK[:, tb * 4 + q, :]))
                            k.op('dve', [pcsk, cik], [cok], lambda e, q=q, ci=ci, co=co: e.tensor_tensor(out=co[:], in0=pcs[:, 16 + 4 * q:20 + 4 * q], in1=ci[:], op=ALU.add))

                    def fox_f3():
                        for h in range(4):
                            pcq, pcqk = psum()
                            for q in range(4):
                                k.op('pe', ['pcx', 'ident_f'], [pcqk], lambda e, q=q, h=h, pcq=pcq: e.matmul(pcq[0:65, q * 128:(q + 1) * 128], lhsT=pcx[:, q, h, :], rhs=ident_f[:], start=True, stop=True))
                            k.op('act', [pcqk], ['qT'], lambda e, h=h, pcq=pcq: e.activation(out=qT[64:65, h, :], in_=pcq[64:65, :], func=AF.Copy, scale=-1.0))


                    def s5_inproj(c):
                        def ev(ps, pk, c=c):
                            k.op('act', [pk], ['uf'], lambda e: e.activation(out=uf[:, c, :], in_=ps[:], func=AF.Copy))
                            k.op('pool', ['uf'], ['ub'], lambda e: e.tensor_copy(out=ub[:, c, :], in_=uf[:, c, :]))
                        inproj_fm(512 + c * 128, 128, ev)

                    def lru_cx(c):
                        inproj_fm(768 + c * 128, 128, lambda ps, pk, c=c: k.op('act', [pk], ['xbuf'], lambda e: e.activation(out=xbuf[:, c, 3:3 + TB], in_=ps[:], func=AF.Copy)))
                    fillers = [lambda: s5_inproj(0), lambda: (s5_inproj(1), fox_f3()), lambda: lru_cx(0), lambda: lru_cx(1)]

                    fox_f1()
                    for c in range(2):
                        inproj_fm(c * 128, 128, lambda ps, pk, c=c: k.op('act', [pk], ['m2a'], lambda e: e.activation(out=ug[:, c, :], in_=ps[:], func=GELU)))
                    fox_f2()
                    for q in range(4):
                        ps, pk = inproj_tm(q, 256, 256)
                        k.op('act', [pk], ['tA1'], lambda e: e.activation(out=gv[:], in_=ps[:, 0:256], func=GELU))
                        k.op('act', ['tA1'], ['tA3', 'sm'], lambda e: e.activation(out=sq0[:, 0:256], in_=gv[:], func=AF.Square, accum_out=sm[:, 0:1]))
                        k.op('dve', ['sm'], ['sm'], lambda e: e.tensor_scalar(out=sm[:, 1:2], in0=sm[:, 0:1], scalar1=1.0 / 256, scalar2=EPS, op0=ALU.mult, op1=ALU.add))
                        k.op('act', ['sm'], ['sm'], lambda e: e.activation(out=sm[:, 3:4], in_=sm[:, 1:2], func=AF.Sqrt))
                        k.op('dve', ['sm'], ['sm'], lambda e: e.reciprocal(out=sm[:, 2:3], in_=sm[:, 3:4]))
                        gv4 = gv[:].rearrange("p (a b d) -> p a b d", a=2, b=2)
                        gg4 = sgug[:].rearrange("p (a b d) -> p a b d", a=2, b=2)
                        for hh in range(2):
                            k.op('dve', ['tA1', 'sm', 'sgug'], ['vA'], lambda e, hh=hh: e.scalar_tensor_tensor(out=vA[:, :, hh, hh * 64:(hh + 1) * 64], in0=gv4[:, :, hh, :], scalar=sm[:, 2:3], in1=gg4[:, :, hh, :], op0=ALU.mult, op1=ALU.mult))
                        fillers[q]()
                        for pr in range(2):
                            pm, pmk = psum()
                            for hh in range(2):
                                k.op('pe', ['vA', 'wm'], [pmk], lambda e, hh=hh, pr=pr, pm=pm: e.matmul(pm[:, 0:128], lhsT=vA[:, pr, hh, :], rhs=wm[:, 2 * pr + hh, :], start=(hh == 0), stop=(hh == 1)))
                            k.op('dve', [pmk, 'sgub'], ['tA0'], lambda e, pr=pr, pm=pm: e.tensor_tensor(out=tA[0][:, 0:128], in0=pm[:, 0:128], in1=sgub[:, pr, :], op=ALU.add))
                            k.op('pool', ['tA0', 'm2a'], ['ya%d' % pr], lambda e, pr=pr, q=q: e.tensor_tensor(out=ya[:, pr, q * 128:(q + 1) * 128], in0=tA[0][:, 0:128], in1=ug[:, pr, q * 128:(q + 1) * 128], op=ALU.mult))

                    def s5_gen():
                        if tb > 0:
                            k.op('dve', ['wend', 's5s'], ['tA0'], lambda e: e.tensor_tensor(out=tA[0][:, 0:8], in0=wend[:, 0, :], in1=c_(C5), op=ALU.mult))
                            k.op('dve', ['wend', 's5s'], ['tA1'], lambda e: e.tensor_tensor(out=tA[1][:, 0:8], in0=wend[:, 1, :], in1=c_(S5_), op=ALU.mult))
                            k.op('dve', ['tA0', 'tA1'], ['winit'], lambda e: e.tensor_tensor(out=winit[:, 0, :], in0=tA[0][:, 0:8], in1=tA[1][:, 0:8], op=ALU.subtract))
                            k.op('dve', ['wend', 's5s'], ['tA0'], lambda e: e.tensor_tensor(out=tA[0][:, 0:8], in0=wend[:, 0, :], in1=c_(S5_), op=ALU.mult))
                            k.op('dve', ['wend', 's5s'], ['tA1'], lambda e: e.tensor_tensor(out=tA[1][:, 0:8], in0=wend[:, 1, :], in1=c_(C5), op=ALU.mult))
                            k.op('dve', ['tA0', 'tA1'], ['winit'], lambda e: e.tensor_tensor(out=winit[:, 1, :], in0=tA[0][:, 0:8], in1=tA[1][:, 0:8], op=ALU.add))
                        def s5_out(cc):
                            py, pyk = psum()
                            for jj in range(4):
                                j = cc * 4 + jj
                                k.op('pe', ['cre', 'sSr%d' % jj], [pyk], lambda e, j=j, jj=jj, py=py: e.matmul(py[:], lhsT=cre[:, j, :], rhs=sS[:, jj, 0, :], start=(jj == 0), stop=False))
                                k.op('pe', ['cimn', 'sSi%d' % jj], [pyk], lambda e, j=j, jj=jj, py=py: e.matmul(py[:], lhsT=cimn[:, j, :], rhs=sS[:, jj, 1, :], start=False, stop=(jj == 3)))
                            k.op('dve', [pyk, 'uf', 's5d'], ['tA0'], lambda e, cc=cc, py=py: e.scalar_tensor_tensor(out=tA[0][:], in0=uf[:, cc, :], scalar=s5d[:, l, cc:cc + 1], in1=py[:], op0=ALU.mult, op1=ALU.add))
                            k.op('act', ['tA0'], ['ygf'], lambda e, cc=cc: e.activation(out=ygf[:, cc, :], in_=tA[0][:], func=GELU))
                            k.op('pool', ['ygf'], ['ygb'], lambda e, cc=cc: e.tensor_copy(out=ygb[:, cc, :], in_=ygf[:, cc, :]))
                        for j in range(8):
                            cc = j // 4
                            jj = j % 4
                            pre, prk = psum()
                            pim, pik = psum()
                            k.op('pe', ['bT', 'ub'], [prk], lambda e, j=j, cc=cc, pre=pre: e.matmul(pre[:], lhsT=bT[:, 0, j, :], rhs=ub[:, cc, :], start=True, stop=True))
                            k.op('pe', ['bT', 'ub'], [pik], lambda e, j=j, cc=cc, pim=pim: e.matmul(pim[:], lhsT=bT[:, 1, j, :], rhs=ub[:, cc, :], start=True, stop=True))
                            cj, sj = cosT[:, j, :], sinT[:, j, :]
                            TT = lambda o, ok, a, ak, b, bk, op, en='dve': k.op(en, [ak, bk], [ok], lambda e: e.tensor_tensor(out=o, in0=a, in1=b, op=op))
                            TT(tA[0][:], 'tA0', pre[:], prk, cj, 'cosT', ALU.mult)
                            TT(tA[1][:], 'tA1', pim[:], pik, sj, 'sinT', ALU.mult)
                            TT(tA[0][:], 'tA0', tA[0][:], 'tA0', tA[1][:], 'tA1', ALU.add)
                            TT(tA[1][:], 'tA1', pim[:], pik, cj, 'cosT', ALU.mult)
                            TT(ug[:, 0, :], 'm2a', pre[:], prk, sj, 'sinT', ALU.mult)
                            TT(tA[1][:], 'tA1', tA[1][:], 'tA1', ug[:, 0, :], 'm2a', ALU.subtract)
                            rbc = s5s[:, MAG, j:j + 1].to_broadcast([128, TB])
                            k.op('dve', ['tA0', 's5s', 'winit'], ['tA2'], lambda e, j=j, rbc=rbc: e.tensor_tensor_scan(out=tA[2][:], data0=rbc, data1=tA[0][:], initial=winit[:, 0, j:j + 1], op0=ALU.mult, op1=ALU.add))
                            k.op('dve', ['tA1', 's5s', 'winit'], ['tA3'], lambda e, j=j, rbc=rbc: e.tensor_tensor_scan(out=tA[3][:], data0=rbc, data1=tA[1][:], initial=winit[:, 1, j:j + 1], op0=ALU.mult, op1=ALU.add))
                            k.op('pool', ['tA2'], ['wend'], lambda e, j=j: e.tensor_copy(out=wend[:, 0, j:j + 1], in_=tA[2][:, TB - 1:TB]))
                            k.op('pool', ['tA3'], ['wend'], lambda e, j=j: e.tensor_copy(out=wend[:, 1, j:j + 1], in_=tA[3][:, TB - 1:TB]))
                            TT(tA[4][:], 'tA4', tA[2][:], 'tA2', cj, 'cosT', ALU.mult, 'pool')
                            TT(tA[5][:], 'tA5', tA[3][:], 'tA3', sj, 'sinT', ALU.mult, 'pool')
                            TT(sS[:, jj, 0, :], 'sSr%d' % jj, tA[4][:], 'tA4', tA[5][:], 'tA5', ALU.subtract, 'pool')
                            TT(tA[0][:], 'tA0', tA[2][:], 'tA2', sj, 'sinT', ALU.mult)
                            TT(tA[1][:], 'tA1', tA[3][:], 'tA3', cj, 'cosT', ALU.mult)
                            TT(sS[:, jj, 1, :], 'sSi%d' % jj, tA[0][:], 'tA0', tA[1][:], 'tA1', ALU.add)
                            yield 10.0
                            if jj == 3:
                                s5_out(cc)
                        for oc in range(2):
                            pg, pgk = psum()
                            for kc in range(2):
                                k.op('pe', ['gluw', 'ygb'], [pgk], lambda e, kc=kc, oc=oc, pg=pg: e.matmul(pg[:], lhsT=gluw[:, kc, oc * 128:(oc + 1) * 128], rhs=ygb[:, kc, :], start=(kc == 0), stop=(kc == 1)))
                            k.op('act', [pgk, 'glub'], ['tA1'], lambda e, oc=oc, pg=pg: e.activation(out=tA[1][:], in_=pg[:], func=AF.Sigmoid, bias=glub[:, l, oc:oc + 1]))
                            k.op('dve', ['tA1', 'ygf'], ['ya%d' % (2 + oc)], lambda e, oc=oc: e.tensor_tensor(out=ya[:, 2 + oc, :], in0=ygf[:, oc, :], in1=tA[1][:], op=ALU.mult))
                        yield 2.0

                    for c in range(2):
                        k.op('dve', ['xbuf', 'cw', 'cb'], ['m2a'], lambda e, c=c: e.tensor_scalar(out=xc[:, c, :], in0=xbuf[:, c, 0:TB], scalar1=cw[:, l, c, 0:1], scalar2=cb[:, l, c:c + 1], op0=ALU.mult, op1=ALU.add))
                        for kk in range(1, 4):
                            k.op('dve', ['xbuf', 'cw', 'm2a'], ['m2a'], lambda e, c=c, kk=kk: e.scalar_tensor_tensor(out=xc[:, c, :], in0=xbuf[:, c, kk:kk + TB], scalar=cw[:, l, c, kk:kk + 1], in1=xc[:, c, :], op0=ALU.mult, op1=ALU.add))
                        k.op('pool', ['xbuf'], ['sm'], lambda e, c=c: e.tensor_copy(out=sm[:, 8:11], in_=xbuf[:, c, TB:TB + 3]))
                        k.op('pool', ['sm'], ['xbuf'], lambda e, c=c: e.tensor_copy(out=xbuf[:, c, 0:3], in_=sm[:, 8:11]))
                        k.op('pool', ['m2a'], ['b2a'], lambda e, c=c: e.tensor_copy(out=xcb[:, c, :], in_=xc[:, c, :]))
                        def evg(ps, pk, c=c):
                            k.op('act', [pk], ['tA5'], lambda e: e.activation(out=tA[5][:], in_=ps[:], func=GELU))
                        inproj_fm(1024 + c * 128, 128, evg)
                        if c == 0:
                            for h in range(4):
                                inproj_fm(1280 + h * 64, 64, lambda ps, pk, h=h: k.op('act', [pk], ['qT'], lambda e: e.activation(out=qT[0:64, h, :], in_=ps[0:64, :], func=AF.Copy, scale=0.125)))
                        else:
                            for h in range(4):
                                inproj_fm(1536 + h * 64, 64, lambda ps, pk, h=h: k.op('act', [pk], ['Kc'], lambda e: e.activation(out=Kc[0:64, h, t0:t0 + TB], in_=ps[0:64, :], func=AF.Copy)))
                        pr_, prk = psum()
                        pi_, pik = psum()
                        k.op('pe', ['wab', 'b2a'], [prk], lambda e, c=c, pr_=pr_: e.matmul(pr_[:], lhsT=wab[:, c, :], rhs=xcb[:, c, :], start=True, stop=True))
                        k.op('pe', ['wxb', 'b2a'], [pik], lambda e, c=c, pi_=pi_: e.matmul(pi_[:], lhsT=wxb[:, c, :], rhs=xcb[:, c, :], start=True, stop=True))
                        k.op('act', [prk, 'ba'], ['tA0'], lambda e, c=c, pr_=pr_: e.activation(out=tA[0][:], in_=pr_[:], func=AF.Sigmoid, bias=ba[:, l, c:c + 1]))
                        k.op('act', [pik, 'bx'], ['tA1'], lambda e, c=c, pi_=pi_: e.activation(out=tA[1][:], in_=pi_[:], func=AF.Sigmoid, bias=bx[:, l, c:c + 1]))
                        k.op('act', ['tA0', 'lsc'], ['tA2'], lambda e, c=c: e.activation(out=tA[2][:], in_=tA[0][:], func=AF.Exp, scale=lsc[:, 4 + c:5 + c]))
                        k.op('pool', ['tA2'], ['tA3'], lambda e: e.tensor_tensor(out=tA[3][:], in0=tA[2][:], in1=tA[2][:], op=ALU.mult))
                        k.op('pool', ['tA3'], ['tA3'], lambda e: e.tensor_scalar(out=tA[3][:], in0=tA[3][:], scalar1=-1.0, scalar2=1.0, op0=ALU.mult, op1=ALU.add))
                        k.op('act', ['tA3'], ['tA3'], lambda e: e.activation(out=tA[3][:], in_=tA[3][:], func=AF.Sqrt))
                        k.op('dve', ['tA1', 'm2a'], ['tA1'], lambda e, c=c: e.tensor_tensor(out=tA[1][:], in0=tA[1][:], in1=xc[:, c, :], op=ALU.mult))
                        k.op('dve', ['tA1', 'tA3'], ['tA1'], lambda e: e.tensor_tensor(out=tA[1][:], in0=tA[1][:], in1=tA[3][:], op=ALU.mult))
                        k.op('dve', ['tA2', 'tA1', 'hcar'], ['tA4'], lambda e, c=c: e.tensor_tensor_scan(out=tA[4][:], data0=tA[2][:], data1=tA[1][:], initial=hcar[:, c:c + 1], op0=ALU.mult, op1=ALU.add))
                        k.op('pool', ['tA4'], ['hcar'], lambda e, c=c: e.tensor_copy(out=hcar[:, c:c + 1], in_=tA[4][:, TB - 1:TB]))
                        k.op('dve', ['tA4', 'tA5'], ['ya%d' % (4 + c)], lambda e, c=c: e.tensor_tensor(out=ya[:, 4 + c, :], in0=tA[4][:], in1=tA[5][:], op=ALU.mult))

                    for q in range(4):
                        ps, pk = inproj_tm(q, 1792, 256)
                        k.op('act', [pk], ['Vc'], lambda e, q=q, ps=ps: e.activation(out=Vc[:, tb * 4 + q, :, 0:64], in_=ps[:, 0:256].rearrange("p (h d) -> p h d", h=4), func=AF.Copy))

                    def fox_gen():
                        pend = []
                        for h in range(4):
                            po, pok = psfix(6 + (h % 2))
                            nkb = 4 * tb + 4

                            def emitS(kb, h=h):
                                pss, psk = psum()
                                diag = kb >= 4 * tb
                                c0 = 128 * (kb - 4 * tb) if diag else 0
                                k.op('pe', ['Kc', 'qT'], [psk], lambda e: e.matmul(pss[:, c0:512], lhsT=Kc[0:65, h, kb * 128:(kb + 1) * 128], rhs=qT[0:65, h, c0:512], start=True, stop=(not diag)))
                                if diag:
                                    k.op('pe', ['ident_b', 'amask'], [psk], lambda e: e.matmul(pss[:, c0:c0 + 128], lhsT=ident_b[:], rhs=amask[:], start=False, stop=True, skip_group_check=True))
                                return pss, psk
                            sq_ = [emitS(0), emitS(1)]
                            for kb in range(nkb):
                                pss, psk = sq_.pop(0)
                                if kb + 2 < nkb:
                                    sq_.append(emitS(kb + 2))
                                if kb == 1 and pend:
                                    pend.pop()()
                                pt, ptk = pT[kb % 3], 'pT%d' % (kb % 3)
                                c0 = 128 * (kb - 4 * tb) if kb >= 4 * tb else 0
                                k.op('act', [psk, 'pcK'], [ptk], lambda e, kb=kb, h=h, pss=pss, pt=pt, c0=c0: e.activation(out=pt[:, c0:512], in_=pss[:, c0:512], func=AF.Exp, bias=pcK[:, kb, h:h + 1]))
                                k.op('pe', ['Vc', ptk], [pok], lambda e, kb=kb, h=h, pt=pt, po=po, nkb=nkb, c0=c0: e.matmul(po[0:65, c0:512], lhsT=Vc[:, kb, h, :], rhs=pt[:, c0:512], start=(kb == 0), stop=(kb == nkb - 1), skip_group_check=True))
                                yield 1.0
                            def finish(h=h, po=po, pok=pok):
                                pb, pbk = psum()
                                k.op('pe', ['ones_f', 'fT1'], [pbk], lambda e, pb=pb: e.matmul(pb[0:64, :], lhsT=ones_f[64:65, 0:64], rhs=fT[64:65, 1, :], start=True, stop=True))
                                k.op('act', [pbk], ['fT0'], lambda e, pb=pb: e.activation(out=fT[0:64, 0, :], in_=pb[0:64, :], func=AF.Copy))
                                k.op('dve', [pok, 'fT0'], ['yd%d' % h], lambda e, h=h, po=po: e.tensor_tensor(out=yd[:, h, :], in0=po[0:64, :], in1=fT[0:64, 0, :], op=ALU.mult))
                            k.op('act', [pok], ['fT1'], lambda e, po=po: e.activation(out=fT[64:65, 1, :], in_=po[64:65, :], func=AF.Ln))
                            k.op('act', ['fT1'], ['fT1'], lambda e: e.activation(out=fT[64:65, 1, :], in_=fT[64:65, 1, :], func=AF.Exp, scale=-1.0))
                            pend.append(finish)
                            yield 1.0
                        while pend:
                            pend.pop()()
                        yield 1.0

                    gens = [s5_gen(), fox_gen()]
                    tot = [82.0, 4.0 * (4 * tb + 5)]
                    prog = [0.0, 0.0]
                    alive = [True, True]
                    while any(alive):
                        i = min((i_ for i_ in range(2) if alive[i_]), key=lambda i_: prog[i_])
                        try:
                            prog[i] += next(gens[i]) / tot[i]
                        except StopIteration:
                            alive[i] = False

                    tsets = [((tA[3], 'tA3'), (tA[4], 'tA4'), (tA[5], 'tA5')), ((tA[0], 'tA0'), (tA[1], 'tA1'), (tA[2], 'tA2'))]
                    for g in range(3):
                        (s0_, s0k), (s1_, s1k), (rs_, rsk_) = tsets[g % 2]
                        pss, pk = psum()
                        for cc in range(2):
                            sqt, sqk = (s0_.bitcast(BF16)[:, 0:512], s0k) if cc == 0 else (s1_.bitcast(BF16)[:, 0:512], s1k)
                            k.op('act', ['ya%d' % (2 * g + cc)], [sqk], lambda e, g=g, cc=cc, sqt=sqt: e.activation(out=sqt[:], in_=ya[:, 2 * g + cc, :], func=AF.Square))
                            k.op('pe', [sqk, 'ones_b'], [pk], lambda e, cc=cc, sqt=sqt, pss=pss: e.matmul(pss[:], lhsT=ones_b[:], rhs=sqt[:], start=(cc == 0), stop=(cc == 1)))
                        k.op('act', [pk], [rsk_], lambda e, pss=pss, rs_=rs_: e.activation(out=rs_[:], in_=pss[:], func=AF.Sqrt, scale=1.0 / 256, bias=epsc[:, 0:1]))
                        k.op('dve', [rsk_], [rsk_], lambda e, rs_=rs_: e.reciprocal(out=rs_[:], in_=rs_[:]))
                        for cc in range(2):
                            ch = 2 * g + cc
                            k.op('dve', ['ya%d' % ch, rsk_, 'gma'], ['ya%d' % ch], lambda e, ch=ch, rs_=rs_: e.scalar_tensor_tensor(out=yn[:, ch, :], in0=ya[:, ch, :], scalar=gma[:, l, ch:ch + 1], in1=rs_[:], op0=ALU.mult, op1=ALU.mult))
                    (s0_, s0k), (s1_, s1k), (rs_, rsk_) = tsets[1]
                    pss, pk = psum()
                    for h in range(4):
                        sqt, sqk = (s0_.bitcast(BF16)[:, 0:512], s0k) if h % 2 == 0 else (s1_.bitcast(BF16)[:, 0:512], s1k)
                        k.op('act', ['yd%d' % h], [sqk], lambda e, h=h, sqt=sqt: e.activation(out=sqt[0:64, :], in_=yd[:, h, :], func=AF.Square))
                        k.op('pe', [sqk, 'ones_b'], [pk], lambda e, h=h, sqt=sqt, pss=pss: e.matmul(pss[:], lhsT=ones_b[0:64, :], rhs=sqt[0:64, :], start=(h == 0), stop=(h == 3)))
                    k.op('act', [pk], [rsk_], lambda e, pss=pss, rs_=rs_: e.activation(out=rs_[:], in_=pss[:], func=AF.Sqrt, scale=1.0 / 256, bias=epsc[:, 0:1]))
                    k.op('dve', [rsk_], [rsk_], lambda e, rs_=rs_: e.reciprocal(out=rs_[:], in_=rs_[:]))
                    for h in range(4):
                        k.op('dve', ['yd%d' % h, rsk_, 'gmd'], ['yd%d' % h], lambda e, h=h, rs_=rs_: e.scalar_tensor_tensor(out=ynd[:, h, :], in0=yd[:, h, :], scalar=gmd[:, l, h:h + 1], in1=rs_[0:64, :], op0=ALU.mult, op1=ALU.mult))
                    for dc in range(8):
                        wa_, wak = woa[dc % 2], 'woa%d' % (dc % 2)
                        wd_, wdk = wod[dc % 2], 'wod%d' % (dc % 2)
                        k.dma('sp', wa_[:].rearrange("p a b -> p (a b)"), wboa[l, dc], [('wboa', l)], [wak])
                        k.dma('sp', wd_[:].rearrange("p a b -> p (a b)"), wbod[l, dc], [('wbod', l)], [wdk])
                        po, pok = psum()
                        for kc in range(6):
                            k.op('pe', [wak, 'ya%d' % kc], [pok], lambda e, kc=kc, po=po, wa_=wa_: e.matmul(po[:], lhsT=wa_[:, kc, :], rhs=yn[:, kc, :], start=(kc == 0), stop=False))
                        for h in range(4):
                            k.op('pe', [wdk, 'yd%d' % h], [pok], lambda e, h=h, po=po, wd_=wd_: e.matmul(po[:], lhsT=wd_[:, h, :], rhs=ynd[:, h, :], start=False, stop=(h == 3)))
                        k.op('dve', [pok, 'xb%d' % dc], ['xb%d' % dc], lambda e, dc=dc, po=po: e.tensor_tensor(out=xb[:, dc, :], in0=xb[:, dc, :], in1=po[:], op=ALU.add))
                        k.dma('pool', xs1[dc * 128:(dc + 1) * 128, t0:t0 + TB], xb[:, dc, :], ['xb%d' % dc], [])
                k.barrier()

            with ExitStack() as eb:
                B = lambda name, shape, dt=F32: sb(eb, name, shape, dt)
                x1 = B("x1", [128, 9, TF]); h2 = B("h2", [128, 8, TF], BF16)
                hid = B("hid", [128, 32, TF], BF16)
                w1t = [B("w1t%d" % i, [128, 8, 512], BF16) for i in range(2)]
                w2t = [B("w2t%d" % i, [128, 16, 128], BF16) for i in range(3)]
                rl = [gls[:].bitcast(BF16)[:, i, :] for i in range(2)]
                sq0, sq1, rs = tS[0], tS[1], tS[2]
                for tf in range(NBF):
                    t0 = tf * TF
                    bi = lambda c_, tf=tf: (c_ - tf) % 9
                    for c8 in range(8):
                        k.dma('sp', x1[:, bi(c8), :], xs1[c8 * 128:(c8 + 1) * 128, t0:t0 + TF], [], ['x1_%d' % bi(c8)])

                    class XV:
                        def __getitem__(s_, idx):
                            return x1[idx[0], bi(idx[1]), idx[2]]
                    rmsnorm(((sq0, sq1), rs), (('tS0', 'tS1'), 'tS2'), XV(), (lambda c_: 'x1_%d' % bi(c_)), 8, TF, g2[:, l, :], h2, (lambda c_: 'h2_%d' % c_), D, eng2='dve')
                    for fg in range(8):
                        wt, wk = w1t[fg % 2], 'w1t%d' % (fg % 2)
                        k.dma('sp', wt[:].rearrange("p a b -> p (a b)"), wb1[l, fg], [('wb1', l, fg // 4)], [wk])
                        for fc in range(4):
                            for sbk in range(TF // 512):
                                ps, pk = psum()
                                for kc in range(8):
                                    k.op('pe', [wk, 'h2_%d' % kc], [pk], lambda e, kc=kc, fc=fc, sbk=sbk, wt=wt, ps=ps: e.matmul(ps[:], lhsT=wt[:, kc, fc * 128:(fc + 1) * 128], rhs=h2[:, kc, sbk * 512:(sbk + 1) * 512], start=(kc == 0), stop=(kc == 7)))
                                f = fg * 4 + fc
                                rt, rk_ = rl[(f * 2 + sbk) % 2], 'rl%d' % ((f * 2 + sbk) % 2)
                                k.op('act', [pk], [rk_], lambda e, ps=ps, rt=rt: e.activation(out=rt[:], in_=ps[:], func=AF.Relu))
                                k.op('dve', [rk_], ['hid%d' % f], lambda e, f=f, sbk=sbk, rt=rt: e.tensor_tensor(out=hid[:, f, sbk * 512:(sbk + 1) * 512], in0=rt[:], in1=rt[:], op=ALU.mult))
                    if tf == 0 and l + 1 < nl:
                        conv(l + 1)
                        setup(l + 1)
                    hk = ['hid%d' % f for f in range(32)]
                    for dc in range(8):
                        pss2 = [psum() for _ in range(TF // 512)]
                        for hf in range(2):
                            wi_ = (dc * 2 + hf) % 3
                            wt, wk = w2t[wi_], 'w2t%d' % wi_
                            k.dma('sp', wt[:].rearrange("p a b -> p (a b)"), wb2[l, dc][:, hf * 2048:(hf + 1) * 2048], [('wb2', l, dc // 4)], [wk])
                            for sbk in range(TF // 512):
                                ps, pk = pss2[sbk]
                                for f16 in range(16):
                                    f = hf * 16 + f16
                                    k.op('pe', [wk] + (hk if f == 0 else []), [pk], lambda e, f=f, f16=f16, sbk=sbk, wt=wt, ps=ps: e.matmul(ps[:], lhsT=wt[:, f16, :], rhs=hid[:, f, sbk * 512:(sbk + 1) * 512], start=(f == 0), stop=(f == 31)))
                        for sbk in range(TF // 512):
                            ps, pk = pss2[sbk]
                            k.op('dve', [pk, 'x1_%d' % bi(dc)], ['x1_%d' % bi(dc)], lambda e, dc=dc, sbk=sbk, ps=ps: e.tensor_tensor(out=x1[:, bi(dc), sbk * 512:(sbk + 1) * 512], in0=x1[:, bi(dc), sbk * 512:(sbk + 1) * 512], in1=ps[:], op=ALU.add))
                        if l != nl - 1:
                            k.dma('pool', xs[dc * 128:(dc + 1) * 128, t0:t0 + TF], x1[:, bi(dc), :], ['x1_%d' % bi(dc)], [])
                    if l == nl - 1:
                        for sbk in range(TF // 512):
                            class _V:
                                def __getitem__(s_, idx):
                                    return x1[idx[0], bi(idx[1]), sbk * 512 + idx[2].start: sbk * 512 + idx[2].stop]
                            rmsnorm(((sq0, sq1), rs), (('tS0', 'tS1'), 'tS2'), _V(), (lambda c_: 'x1_%d' % bi(c_)), 8, 512, gf, _V(), (lambda c_: 'x1_%d' % bi(c_)), D, eng2='dve')
                            for c8 in range(8):
                                k.dma('sp', outT[c8 * 128:(c8 + 1) * 128, t0 + sbk * 512:t0 + (sbk + 1) * 512], x1[:, bi(c8), sbk * 512:(sbk + 1) * 512], ['x1_%d' % bi(c8)], [])
                    else:
                        pass
                k.barrier()
        for i in range(NDS):
            if k.dcnt[i] > 0:
                k._wait('sp', i, k.dcnt[i])
    return nc


def _prep(inputs, b):
    f = lambda a: np.ascontiguousarray(np.asarray(a, dtype=np.float32))
    I = {k_: np.asarray(v, dtype=np.float32) for k_, v in inputs.items()}
    m = {}
    m["xT"] = f(I["x"][b].T)
    wi_ = I["w_in"]
    m["w_in_t"] = f(wi_[:, :, :2048].reshape(L, 8, 128, 8, 256).transpose(0, 3, 2, 1, 4).reshape(L, 8, 128, 2048))
    m["wif_t"] = f(wi_[:, :, 2048:2052].reshape(L, 8, 128, 4).transpose(0, 2, 1, 3).reshape(L, 128, 32))
    wo_ = I["w_out"]
    m["woa_t"] = f(wo_[:, 0:768, :].reshape(L, 6, 128, 8, 128).transpose(0, 3, 2, 1, 4).reshape(L, 8, 128, 768))
    m["wod_t"] = f(wo_[:, 768:1024, :].reshape(L, 4, 64, 8, 128).transpose(0, 3, 2, 1, 4).reshape(L, 8, 64, 512))
    m["w1_t"] = f(I["w_mlp_in"].reshape(L, 8, 128, 8, 512).transpose(0, 3, 2, 1, 4).reshape(L, 8, 128, 4096))
    m["w2_t"] = f(I["w_mlp_out"].reshape(L, 32, 128, 8, 128).transpose(0, 3, 2, 1, 4).reshape(L, 8, 128, 4096))
    pl = lambda a, nchunk, p=128: f(a.reshape(L, nchunk, p).transpose(2, 0, 1))
    m["g1"] = pl(I["norm1_g"], 8); m["g2"] = pl(I["norm2_g"], 8)
    m["gma"] = pl(I["mix_norm_g"][:, :768], 6); m["gmd"] = pl(I["mix_norm_g"][:, 768:], 4, 64)
    m["gf"] = f(I["final_g"].reshape(8, 128).T)
    m["sgug"] = f(np.broadcast_to(I["sgu_norm_g"][None], (128, L, 256)))
    m["sguw"] = f(I["sgu_w"].transpose(0, 3, 1, 2))
    sb_ = I["sgu_b"].reshape(L, 2, 2, 128)
    m["sgub"] = f(np.broadcast_to(sb_.transpose(2, 0, 1, 3)[:, None], (2, 64, L, 2, 128)).reshape(128, L, 2, 128))
    st = lambda a: f(a.reshape(L, 8, 2, 64).transpose(2, 3, 0, 1).reshape(128, L, 8))
    m["lre"] = st(I["s5_lambda_re"]); m["lim"] = st(I["s5_lambda_im"])
    m["ldt"] = st(np.broadcast_to(I["s5_log_dt"][:, :, None], (L, 16, 64)))

    def padb(a):
        o = np.zeros((L, 2, 64, 8, 128), np.float32)
        for j in range(8):
            for gl in range(2):
                c0 = 32 * (j % 4) + 16 * gl
                o[:, gl, :, j, c0:c0 + 16] = a[:, 2 * j + gl]
        return o.reshape(L, 128, 8, 128)
    m["bre"] = padb(I["s5_b_re"]); m["bim"] = padb(I["s5_b_im"])
    m["cre"] = padb(I["s5_c_re"].transpose(0, 1, 3, 2)); m["cim"] = padb(I["s5_c_im"].transpose(0, 1, 3, 2))
    m["s5d"] = pl(I["s5_d"], 2); m["gluw"] = f(I["s5_glu_w"]); m["glub"] = pl(I["s5_glu_b"], 2)
    m["cw"] = f(I["lru_conv_w"].reshape(L, 4, 2, 128).transpose(3, 0, 2, 1))
    m["cb"] = pl(I["lru_conv_b"], 2)

    def bd(a):
        o = np.zeros((L, 2, 64, 2, 2, 64), np.float32)
        for c in range(2):
            for hl in range(2):
                o[:, hl, :, c, hl, :] = a[:, 2 * c + hl]
        return o.reshape(L, 128, 2, 128)
    m["wa"] = bd(I["lru_wa"]); m["wx"] = bd(I["lru_wx"])
    m["ba"] = pl(I["lru_ba"].reshape(L, 256), 2); m["bx"] = pl(I["lru_bx"].reshape(L, 256), 2)
    m["llam"] = pl(I["lru_lambda"], 2)
    m["bfr"] = f(np.broadcast_to(np.tile(I["fox_fgate_b"], (1, 4))[None], (128, L, 16)))
    m["ident"] = np.eye(128, dtype=np.float32)
    m["tri"] = np.triu(np.ones((128, 128), np.float32))
    kk = np.arange(128)[:, None, None] + 128 * np.arange(4)[None, :, None]
    m["amask"] = np.where(kk <= np.arange(512)[None, None, :], 0.0, NEG).astype(np.float32)
    m["iota"] = f(np.broadcast_to(np.arange(512, dtype=np.float32)[None], (128, 512)))
    return m


def kernel(**inputs):
    nc = build(L)
    in_maps = [_prep(inputs, b) for b in range(NCORE)]
    res = run_bass_kernel_spmd(nc, in_maps, core_ids=list(range(NCORE)))
    out = np.stack([np.asarray(res.results[b]["outT"], dtype=np.float32).T for b in range(NCORE)], 0)
    return np.ascontiguousarray(out)
```

```python
import math
from contextlib import ExitStack
import numpy as np
import concourse.bass as bass
import concourse.mybir as mybir
from concourse.bass_utils import run_bass_kernel_spmd

F32 = mybir.dt.float32
BF16 = mybir.dt.bfloat16
AF = mybir.ActivationFunctionType
ALU = mybir.AluOpType

D = 1024
S = 4096
L = 4
NCORE = 4
TB = 512
NB = S // TB
TF = 1024
NBF = S // TF
DIN = 2052
EPS = 1e-6
PI = math.pi
NEG = -30000.0
NDS = 24
GELU = AF.Gelu
NOSYNC_SAME = ()


class K:
    def __init__(s, nc, es):
        s.nc = nc
        s.es = es
        s.epoch = 0
        s.engs = {'pe': nc.tensor, 'act': nc.scalar, 'dve': nc.vector, 'pool': nc.gpsimd, 'sp': nc.sync}
        s.sem = {e: es.enter_context(nc.semaphore('s_' + e)) for e in s.engs}
        s.cnt = {e: 0 for e in s.engs}
        s.dsem = [es.enter_context(nc.semaphore('d%d' % i)) for i in range(NDS)]
        s.dcnt = [0] * NDS
        s.dnext = 0
        s.seen = {e: {} for e in s.engs}
        s.lastw = {}
        s.readers = {}
        s.psn = 0
        s.xsem = {}
        s.xcnt = {}
        s.xn = 0

    def _wait(s, e, sk, val):
        if e == sk and (e == 'pe' or e in NOSYNC_SAME):
            return
        if s.seen[e].get(sk, 0) >= val:
            return
        if isinstance(sk, str):
            sem = s.sem[sk]
        elif isinstance(sk, int):
            sem = s.dsem[sk]
        else:
            sem = s.xsem[sk]
        s.engs[e].wait_ge(sem, val)
        s.seen[e][sk] = val

    def deps(s, e, reads, writes):
        for k in reads:
            if k in s.lastw:
                s._wait(e, *s.lastw[k])
        for k in writes:
            if k in s.lastw:
                s._wait(e, *s.lastw[k])
            for sk, v in s.readers.get(k, {}).items():
                s._wait(e, sk, v)

    def _record(s, tok, reads, writes):
        for k in reads:
            d = s.readers.setdefault(k, {})
            d[tok[0]] = max(d.get(tok[0], 0), tok[1])
        for k in writes:
            s.lastw[k] = tok
            s.readers[k] = {}

    def op(s, e, reads, writes, fn):
        s.deps(e, reads, writes)
        ins = fn(s.engs[e])
        s.cnt[e] += 1
        ins.then_inc(s.sem[e], 1)
        s._record((e, s.cnt[e]), reads, writes)

    def dma(s, q, out, in_, reads, writes):
        i = s.dnext
        s.dnext = (i + 1) % NDS
        if s.dcnt[i] > 0:
            s._wait(q, i, s.dcnt[i])
        s.deps(q, reads, writes)
        s.dcnt[i] += 16
        s.engs[q].dma_start(out=out, in_=in_).then_inc(s.dsem[i], 16)
        s._record((i, s.dcnt[i]), reads, writes)

    def dma_sw(s, out, in_, reads, writes, **kw):
        key = ('x', s.xn % 4)
        s.xn += 1
        if key not in s.xsem:
            s.xsem[key] = s.es.enter_context(s.nc.semaphore('x%d' % key[1]))
            s.xcnt[key] = 0
        s.deps('pool', reads, writes)
        s.xcnt[key] += 16
        s.engs['pool'].dma_start(out=out, in_=in_, **kw).then_inc(s.xsem[key], 16)
        s._record((key, s.xcnt[key]), reads, writes)
        return (key, s.xcnt[key])

    def barrier(s):
        for e in s.engs:
            for f in s.engs:
                if f != e and s.cnt[f] > 0:
                    s._wait(e, f, s.cnt[f])
            for i in range(NDS):
                if s.dcnt[i] > 0:
                    s._wait(e, i, s.dcnt[i])
        s.lastw = {k_: v_ for k_, v_ in s.lastw.items() if isinstance(v_[0], tuple)}
        s.readers = {}
        s.epoch += 1
        for e in s.engs:
            s.sem[e] = s.es.enter_context(s.nc.semaphore('s_%s_%d' % (e, s.epoch)))
            s.cnt[e] = 0
        for e in s.engs:
            for f in s.engs:
                s.seen[e].pop(f, None)


def build(nl=L):
    nc = bass.Bass("TRN2", target_bir_lowering=False)

    def din(name, shape):
        return nc.dram_tensor(name, list(shape), F32, kind="ExternalInput").ap()

    xT_in = din("xT", [D, S])
    w_in_t = din("w_in_t", [L, 8, 128, 2048])
    wif_t = din("wif_t", [L, 128, 32])
    woa_t = din("woa_t", [L, 8, 128, 768])
    wod_t = din("wod_t", [L, 8, 64, 512])
    w1_t = din("w1_t", [L, 8, 128, 4096])
    w2_t = din("w2_t", [L, 8, 128, 4096])
    wbin = nc.dram_tensor("wbin", [L, 8, 128, 2048], BF16).ap()
    wbif = nc.dram_tensor("wbif", [L, 128, 32], BF16).ap()
    wboa = nc.dram_tensor("wboa", [L, 8, 128, 768], BF16).ap()
    wbod = nc.dram_tensor("wbod", [L, 8, 64, 512], BF16).ap()
    wb1 = nc.dram_tensor("wb1", [L, 8, 128, 4096], BF16).ap()
    wb2 = nc.dram_tensor("wb2", [L, 8, 128, 4096], BF16).ap()
    g1_d = din("g1", [128, L, 8])
    g2_d = din("g2", [128, L, 8])
    gma_d = din("gma", [128, L, 6])
    gmd_d = din("gmd", [64, L, 4])
    gf_d = din("gf", [128, 8])
    sgug_d = din("sgug", [128, L, 256])
    sguw_d = din("sguw", [L, 128, 4, 128])
    sgub_d = din("sgub", [128, L, 2, 128])
    lre_d = din("lre", [128, L, 8])
    lim_d = din("lim", [128, L, 8])
    ldt_d = din("ldt", [128, L, 8])
    bre_d = din("bre", [L, 128, 8, 128])
    bim_d = din("bim", [L, 128, 8, 128])
    cre_d = din("cre", [L, 128, 8, 128])
    cim_d = din("cim", [L, 128, 8, 128])
    s5d_d = din("s5d", [128, L, 2])
    gluw_d = din("gluw", [L, 256, 256])
    glub_d = din("glub", [128, L, 2])
    cw_d = din("cw", [128, L, 2, 4])
    cb_d = din("cb", [128, L, 2])
    wa_d = din("wa", [L, 128, 2, 128])
    wx_d = din("wx", [L, 128, 2, 128])
    ba_d = din("ba", [128, L, 2])
    bx_d = din("bx", [128, L, 2])
    llam_d = din("llam", [128, L, 2])
    bf_d = din("bfr", [128, L, 16])
    ident_d = din("ident", [128, 128])
    tri_d = din("tri", [128, 128])
    amask_d = din("amask", [128, 4, 512])
    iota_d = din("iota", [128, 512])
    outT = nc.dram_tensor("outT", [D, S], F32, kind="ExternalOutput").ap()
    xs = nc.dram_tensor("xs", [D, S], F32).ap()
    xs1 = nc.dram_tensor("xs1", [D, S], F32).ap()

    def fm(ap2d, t0, n):
        return ap2d[:, t0:t0 + n].rearrange("(c p) t -> p c t", p=128)

    with ExitStack() as es:
        E = es.enter_context
        k = K(nc, es)
        PS = [E(nc.psum_tensor("ps%d" % i, [128, 512], F32)) for i in range(8)]

        def psum():
            i = k.psn
            k.psn = (i + 1) % 6
            return PS[i], 'ps%d' % i

        def psfix(i):
            return PS[i], 'ps%d' % i

        uid = [0]

        def sb(stack, name, shape, dt=F32):
            uid[0] += 1
            return stack.enter_context(nc.sbuf_tensor("sb_%s_%d" % (name, uid[0]), list(shape), dt))

        g1 = sb(es, "g1", [128, L, 8]); g2 = sb(es, "g2", [128, L, 8])
        gma = sb(es, "gma", [128, L, 6]); gmd = sb(es, "gmd", [64, L, 4]); gf = sb(es, "gf", [128, 8])
        lre = sb(es, "lre", [128, L, 8]); lim = sb(es, "lim", [128, L, 8]); ldt = sb(es, "ldt", [128, L, 8])
        s5d = sb(es, "s5d", [128, L, 2]); glub = sb(es, "glub", [128, L, 2])
        cw = sb(es, "cw", [128, L, 2, 4]); cb = sb(es, "cb", [128, L, 2])
        ba = sb(es, "ba", [128, L, 2]); bx = sb(es, "bx", [128, L, 2]); llam = sb(es, "llam", [128, L, 2])
        bfr = sb(es, "bfr", [128, L, 16])
        ident_f = sb(es, "ident_f", [128, 128]); ident_b = sb(es, "ident_b", [128, 128], BF16)
        tri_f = sb(es, "tri_f", [128, 128])
        ones_f = sb(es, "ones_f", [128, 128]); ones_b = sb(es, "ones_b", [128, 128], BF16)
        for t, d_, nm in [(g1, g1_d, 'g1'), (g2, g2_d, 'g2'), (gma, gma_d, 'gma'), (gmd, gmd_d, 'gmd'),
                          (gf, gf_d, 'gf'), (lre, lre_d, 'lre'), (lim, lim_d, 'lim'), (ldt, ldt_d, 'ldt'),
                          (s5d, s5d_d, 's5d'), (glub, glub_d, 'glub'), (cw, cw_d, 'cw'), (cb, cb_d, 'cb'),
                          (ba, ba_d, 'ba'), (bx, bx_d, 'bx'), (llam, llam_d, 'llam'), (bfr, bf_d, 'bfr'),
                          (ident_f, ident_d, 'ident_f'), (tri_f, tri_d, 'tri_f')]:
            k.dma('sp', t[:], d_, [], [nm])
        conv_hist = []
        conv_hist.append(k.dma_sw(ident_b[:], ident_d, [], ['ident_b']))

        def f2k(ap):
            names = " ".join("d%d" % i for i in range(len(ap.shape)))
            return ap.rearrange("%s -> (%s)" % (names, names)).rearrange("(r c) -> r c", c=2048)
        def conv1(out, in_, key):
            if len(conv_hist) >= 2:
                k._wait('pool', *conv_hist[-2])
            conv_hist.append(k.dma_sw(out, in_, [], [key]))

        def conv(l_):
            conv1(f2k(wbin[l_]), f2k(w_in_t[l_]), ('wbin', l_))
            conv1(wbif[l_], wif_t[l_], ('wbif', l_))
            conv1(f2k(wboa[l_]), f2k(woa_t[l_]), ('wboa', l_))
            conv1(f2k(wbod[l_]), f2k(wod_t[l_]), ('wbod', l_))
            for hh_ in range(2):
                conv1(f2k(wb1[l_, 4 * hh_:4 * hh_ + 4]), f2k(w1_t[l_, 4 * hh_:4 * hh_ + 4]), ('wb1', l_, hh_))
            for hh_ in range(2):
                conv1(f2k(wb2[l_, 4 * hh_:4 * hh_ + 4]), f2k(w2_t[l_, 4 * hh_:4 * hh_ + 4]), ('wb2', l_, hh_))
        conv(0)
        epsc = sb(es, "epsc", [128, 1])
        k.op('dve', [], ['epsc'], lambda e: e.memset(epsc[:], EPS))
        k.op('dve', [], ['ones_f'], lambda e: e.memset(ones_f[:], 1.0))
        k.op('dve', [], ['ones_b'], lambda e: e.memset(ones_b[:], 1.0))

        def rmsnorm(stack_tmp, keys, src, srck, nch, ntok, gain, dst, dstk, dim, eng2='dve'):
            sq, rs = stack_tmp
            sqks, rsk = keys
            skf = srck if callable(srck) else (lambda c_: srck)
            dkf = dstk if callable(dstk) else (lambda c_: dstk)
            for s0 in range(0, ntok, 512):
                pss, pk = psum()
                for c in range(nch):
                    sqt, sqk = sq[c % 2].bitcast(BF16)[:, 0:512], sqks[c % 2]
                    if c % 2 == 0:
                        k.op('act', [skf(c)], [sqk], lambda e, c=c, sqt=sqt: e.activation(out=sqt[:], in_=src[:, c, s0:s0 + 512], func=AF.Square))
                    else:
                        k.op('dve', [skf(c)], [sqk], lambda e, c=c, sqt=sqt: e.tensor_tensor(out=sqt[:], in0=src[:, c, s0:s0 + 512], in1=src[:, c, s0:s0 + 512], op=ALU.mult))
                    k.op('pe', [sqk, 'ones_b'], [pk], lambda e, c=c, sqt=sqt: e.matmul(pss[:], lhsT=ones_b[:], rhs=sqt[:], start=(c == 0), stop=(c == nch - 1)))
                k.op('act', [pk], [rsk], lambda e: e.activation(out=rs[:], in_=pss[:], func=AF.Sqrt, scale=1.0 / dim, bias=epsc[:, 0:1]))
                k.op('dve', [rsk], [rsk], lambda e: e.reciprocal(out=rs[:], in_=rs[:]))
                for c in range(nch):
                    en = eng2 if c % 2 else 'dve'
                    k.op(en, [skf(c), rsk], [dkf(c)], lambda e, c=c: e.scalar_tensor_tensor(out=dst[:, c, s0:s0 + 512], in0=src[:, c, s0:s0 + 512], scalar=gain[:, c:c + 1], in1=rs[:], op0=ALU.mult, op1=ALU.mult))

        cosT = sb(es, "cosT", [128, 8, 512]); sinT = sb(es, "sinT", [128, 8, 512])
        bT = sb(es, "bT", [128, 2, 8, 128], BF16)
        cre = sb(es, "cre", [128, 8, 128], BF16); cimn = sb(es, "cimn", [128, 8, 128], BF16)
        wm = sb(es, "wm", [128, 4, 128], BF16); sgug = sb(es, "sgug", [128, 256]); sgub = sb(es, "sgub", [128, 2, 128])
        gluw = sb(es, "gluw", [128, 2, 256], BF16); gls = sb(es, "gls", [128, 2, 256])
        wab = sb(es, "wab", [128, 2, 128], BF16); wxb = sb(es, "wxb", [128, 2, 128], BF16)
        s5s = sb(es, "s5s", [128, 16, 8]); s5i = sb(es, "s5i", [128, 1, 8], mybir.dt.int32)
        tI = sb(es, "tI", [128, 512], mybir.dt.int32); lsc = sb(es, "lsc", [128, 8])
        tSall = sb(es, "tSall", [128, 4, 512])
        tS = [tSall[:, i, :] for i in range(4)]
        iota = tS[3]
        stg = tSall[:, 0:2, :].rearrange("p a (b c) -> p (a b) c", b=4)
        c_ = lambda i: s5s[:, i, :]
        DT, MAG, TH, SN, CS, ARE, AIM, DEN, FRE, FIM, T1, T2, C5, S5_, RR = range(15)

        def setup(l):
            k.dma('sp', gls[:], gluw_d[l].rearrange("(c p) n -> p c n", p=128), [], ['rl0', 'rl1'])
            k.op('dve', ['rl0', 'rl1'], ['gluw'], lambda e: e.tensor_copy(out=gluw[:], in_=gls[:]))
            k.dma('sp', stg[:, 0:2, :], wa_d[l], [], ['tS0', 'tS1'])
            k.op('dve', ['tS0', 'tS1'], ['wab'], lambda e: e.tensor_copy(out=wab[:], in_=stg[:, 0:2, :]))
            k.dma('sp', stg[:, 2:4, :], wx_d[l], [], ['tS0', 'tS1'])
            k.op('dve', ['tS0', 'tS1'], ['wxb'], lambda e: e.tensor_copy(out=wxb[:], in_=stg[:, 2:4, :]))
            k.dma('sp', stg[:], cre_d[l], ['tS0', 'tS1'], ['tS0', 'tS1'])
            k.op('dve', ['tS0', 'tS1'], ['cre'], lambda e: e.tensor_copy(out=cre[:], in_=stg[:]))
            k.dma('sp', iota[:], iota_d, [], ['tS3'])
            k.dma('sp', sgug[:], sgug_d[:, l, :], [], ['sgug'])
            k.dma('sp', sgub[:], sgub_d[:, l, :, :], [], ['sgub'])
            k.dma('sp', stg[:], cim_d[l], [], ['tS0', 'tS1'])
            k.op('act', ['tS0', 'tS1'], ['cimn'], lambda e: e.activation(out=cimn[:], in_=stg[:], func=AF.Copy, scale=-1.0))
            k.dma('sp', stg[:, 0:4, :], sguw_d[l], ['tS0', 'tS1'], ['tS0', 'tS1'])
            for h in range(4):
                k.op('dve', ['tS0', 'tS1', 'tri_f'], ['wm'], lambda e, h=h: e.tensor_tensor(out=wm[:, h, :], in0=stg[:, h, :], in1=tri_f[:], op=ALU.mult))
            c_ = lambda i: s5s[:, i, :]
            lr, li, ld = lre[:, l, :], lim[:, l, :], ldt[:, l, :]
            DT, MAG, TH, SN, CS, ARE, AIM, DEN, FRE, FIM, T1, T2, C5, S5_, RR = range(15)
            v = lambda fn: k.op('dve', ['s5s', 'lre', 'lim', 'ldt'], ['s5s'], fn)
            a_ = lambda fn: k.op('act', ['s5s', 'lre', 'lim', 'ldt'], ['s5s'], fn)
            a_(lambda e: e.activation(out=c_(DT), in_=ld, func=AF.Exp))
            v(lambda e: e.tensor_tensor(out=c_(T1), in0=lr, in1=c_(DT), op=ALU.mult))
            a_(lambda e: e.activation(out=c_(MAG), in_=c_(T1), func=AF.Exp))
            v(lambda e: e.tensor_tensor(out=c_(TH), in0=li, in1=c_(DT), op=ALU.mult))

            C1 = 6.28125
            C2 = 2 * PI - C1

            def red_sin(dst, X, KI, KF, M, rk, wk):
                o = lambda en, fn: k.op(en, rk + wk, wk, fn)
                o('dve', lambda e: e.tensor_scalar(out=KF, in0=X, scalar1=1.0 / (2 * PI), scalar2=None, op0=ALU.mult))
                o('dve', lambda e: e.tensor_copy(out=KI, in_=KF))
                o('dve', lambda e: e.tensor_copy(out=KF, in_=KI))
                o('dve', lambda e: e.scalar_tensor_tensor(out=X, in0=KF, scalar=-C1, in1=X, op0=ALU.mult, op1=ALU.add))
                o('dve', lambda e: e.scalar_tensor_tensor(out=X, in0=KF, scalar=-C2, in1=X, op0=ALU.mult, op1=ALU.add))
                o('dve', lambda e: e.tensor_scalar(out=M, in0=X, scalar1=PI, scalar2=2 * PI, op0=ALU.is_gt, op1=ALU.mult))
                o('dve', lambda e: e.tensor_tensor(out=X, in0=X, in1=M, op=ALU.subtract))
                o('dve', lambda e: e.tensor_scalar(out=M, in0=X, scalar1=-PI, scalar2=2 * PI, op0=ALU.is_lt, op1=ALU.mult))
                o('dve', lambda e: e.tensor_tensor(out=X, in0=X, in1=M, op=ALU.add))
                o('act', lambda e: e.activation(out=dst, in_=X, func=AF.Sin))

            def sincos(dst_s, dst_c, src, mul):
                kk_ = ['s5s', 'lre', 'lim', 'ldt']
                v(lambda e: e.tensor_scalar(out=c_(T1), in0=src, scalar1=float(mul), scalar2=16 * PI, op0=ALU.mult, op1=ALU.add))
                red_sin(dst_s, c_(T1), s5i[:, 0, :], c_(T2), c_(RR), kk_, ['s5s', 's5i'])
                v(lambda e: e.tensor_scalar(out=c_(T1), in0=src, scalar1=float(mul), scalar2=16.5 * PI, op0=ALU.mult, op1=ALU.add))
                red_sin(dst_c, c_(T1), s5i[:, 0, :], c_(T2), c_(RR), kk_, ['s5s', 's5i'])
            sincos(c_(SN), c_(CS), c_(TH), 1.0)
            sincos(c_(S5_), c_(C5), c_(TH), float(TB))
            v(lambda e: e.tensor_tensor(out=c_(ARE), in0=c_(MAG), in1=c_(CS), op=ALU.mult))
            v(lambda e: e.tensor_tensor(out=c_(AIM), in0=c_(MAG), in1=c_(SN), op=ALU.mult))
            v(lambda e: e.tensor_tensor(out=c_(T1), in0=lr, in1=lr, op=ALU.mult))
            v(lambda e: e.tensor_tensor(out=c_(T2), in0=li, in1=li, op=ALU.mult))
            v(lambda e: e.tensor_tensor(out=c_(DEN), in0=c_(T1), in1=c_(T2), op=ALU.add))
            v(lambda e: e.reciprocal(out=c_(DEN), in_=c_(DEN)))
            v(lambda e: e.tensor_scalar(out=c_(RR), in0=c_(ARE), scalar1=-1.0, scalar2=None, op0=ALU.add))
            v(lambda e: e.tensor_tensor(out=c_(T1), in0=c_(RR), in1=lr, op=ALU.mult))
            v(lambda e: e.tensor_tensor(out=c_(T2), in0=c_(AIM), in1=li, op=ALU.mult))
            v(lambda e: e.tensor_tensor(out=c_(T1), in0=c_(T1), in1=c_(T2), op=ALU.add))
            v(lambda e: e.tensor_tensor(out=c_(FRE), in0=c_(T1), in1=c_(DEN), op=ALU.mult))
            v(lambda e: e.tensor_tensor(out=c_(T1), in0=c_(AIM), in1=lr, op=ALU.mult))
            v(lambda e: e.tensor_tensor(out=c_(T2), in0=c_(RR), in1=li, op=ALU.mult))
            v(lambda e: e.tensor_tensor(out=c_(T1), in0=c_(T1), in1=c_(T2), op=ALU.subtract))
            v(lambda e: e.tensor_tensor(out=c_(FIM), in0=c_(T1), in1=c_(DEN), op=ALU.mult))
            Bre = cosT
            Bim = sinT
            k.dma('sp', Bre[:, :, 0:128], bre_d[l], [], ['cosT'])
            k.dma('sp', Bim[:, :, 0:128], bim_d[l], [], ['sinT'])
            for j in range(8):
                fr, fi = s5s[:, FRE, j:j + 1], s5s[:, FIM, j:j + 1]
                k.op('dve', ['sinT', 's5s'], ['tS0'], lambda e, j=j, fi=fi: e.tensor_scalar(out=tS[0][:, 0:128], in0=Bim[:, j, 0:128], scalar1=fi, scalar2=None, op0=ALU.mult))
                k.op('dve', ['cosT', 'tS0', 's5s'], ['tS1'], lambda e, j=j, fr=fr: e.scalar_tensor_tensor(out=tS[1][:, 0:128], in0=Bre[:, j, 0:128], scalar=fr, in1=tS[0][:, 0:128], op0=ALU.mult, op1=ALU.subtract))
                k.op('dve', ['cosT', 's5s'], ['tS0'], lambda e, j=j, fi=fi: e.tensor_scalar(out=tS[0][:, 0:128], in0=Bre[:, j, 0:128], scalar1=fi, scalar2=None, op0=ALU.mult))
                k.op('dve', ['sinT', 'tS0', 's5s'], ['tS2'], lambda e, j=j, fr=fr: e.scalar_tensor_tensor(out=tS[2][:, 0:128], in0=Bim[:, j, 0:128], scalar=fr, in1=tS[0][:, 0:128], op0=ALU.mult, op1=ALU.add))
                for ri, tt, tk in ((0, tS[1], 'tS1'), (1, tS[2], 'tS2')):
                    pst, pk = psum()
                    k.op('pe', [tk, 'ident_f'], [pk], lambda e, tt=tt, pst=pst: e.transpose(out=pst[:, 0:128], in_=tt[:, 0:128], identity=ident_f[:]))
                    k.op('act', [pk], ['bT'], lambda e, ri=ri, j=j, pst=pst: e.activation(out=bT[:, ri, j, :], in_=pst[:, 0:128], func=AF.Copy))
            for j in range(8):
                th = s5s[:, TH, j:j + 1]
                for tab, tkk, off in ((sinT, 'sinT', 16 * PI), (cosT, 'cosT', 16.5 * PI)):
                    k.op('dve', ['tS3', 's5s'], ['tS0'], lambda e, th=th, off=off: e.tensor_scalar(out=tS[0][:], in0=iota[:], scalar1=th, scalar2=off, op0=ALU.mult, op1=ALU.add))
                    red_sin(tab[:, j, :], tS[0][:], tI[:], tS[1][:], tS[2][:], [], ['tS0', 'tS1', 'tS2', 'tI', tkk])
            k.op('act', ['llam'], ['lsc'], lambda e: e.activation(out=lsc[:, 0:2], in_=llam[:, l, :], func=AF.Exp, scale=-1.0))
            k.op('act', ['lsc'], ['lsc'], lambda e: e.activation(out=lsc[:, 2:4], in_=lsc[:, 0:2], func=AF.Ln, bias=1.0))
            k.op('dve', ['lsc'], ['lsc'], lambda e: e.tensor_scalar(out=lsc[:, 4:6], in0=lsc[:, 2:4], scalar1=-8.0, scalar2=None, op0=ALU.mult))


        setup(0)
        k.barrier()
        for l in range(nl):
            x_src = xT_in if l == 0 else xs
            with ExitStack() as ea:
                A = lambda name, shape, dt=F32: sb(ea, name, shape, dt)
                wr = [A("wr%d" % i, [128, 8, 256], BF16) for i in range(3)]
                wif = A("wif", [128, 8, 4], BF16)
                woa = [A("woa%d" % i, [128, 6, 128], BF16) for i in range(2)]
                wod = [A("wod%d" % i, [64, 4, 128], BF16) for i in range(2)]
                Kc = A("Kc", [65, 4, S], BF16); Vc = A("Vc", [128, S // 128, 4, 65], BF16)
                pcK = A("pcK", [128, S // 128, 4])
                amask = A("amask", [128, 128], BF16)
                vA = A("vA", [128, 2, 2, 128], BF16)
                xb = A("xb", [128, 8, TB]); hT = A("hT", [128, 8, TB], BF16)
                ya = A("ya", [128, 6, TB], BF16); yd = A("yd", [64, 4, TB], BF16)
                yn = ya; ynd = yd
                ug = A("m2a", [128, 2, TB]); uf = A("uf", [128, 2, TB]); xc = ug; ub = A("ub", [128, 2, TB], BF16); xcb = A("b2a", [128, 2, TB], BF16); qT = A("qT", [65, 4, TB], BF16); pcx = A("pcx", [128, 4, 4, 65])
                fT = tSall[:, 2:4, :]
                tAall = A("tAall", [128, 6, 512])
                tA = [tAall[:, i, :] for i in range(6)]
                sq0, sq1, rs = tA[3], tA[4], tA[5]
                gv = tA[1][:, 0:256]; rd = tA[1]
                sS = A("sS", [128, 4, 2, TB], BF16)
                pT = [tI[:].bitcast(BF16)[:, 0:512], tI[:].bitcast(BF16)[:, 512:1024], A("pT2", [128, 512], BF16)]
                sm = A("sm", [128, 12])
                wend = A("wend", [128, 2, 8]); winit = A("winit", [128, 2, 8])
                hcar = A("hcar", [128, 2])
                xbuf = A("xbuf", [128, 2, TB + 3])
                lft = A("lft", [128, 16]); car = [A("car%d" % i, [128, 4]) for i in range(2)]
                ygf = tSall[:, 0:2, :]; ygb = gls[:].bitcast(BF16)

                k.dma('sp', wif[:].rearrange("p a b -> p (a b)"), wbif[l], [('wbif', l)], ['wif'])
                k.dma('sp', tA[0][:, 0:128], amask_d[:, 0, 0:128], [], ['tA0'])
                k.op('pool', ['tA0'], ['amask'], lambda e: e.tensor_copy(out=amask[:], in_=tA[0][:, 0:128]))
                wslots = [None, None, None]
                wuse = [0, 0, 0]
                wtick = [0]

                def wget(grp):
                    wtick[0] += 1
                    for i in (0, 1, 2):
                        if wslots[i] == grp:
                            wuse[i] = wtick[0]
                            return wr[i], 'wr%d' % i
                    i = min((0, 1, 2), key=lambda i_: wuse[i_])
                    wslots[i] = grp
                    wuse[i] = wtick[0]
                    k.dma('sp', wr[i][:].rearrange("p a b -> p (a b)"), wbin[l, grp], [('wbin', l)], ['wr%d' % i])
                    return wr[i], 'wr%d' % i
                k.op('dve', [], ['vA'], lambda e: e.memset(vA[:], 0.0))
                k.op('pool', [], ['Vc'], lambda e: e.memset(Vc[:], 1.0))
                k.op('dve', [], ['xbuf'], lambda e: e.memset(xbuf[:], 0.0))
                k.op('dve', [], ['pcx'], lambda e: e.memset(pcx[:], 0.0))
                k.op('pool', [], ['Kc'], lambda e: e.memset(Kc[64:65, :, :], 1.0))
                k.op('dve', [], ['hcar'], lambda e: e.memset(hcar[:], 0.0))
                k.op('dve', [], ['winit'], lambda e: e.memset(winit[:], 0.0))
                k.op('dve', [], ['car0'], lambda e: e.memset(car[0][:], 0.0))

                wcur = {}

                def inproj_fm(col0, ncol, evac):
                    base = (col0 // 256) * 256
                    wt, wk = wget(base // 256)
                    o = col0 - base
                    ps, pk = psum()
                    for kc in range(8):
                        k.op('pe', [wk, 'hT%d' % kc], [pk], lambda e, kc=kc: e.matmul(ps[0:ncol, :], lhsT=wt[:, kc, o:o + ncol], rhs=hT[:, kc, :], start=(kc == 0), stop=(kc == 7)))
                    evac(ps, pk)

                def inproj_tm(q, col0, ncol):
                    wt, wk = wget(col0 // 256)
                    ps, pk = psum()
                    for kc in range(8):
                        k.op('pe', [wk, 'hT%d' % kc], [pk], lambda e, kc=kc: e.matmul(ps[:, 0:ncol], lhsT=hT[:, kc, q * 128:(q + 1) * 128], rhs=wt[:, kc, 0:ncol], start=(kc == 0), stop=(kc == 7)))
                    return ps, pk

                for tb in range(NB):
                    t0 = tb * TB
                    for c8 in range(8):
                        k.dma('sp', xb[:, c8, :], x_src[c8 * 128:(c8 + 1) * 128, t0:t0 + TB], [], ['xb%d' % c8])
                    rmsnorm(((sq0, sq1), rs), (('tA3', 'tA4'), 'tA5'), xb, (lambda c_: 'xb%d' % c_), 8, TB, g1[:, l, :], hT, (lambda c_: 'hT%d' % c_), D, eng2='dve')

                    fox_st = {}
                    def fox_f1():
                        pf, pfk = psfix(7)
                        for q in range(4):
                            for kc in range(8):
                                k.op('pe', ['wif', 'hT%d' % kc], [pfk], lambda e, kc=kc, q=q: e.matmul(pf[:, 4 * q:4 * q + 4], lhsT=hT[:, kc, q * 128:(q + 1) * 128], rhs=wif[:, kc, 0:4], start=(kc == 0), stop=(kc == 7)))
                        k.op('dve', [pfk, 'bfr'], ['lft'], lambda e: e.tensor_tensor(out=lft[:], in0=pf[:, 0:16], in1=bfr[:, l, :], op=ALU.add))
                        k.op('act', ['lft'], ['lft'], lambda e: e.activation(out=lft[:], in_=lft[:], func=AF.Exp, scale=-1.0))
                        k.op('act', ['lft'], ['lft'], lambda e: e.activation(out=lft[:], in_=lft[:], func=AF.Ln, bias=1.0))

                    def fox_f2():
                        pcs, pcsk = psum()
                        fox_st['pcs'] = (pcs, pcsk)
                        for q in range(4):
                            k.op('pe', ['tri_f', 'lft'], [pcsk], lambda e, q=q: e.matmul(pcs[:, 4 * q:4 * q + 4], lhsT=tri_f[:], rhs=lft[:, 4 * q:4 * q + 4], start=True, stop=True))
                        k.op('pe', ['ones_f', 'lft'], [pcsk], lambda e: e.matmul(pcs[:, 16:32], lhsT=ones_f[:], rhs=lft[:, 0:16], start=True, stop=True))
                        for q in range(4):
                            ci, co = car[(tb * 4 + q) % 2], car[(tb * 4 + q + 1) % 2]
                            cik, cok = 'car%d' % ((tb * 4 + q) % 2), 'car%d' % ((tb * 4 + q + 1) % 2)
                            k.op('dve', [pcsk, cik], ['pcK'], lambda e, q=q, ci=ci: e.tensor_tensor(out=pcK[:, tb * 4 + q, :], in0=pcs[:, 4 * q:4 * q + 4], in1=ci[:], op=ALU.add))
                            k.op('dve', ['pcK'], ['pcx'], lambda e, q=q: e.tensor_copy(out=pcx[:, q, :, 64], in_=pcK[:, tb * 4 + q, :]))
                            k.op('dve', [pcsk, cik], [cok], lambda e, q=q, ci=ci, co=co: e.tensor_tensor(out=co[:], in0=pcs[:, 16 + 4 * q:20 + 4 * q], in1=ci[:], op=ALU.add))

                    def fox_f3():
                        for h in range(4):
                            pcq, pcqk = psum()
                            for q in range(4):
                                k.op('pe', ['pcx', 'ident_f'], [pcqk], lambda e, q=q, h=h, pcq=pcq: e.matmul(pcq[0:65, q * 128:(q + 1) * 128], lhsT=pcx[:, q, h, :], rhs=ident_f[:], start=True, stop=True))
                            k.op('act', [pcqk], ['qT'], lambda e, h=h, pcq=pcq: e.activation(out=qT[64:65, h, :], in_=pcq[64:65, :], func=AF.Copy, scale=-1.0))


                    def s5_inproj(c):
                        def ev(ps, pk, c=c):
                            k.op('act', [pk], ['uf'], lambda e: e.activation(out=uf[:, c, :], in_=ps[:], func=AF.Copy))
                            k.op('pool', ['uf'], ['ub'], lambda e: e.tensor_copy(out=ub[:, c, :], in_=uf[:, c, :]))
                        inproj_fm(512 + c * 128, 128, ev)

                    def lru_cx(c):
                        inproj_fm(768 + c * 128, 128, lambda ps, pk, c=c: k.op('act', [pk], ['xbuf'], lambda e: e.activation(out=xbuf[:, c, 3:3 + TB], in_=ps[:], func=AF.Copy)))
                    fillers = [lambda: s5_inproj(0), lambda: (s5_inproj(1), fox_f3()), lambda: lru_cx(0), lambda: lru_cx(1)]

                    fox_f1()
                    for c in range(2):
                        inproj_fm(c * 128, 128, lambda ps, pk, c=c: k.op('act', [pk], ['m2a'], lambda e: e.activation(out=ug[:, c, :], in_=ps[:], func=GELU)))
                    fox_f2()
                    for q in range(4):
                        ps, pk = inproj_tm(q, 256, 256)
                        k.op('act', [pk], ['tA1'], lambda e: e.activation(out=gv[:], in_=ps[:, 0:256], func=GELU))
                        k.op('act', ['tA1'], ['tA3', 'sm'], lambda e: e.activation(out=sq0[:, 0:256], in_=gv[:], func=AF.Square, accum_out=sm[:, 0:1]))
                        k.op('dve', ['sm'], ['sm'], lambda e: e.tensor_scalar(out=sm[:, 1:2], in0=sm[:, 0:1], scalar1=1.0 / 256, scalar2=EPS, op0=ALU.mult, op1=ALU.add))
                        k.op('act', ['sm'], ['sm'], lambda e: e.activation(out=sm[:, 3:4], in_=sm[:, 1:2], func=AF.Sqrt))
                        k.op('dve', ['sm'], ['sm'], lambda e: e.reciprocal(out=sm[:, 2:3], in_=sm[:, 3:4]))
                        gv4 = gv[:].rearrange("p (a b d) -> p a b d", a=2, b=2)
                        gg4 = sgug[:].rearrange("p (a b d) -> p a b d", a=2, b=2)
                        for hh in range(2):
                            k.op('dve', ['tA1', 'sm', 'sgug'], ['vA'], lambda e, hh=hh: e.scalar_tensor_tensor(out=vA[:, :, hh, hh * 64:(hh + 1) * 64], in0=gv4[:, :, hh, :], scalar=sm[:, 2:3], in1=gg4[:, :, hh, :], op0=ALU.mult, op1=ALU.mult))
                        fillers[q]()
                        for pr in range(2):
                            pm, pmk = psum()
                            for hh in range(2):
                                k.op('pe', ['vA', 'wm'], [pmk], lambda e, hh=hh, pr=pr, pm=pm: e.matmul(pm[:, 0:128], lhsT=vA[:, pr, hh, :], rhs=wm[:, 2 * pr + hh, :], start=(hh == 0), stop=(hh == 1)))
                            k.op('dve', [pmk, 'sgub'], ['tA0'], lambda e, pr=pr, pm=pm: e.tensor_tensor(out=tA[0][:, 0:128], in0=pm[:, 0:128], in1=sgub[:, pr, :], op=ALU.add))
                            k.op('pool', ['tA0', 'm2a'], ['ya%d' % pr], lambda e, pr=pr, q=q: e.tensor_tensor(out=ya[:, pr, q * 128:(q + 1) * 128], in0=tA[0][:, 0:128], in1=ug[:, pr, q * 128:(q + 1) * 128], op=ALU.mult))

                    def s5_gen():
                        if tb > 0:
                            k.op('dve', ['wend', 's5s'], ['tA0'], lambda e: e.tensor_tensor(out=tA[0][:, 0:8], in0=wend[:, 0, :], in1=c_(C5), op=ALU.mult))
                            k.op('dve', ['wend', 's5s'], ['tA1'], lambda e: e.tensor_tensor(out=tA[1][:, 0:8], in0=wend[:, 1, :], in1=c_(S5_), op=ALU.mult))
                            k.op('dve', ['tA0', 'tA1'], ['winit'], lambda e: e.tensor_tensor(out=winit[:, 0, :], in0=tA[0][:, 0:8], in1=tA[1][:, 0:8], op=ALU.subtract))
                            k.op('dve', ['wend', 's5s'], ['tA0'], lambda e: e.tensor_tensor(out=tA[0][:, 0:8], in0=wend[:, 0, :], in1=c_(S5_), op=ALU.mult))
                            k.op('dve', ['wend', 's5s'], ['tA1'], lambda e: e.tensor_tensor(out=tA[1][:, 0:8], in0=wend[:, 1, :], in1=c_(C5), op=ALU.mult))
                            k.op('dve', ['tA0', 'tA1'], ['winit'], lambda e: e.tensor_tensor(out=winit[:, 1, :], in0=tA[0][:, 0:8], in1=tA[1][:, 0:8], op=ALU.add))
                        def s5_out(cc):
                            py, pyk = psum()
                            for jj in range(4):
                                j = cc * 4 + jj
                                k.op('pe', ['cre', 'sSr%d' % jj], [pyk], lambda e, j=j, jj=jj, py=py: e.matmul(py[:], lhsT=cre[:, j, :], rhs=sS[:, jj, 0, :], start=(jj == 0), stop=False))
                                k.op('pe', ['cimn', 'sSi%d' % jj], [pyk], lambda e, j=j, jj=jj, py=py: e.matmul(py[:], lhsT=cimn[:, j, :], rhs=sS[:, jj, 1, :], start=False, stop=(jj == 3)))
                            k.op('dve', [pyk, 'uf', 's5d'], ['tA0'], lambda e, cc=cc, py=py: e.scalar_tensor_tensor(out=tA[0][:], in0=uf[:, cc, :], scalar=s5d[:, l, cc:cc + 1], in1=py[:], op0=ALU.mult, op1=ALU.add))
                            k.op('act', ['tA0'], ['ygf'], lambda e, cc=cc: e.activation(out=ygf[:, cc, :], in_=tA[0][:], func=GELU))
                            k.op('pool', ['ygf'], ['ygb'], lambda e, cc=cc: e.tensor_copy(out=ygb[:, cc, :], in_=ygf[:, cc, :]))
                        for j in range(8):
                            cc = j // 4
                            jj = j % 4
                            pre, prk = psum()
                            pim, pik = psum()
                            k.op('pe', ['bT', 'ub'], [prk], lambda e, j=j, cc=cc, pre=pre: e.matmul(pre[:], lhsT=bT[:, 0, j, :], rhs=ub[:, cc, :], start=True, stop=True))
                            k.op('pe', ['bT', 'ub'], [pik], lambda e, j=j, cc=cc, pim=pim: e.matmul(pim[:], lhsT=bT[:, 1, j, :], rhs=ub[:, cc, :], start=True, stop=True))
                            cj, sj = cosT[:, j, :], sinT[:, j, :]
                            TT = lambda o, ok, a, ak, b, bk, op, en='dve': k.op(en, [ak, bk], [ok], lambda e: e.tensor_tensor(out=o, in0=a, in1=b, op=op))
                            TT(tA[0][:], 'tA0', pre[:], prk, cj, 'cosT', ALU.mult)
                            TT(tA[1][:], 'tA1', pim[:], pik, sj, 'sinT', ALU.mult)
                            TT(tA[0][:], 'tA0', tA[0][:], 'tA0', tA[1][:], 'tA1', ALU.add)
                            TT(tA[1][:], 'tA1', pim[:], pik, cj, 'cosT', ALU.mult)
                            TT(ug[:, 0, :], 'm2a', pre[:], prk, sj, 'sinT', ALU.mult)
                            TT(tA[1][:], 'tA1', tA[1][:], 'tA1', ug[:, 0, :], 'm2a', ALU.subtract)
                            rbc = s5s[:, MAG, j:j + 1].to_broadcast([128, TB])
                            k.op('dve', ['tA0', 's5s', 'winit'], ['tA2'], lambda e, j=j, rbc=rbc: e.tensor_tensor_scan(out=tA[2][:], data0=rbc, data1=tA[0][:], initial=winit[:, 0, j:j + 1], op0=ALU.mult, op1=ALU.add))
                            k.op('dve', ['tA1', 's5s', 'winit'], ['tA3'], lambda e, j=j, rbc=rbc: e.tensor_tensor_scan(out=tA[3][:], data0=rbc, data1=tA[1][:], initial=winit[:, 1, j:j + 1], op0=ALU.mult, op1=ALU.add))
                            k.op('pool', ['tA2'], ['wend'], lambda e, j=j: e.tensor_copy(out=wend[:, 0, j:j + 1], in_=tA[2][:, TB - 1:TB]))
                            k.op('pool', ['tA3'], ['wend'], lambda e, j=j: e.tensor_copy(out=wend[:, 1, j:j + 1], in_=tA[3][:, TB - 1:TB]))
                            TT(tA[4][:], 'tA4', tA[2][:], 'tA2', cj, 'cosT', ALU.mult, 'pool')
                            TT(tA[5][:], 'tA5', tA[3][:], 'tA3', sj, 'sinT', ALU.mult, 'pool')
                            TT(sS[:, jj, 0, :], 'sSr%d' % jj, tA[4][:], 'tA4', tA[5][:], 'tA5', ALU.subtract, 'pool')
                            TT(tA[0][:], 'tA0', tA[2][:], 'tA2', sj, 'sinT', ALU.mult)
                            TT(tA[1][:], 'tA1', tA[3][:], 'tA3', cj, 'cosT', ALU.mult)
                            TT(sS[:, jj, 1, :], 'sSi%d' % jj, tA[0][:], 'tA0', tA[1][:], 'tA1', ALU.add)
                            yield 10.0
                            if jj == 3:
                                s5_out(cc)
                        for oc in range(2):
                            pg, pgk = psum()
                            for kc in range(2):
                                k.op('pe', ['gluw', 'ygb'], [pgk], lambda e, kc=kc, oc=oc, pg=pg: e.matmul(pg[:], lhsT=gluw[:, kc, oc * 128:(oc + 1) * 128], rhs=ygb[:, kc, :], start=(kc == 0), stop=(kc == 1)))
                            k.op('act', [pgk, 'glub'], ['tA1'], lambda e, oc=oc, pg=pg: e.activation(out=tA[1][:], in_=pg[:], func=AF.Sigmoid, bias=glub[:, l, oc:oc + 1]))
                            k.op('dve', ['tA1', 'ygf'], ['ya%d' % (2 + oc)], lambda e, oc=oc: e.tensor_tensor(out=ya[:, 2 + oc, :], in0=ygf[:, oc, :], in1=tA[1][:], op=ALU.mult))
                        yield 2.0

                    for c in range(2):
                        k.op('dve', ['xbuf', 'cw', 'cb'], ['m2a'], lambda e, c=c: e.tensor_scalar(out=xc[:, c, :], in0=xbuf[:, c, 0:TB], scalar1=cw[:, l, c, 0:1], scalar2=cb[:, l, c:c + 1], op0=ALU.mult, op1=ALU.add))
                        for kk in range(1, 4):
                            k.op('dve', ['xbuf', 'cw', 'm2a'], ['m2a'], lambda e, c=c, kk=kk: e.scalar_tensor_tensor(out=xc[:, c, :], in0=xbuf[:, c, kk:kk + TB], scalar=cw[:, l, c, kk:kk + 1], in1=xc[:, c, :], op0=ALU.mult, op1=ALU.add))
                        k.op('pool', ['xbuf'], ['sm'], lambda e, c=c: e.tensor_copy(out=sm[:, 8:11], in_=xbuf[:, c, TB:TB + 3]))
                        k.op('pool', ['sm'], ['xbuf'], lambda e, c=c: e.tensor_copy(out=xbuf[:, c, 0:3], in_=sm[:, 8:11]))
                        k.op('pool', ['m2a'], ['b2a'], lambda e, c=c: e.tensor_copy(out=xcb[:, c, :], in_=xc[:, c, :]))
                        def evg(ps, pk, c=c):
                            k.op('act', [pk], ['tA5'], lambda e: e.activation(out=tA[5][:], in_=ps[:], func=GELU))
                        inproj_fm(1024 + c * 128, 128, evg)
                        if c == 0:
                            for h in range(4):
                                inproj_fm(1280 + h * 64, 64, lambda ps, pk, h=h: k.op('act', [pk], ['qT'], lambda e: e.activation(out=qT[0:64, h, :], in_=ps[0:64, :], func=AF.Copy, scale=0.125)))
                        else:
                            for h in range(4):
                                inproj_fm(1536 + h * 64, 64, lambda ps, pk, h=h: k.op('act', [pk], ['Kc'], lambda e: e.activation(out=Kc[0:64, h, t0:t0 + TB], in_=ps[0:64, :], func=AF.Copy)))
                        pr_, prk = psum()
                        pi_, pik = psum()
                        k.op('pe', ['wab', 'b2a'], [prk], lambda e, c=c, pr_=pr_: e.matmul(pr_[:], lhsT=wab[:, c, :], rhs=xcb[:, c, :], start=True, stop=True))
                        k.op('pe', ['wxb', 'b2a'], [pik], lambda e, c=c, pi_=pi_: e.matmul(pi_[:], lhsT=wxb[:, c, :], rhs=xcb[:, c, :], start=True, stop=True))
                        k.op('act', [prk, 'ba'], ['tA0'], lambda e, c=c, pr_=pr_: e.activation(out=tA[0][:], in_=pr_[:], func=AF.Sigmoid, bias=ba[:, l, c:c + 1]))
                        k.op('act', [pik, 'bx'], ['tA1'], lambda e, c=c, pi_=pi_: e.activation(out=tA[1][:], in_=pi_[:], func=AF.Sigmoid, bias=bx[:, l, c:c + 1]))
                        k.op('act', ['tA0', 'lsc'], ['tA2'], lambda e, c=c: e.activation(out=tA[2][:], in_=tA[0][:], func=AF.Exp, scale=lsc[:, 4 + c:5 + c]))
                        k.op('pool', ['tA2'], ['tA3'], lambda e: e.tensor_tensor(out=tA[3][:], in0=tA[2][:], in1=tA[2][:], op=ALU.mult))
                        k.op('pool', ['tA3'], ['tA3'], lambda e: e.tensor_scalar(out=tA[3][:], in0=tA[3][:], scalar1=-1.0, scalar2=1.0, op0=ALU.mult, op1=ALU.add))
                        k.op('act', ['tA3'], ['tA3'], lambda e: e.activation(out=tA[3][:], in_=tA[3][:], func=AF.Sqrt))
                        k.op('dve', ['tA1', 'm2a'], ['tA1'], lambda e, c=c: e.tensor_tensor(out=tA[1][:], in0=tA[1][:], in1=xc[:, c, :], op=ALU.mult))
                        k.op('dve', ['tA1', 'tA3'], ['tA1'], lambda e: e.tensor_tensor(out=tA[1][:], in0=tA[1][:], in1=tA[3][:], op=ALU.mult))
                        k.op('dve', ['tA2', 'tA1', 'hcar'], ['tA4'], lambda e, c=c: e.tensor_tensor_scan(out=tA[4][:], data0=tA[2][:], data1=tA[1][:], initial=hcar[:, c:c + 1], op0=ALU.mult, op1=ALU.add))
                        k.op('pool', ['tA4'], ['hcar'], lambda e, c=c: e.tensor_copy(out=hcar[:, c:c + 1], in_=tA[4][:, TB - 1:TB]))
                        k.op('dve', ['tA4', 'tA5'], ['ya%d' % (4 + c)], lambda e, c=c: e.tensor_tensor(out=ya[:, 4 + c, :], in0=tA[4][:], in1=tA[5][:], op=ALU.mult))

                    for q in range(4):
                        ps, pk = inproj_tm(q, 1792, 256)
                        k.op('act', [pk], ['Vc'], lambda e, q=q, ps=ps: e.activation(out=Vc[:, tb * 4 + q, :, 0:64], in_=ps[:, 0:256].rearrange("p (h d) -> p h d", h=4), func=AF.Copy))

                    fox_tail = []

                    def fox_gen():
                        pend = []
                        for h in range(4):
                            po, pok = psfix(6 + (h % 2))
                            nkb = 4 * tb + 4

                            def emitS(kb, h=h):
                                pss, psk = psum()
                                diag = kb >= 4 * tb
                                c0 = 128 * (kb - 4 * tb) if diag else 0
                                k.op('pe', ['Kc', 'qT'], [psk], lambda e: e.matmul(pss[:, c0:512], lhsT=Kc[0:65, h, kb * 128:(kb + 1) * 128], rhs=qT[0:65, h, c0:512], start=True, stop=(not diag)))
                                if diag:
                                    k.op('pe', ['ident_b', 'amask'], [psk], lambda e: e.matmul(pss[:, c0:c0 + 128], lhsT=ident_b[:], rhs=amask[:], start=False, stop=True, skip_group_check=True))
                                return pss, psk
                            sq_ = [emitS(0), emitS(1)]
                            for kb in range(nkb):
                                pss, psk = sq_.pop(0)
                                if kb + 2 < nkb:
                                    sq_.append(emitS(kb + 2))
                                if kb == 2 and pend:
                                    pend.pop()()
                                pt, ptk = pT[kb % 3], 'pT%d' % (kb % 3)
                                c0 = 128 * (kb - 4 * tb) if kb >= 4 * tb else 0
                                k.op('act', [psk, 'pcK'], [ptk], lambda e, kb=kb, h=h, pss=pss, pt=pt, c0=c0: e.activation(out=pt[:, c0:512], in_=pss[:, c0:512], func=AF.Exp, bias=pcK[:, kb, h:h + 1]))
                                k.op('pe', ['Vc', ptk], [pok], lambda e, kb=kb, h=h, pt=pt, po=po, nkb=nkb, c0=c0: e.matmul(po[0:65, c0:512], lhsT=Vc[:, kb, h, :], rhs=pt[:, c0:512], start=(kb == 0), stop=(kb == nkb - 1), skip_group_check=True))
                                yield 1.0
                            def finish(h=h, po=po, pok=pok):
                                pb, pbk = psum()
                                k.op('pe', ['ones_f', 'fT1'], [pbk], lambda e, pb=pb: e.matmul(pb[0:64, :], lhsT=ones_f[64:65, 0:64], rhs=fT[64:65, 1, :], start=True, stop=True))
                                k.op('act', [pbk], ['fT0'], lambda e, pb=pb: e.activation(out=fT[0:64, 0, :], in_=pb[0:64, :], func=AF.Copy))
                                k.op('dve', [pok, 'fT0'], ['yd%d' % h], lambda e, h=h, po=po: e.tensor_tensor(out=yd[:, h, :], in0=po[0:64, :], in1=fT[0:64, 0, :], op=ALU.mult))
                            k.op('act', [pok], ['fT1'], lambda e, po=po: e.activation(out=fT[64:65, 1, :], in_=po[64:65, :], func=AF.Ln))
                            k.op('act', ['fT1'], ['fT1'], lambda e: e.activation(out=fT[64:65, 1, :], in_=fT[64:65, 1, :], func=AF.Exp, scale=-1.0))
                            pend.append(finish)
                            yield 1.0
                        fox_tail.extend(pend)
                        yield 1.0

                    gens = [s5_gen(), fox_gen()]
                    tot = [82.0, 4.0 * (4 * tb + 5)]
                    prog = [0.0, 0.0]
                    alive = [True, True]
                    while any(alive):
                        i = min((i_ for i_ in range(2) if alive[i_]), key=lambda i_: prog[i_])
                        try:
                            prog[i] += next(gens[i]) / tot[i]
                        except StopIteration:
                            alive[i] = False

                    tsets = [((tA[3], 'tA3'), (tA[4], 'tA4'), (tA[5], 'tA5')), ((tA[0], 'tA0'), (tA[1], 'tA1'), (tA[2], 'tA2'))]
                    for g in range(3):
                        (s0_, s0k), (s1_, s1k), (rs_, rsk_) = tsets[g % 2]
                        pss, pk = psum()
                        for cc in range(2):
                            sqt, sqk = (s0_.bitcast(BF16)[:, 0:512], s0k) if cc == 0 else (s1_.bitcast(BF16)[:, 0:512], s1k)
                            k.op('act', ['ya%d' % (2 * g + cc)], [sqk], lambda e, g=g, cc=cc, sqt=sqt: e.activation(out=sqt[:], in_=ya[:, 2 * g + cc, :], func=AF.Square))
                            k.op('pe', [sqk, 'ones_b'], [pk], lambda e, cc=cc, sqt=sqt, pss=pss: e.matmul(pss[:], lhsT=ones_b[:], rhs=sqt[:], start=(cc == 0), stop=(cc == 1)))
                        k.op('act', [pk], [rsk_], lambda e, pss=pss, rs_=rs_: e.activation(out=rs_[:], in_=pss[:], func=AF.Sqrt, scale=1.0 / 256, bias=epsc[:, 0:1]))
                        k.op('dve', [rsk_], [rsk_], lambda e, rs_=rs_: e.reciprocal(out=rs_[:], in_=rs_[:]))
                        for cc in range(2):
                            ch = 2 * g + cc
                            k.op('dve', ['ya%d' % ch, rsk_, 'gma'], ['ya%d' % ch], lambda e, ch=ch, rs_=rs_: e.scalar_tensor_tensor(out=yn[:, ch, :], in0=ya[:, ch, :], scalar=gma[:, l, ch:ch + 1], in1=rs_[:], op0=ALU.mult, op1=ALU.mult))
                    while fox_tail:
                        fox_tail.pop()()
                    (s0_, s0k), (s1_, s1k), (rs_, rsk_) = tsets[1]
                    pss, pk = psum()
                    for h in range(4):
                        sqt, sqk = (s0_.bitcast(BF16)[:, 0:512], s0k) if h % 2 == 0 else (s1_.bitcast(BF16)[:, 0:512], s1k)
                        k.op('act', ['yd%d' % h], [sqk], lambda e, h=h, sqt=sqt: e.activation(out=sqt[0:64, :], in_=yd[:, h, :], func=AF.Square))
                        k.op('pe', [sqk, 'ones_b'], [pk], lambda e, h=h, sqt=sqt, pss=pss: e.matmul(pss[:], lhsT=ones_b[0:64, :], rhs=sqt[0:64, :], start=(h == 0), stop=(h == 3)))
                    k.op('act', [pk], [rsk_], lambda e, pss=pss, rs_=rs_: e.activation(out=rs_[:], in_=pss[:], func=AF.Sqrt, scale=1.0 / 256, bias=epsc[:, 0:1]))
                    k.op('dve', [rsk_], [rsk_], lambda e, rs_=rs_: e.reciprocal(out=rs_[:], in_=rs_[:]))
                    for h in range(4):
                        k.op('dve', ['yd%d' % h, rsk_, 'gmd'], ['yd%d' % h], lambda e, h=h, rs_=rs_: e.scalar_tensor_tensor(out=ynd[:, h, :], in0=yd[:, h, :], scalar=gmd[:, l, h:h + 1], in1=rs_[0:64, :], op0=ALU.mult, op1=ALU.mult))
                    for dc in range(8):
                        wa_, wak = woa[dc % 2], 'woa%d' % (dc % 2)
                        wd_, wdk = wod[dc % 2], 'wod%d' % (dc % 2)
                        k.dma('sp', wa_[:].rearrange("p a b -> p (a b)"), wboa[l, dc], [('wboa', l)], [wak])
                        k.dma('sp', wd_[:].rearrange("p a b -> p (a b)"), wbod[l, dc], [('wbod', l)], [wdk])
                        po, pok = psum()
                        for kc in range(6):
                            k.op('pe', [wak, 'ya%d' % kc], [pok], lambda e, kc=kc, po=po, wa_=wa_: e.matmul(po[:], lhsT=wa_[:, kc, :], rhs=yn[:, kc, :], start=(kc == 0), stop=False))
                        for h in range(4):
                            k.op('pe', [wdk, 'yd%d' % h], [pok], lambda e, h=h, po=po, wd_=wd_: e.matmul(po[:], lhsT=wd_[:, h, :], rhs=ynd[:, h, :], start=False, stop=(h == 3)))
                        k.op('dve', [pok, 'xb%d' % dc], ['xb%d' % dc], lambda e, dc=dc, po=po: e.tensor_tensor(out=xb[:, dc, :], in0=xb[:, dc, :], in1=po[:], op=ALU.add))
                        k.dma('pool', xs1[dc * 128:(dc + 1) * 128, t0:t0 + TB], xb[:, dc, :], ['xb%d' % dc], [])
                k.barrier()

            with ExitStack() as eb:
                B = lambda name, shape, dt=F32: sb(eb, name, shape, dt)
                x1 = B("x1", [128, 9, TF]); h2 = B("h2", [128, 8, TF], BF16)
                hid = B("hid", [128, 32, TF], BF16)
                w1t = [B("w1t%d" % i, [128, 8, 512], BF16) for i in range(2)]
                w2t = [B("w2t%d" % i, [128, 16, 128], BF16) for i in range(3)]
                rl = [gls[:].bitcast(BF16)[:, i, :] for i in range(2)]
                sq0, sq1, rs = tS[0], tS[1], tS[2]
                for tf in range(NBF):
                    t0 = tf * TF
                    bi = lambda c_, tf=tf: (c_ - tf) % 9
                    for c8 in range(8):
                        k.dma('sp', x1[:, bi(c8), :], xs1[c8 * 128:(c8 + 1) * 128, t0:t0 + TF], [], ['x1_%d' % bi(c8)])

                    class XV:
                        def __getitem__(s_, idx):
                            return x1[idx[0], bi(idx[1]), idx[2]]
                    rmsnorm(((sq0, sq1), rs), (('tS0', 'tS1'), 'tS2'), XV(), (lambda c_: 'x1_%d' % bi(c_)), 8, TF, g2[:, l, :], h2, (lambda c_: 'h2_%d' % c_), D, eng2='dve')
                    for fg in range(8):
                        wt, wk = w1t[fg % 2], 'w1t%d' % (fg % 2)
                        k.dma('sp', wt[:].rearrange("p a b -> p (a b)"), wb1[l, fg], [('wb1', l, fg // 4)], [wk])
                        for fc in range(4):
                            for sbk in range(TF // 512):
                                ps, pk = psum()
                                for kc in range(8):
                                    k.op('pe', [wk, 'h2_%d' % kc], [pk], lambda e, kc=kc, fc=fc, sbk=sbk, wt=wt, ps=ps: e.matmul(ps[:], lhsT=wt[:, kc, fc * 128:(fc + 1) * 128], rhs=h2[:, kc, sbk * 512:(sbk + 1) * 512], start=(kc == 0), stop=(kc == 7)))
                                f = fg * 4 + fc
                                rt, rk_ = rl[(f * 2 + sbk) % 2], 'rl%d' % ((f * 2 + sbk) % 2)
                                k.op('act', [pk], [rk_], lambda e, ps=ps, rt=rt: e.activation(out=rt[:], in_=ps[:], func=AF.Relu))
                                k.op('dve', [rk_], ['hid%d' % f], lambda e, f=f, sbk=sbk, rt=rt: e.tensor_tensor(out=hid[:, f, sbk * 512:(sbk + 1) * 512], in0=rt[:], in1=rt[:], op=ALU.mult))
                    if tf == 0 and l + 1 < nl:
                        conv(l + 1)
                        setup(l + 1)
                    hk = ['hid%d' % f for f in range(32)]
                    for dc in range(8):
                        pss2 = [psum() for _ in range(TF // 512)]
                        for hf in range(2):
                            wi_ = (dc * 2 + hf) % 3
                            wt, wk = w2t[wi_], 'w2t%d' % wi_
                            k.dma('sp', wt[:].rearrange("p a b -> p (a b)"), wb2[l, dc][:, hf * 2048:(hf + 1) * 2048], [('wb2', l, dc // 4)], [wk])
                            for sbk in range(TF // 512):
                                ps, pk = pss2[sbk]
                                for f16 in range(16):
                                    f = hf * 16 + f16
                                    k.op('pe', [wk] + (hk if f == 0 else []), [pk], lambda e, f=f, f16=f16, sbk=sbk, wt=wt, ps=ps: e.matmul(ps[:], lhsT=wt[:, f16, :], rhs=hid[:, f, sbk * 512:(sbk + 1) * 512], start=(f == 0), stop=(f == 31)))
                        for sbk in range(TF // 512):
                            ps, pk = pss2[sbk]
                            k.op('dve', [pk, 'x1_%d' % bi(dc)], ['x1_%d' % bi(dc)], lambda e, dc=dc, sbk=sbk, ps=ps: e.tensor_tensor(out=x1[:, bi(dc), sbk * 512:(sbk + 1) * 512], in0=x1[:, bi(dc), sbk * 512:(sbk + 1) * 512], in1=ps[:], op=ALU.add))
                        if l != nl - 1:
                            k.dma('pool', xs[dc * 128:(dc + 1) * 128, t0:t0 + TF], x1[:, bi(dc), :], ['x1_%d' % bi(dc)], [])
                    if l == nl - 1:
                        for sbk in range(TF // 512):
                            class _V:
                                def __getitem__(s_, idx):
                                    return x1[idx[0], bi(idx[1]), sbk * 512 + idx[2].start: sbk * 512 + idx[2].stop]
                            rmsnorm(((sq0, sq1), rs), (('tS0', 'tS1'), 'tS2'), _V(), (lambda c_: 'x1_%d' % bi(c_)), 8, 512, gf, _V(), (lambda c_: 'x1_%d' % bi(c_)), D, eng2='dve')
                            for c8 in range(8):
                                k.dma('sp', outT[c8 * 128:(c8 + 1) * 128, t0 + sbk * 512:t0 + (sbk + 1) * 512], x1[:, bi(c8), sbk * 512:(sbk + 1) * 512], ['x1_%d' % bi(c8)], [])
                    else:
                        pass
                k.barrier()
        for i in range(NDS):
            if k.dcnt[i] > 0:
                k._wait('sp', i, k.dcnt[i])
    return nc


def _prep(inputs, b):
    f = lambda a: np.ascontiguousarray(np.asarray(a, dtype=np.float32))
    I = {k_: np.asarray(v, dtype=np.float32) for k_, v in inputs.items()}
    m = {}
    m["xT"] = f(I["x"][b].T)
    wi_ = I["w_in"]
    m["w_in_t"] = f(wi_[:, :, :2048].reshape(L, 8, 128, 8, 256).transpose(0, 3, 2, 1, 4).reshape(L, 8, 128, 2048))
    m["wif_t"] = f(wi_[:, :, 2048:2052].reshape(L, 8, 128, 4).transpose(0, 2, 1, 3).reshape(L, 128, 32))
    wo_ = I["w_out"]
    m["woa_t"] = f(wo_[:, 0:768, :].reshape(L, 6, 128, 8, 128).transpose(0, 3, 2, 1, 4).reshape(L, 8, 128, 768))
    m["wod_t"] = f(wo_[:, 768:1024, :].reshape(L, 4, 64, 8, 128).transpose(0, 3, 2, 1, 4).reshape(L, 8, 64, 512))
    m["w1_t"] = f(I["w_mlp_in"].reshape(L, 8, 128, 8, 512).transpose(0, 3, 2, 1, 4).reshape(L, 8, 128, 4096))
    m["w2_t"] = f(I["w_mlp_out"].reshape(L, 32, 128, 8, 128).transpose(0, 3, 2, 1, 4).reshape(L, 8, 128, 4096))
    pl = lambda a, nchunk, p=128: f(a.reshape(L, nchunk, p).transpose(2, 0, 1))
    m["g1"] = pl(I["norm1_g"], 8); m["g2"] = pl(I["norm2_g"], 8)
    m["gma"] = pl(I["mix_norm_g"][:, :768], 6); m["gmd"] = pl(I["mix_norm_g"][:, 768:], 4, 64)
    m["gf"] = f(I["final_g"].reshape(8, 128).T)
    m["sgug"] = f(np.broadcast_to(I["sgu_norm_g"][None], (128, L, 256)))
    m["sguw"] = f(I["sgu_w"].transpose(0, 3, 1, 2))
    sb_ = I["sgu_b"].reshape(L, 2, 2, 128)
    m["sgub"] = f(np.broadcast_to(sb_.transpose(2, 0, 1, 3)[:, None], (2, 64, L, 2, 128)).reshape(128, L, 2, 128))
    st = lambda a: f(a.reshape(L, 8, 2, 64).transpose(2, 3, 0, 1).reshape(128, L, 8))
    m["lre"] = st(I["s5_lambda_re"]); m["lim"] = st(I["s5_lambda_im"])
    m["ldt"] = st(np.broadcast_to(I["s5_log_dt"][:, :, None], (L, 16, 64)))

    def padb(a):
        o = np.zeros((L, 2, 64, 8, 128), np.float32)
        for j in range(8):
            for gl in range(2):
                c0 = 32 * (j % 4) + 16 * gl
                o[:, gl, :, j, c0:c0 + 16] = a[:, 2 * j + gl]
        return o.reshape(L, 128, 8, 128)
    m["bre"] = padb(I["s5_b_re"]); m["bim"] = padb(I["s5_b_im"])
    m["cre"] = padb(I["s5_c_re"].transpose(0, 1, 3, 2)); m["cim"] = padb(I["s5_c_im"].transpose(0, 1, 3, 2))
    m["s5d"] = pl(I["s5_d"], 2); m["gluw"] = f(I["s5_glu_w"]); m["glub"] = pl(I["s5_glu_b"], 2)
    m["cw"] = f(I["lru_conv_w"].reshape(L, 4, 2, 128).transpose(3, 0, 2, 1))
    m["cb"] = pl(I["lru_conv_b"], 2)

    def bd(a):
        o = np.zeros((L, 2, 64, 2, 2, 64), np.float32)
        for c in range(2):
            for hl in range(2):
                o[:, hl, :, c, hl, :] = a[:, 2 * c + hl]
        return o.reshape(L, 128, 2, 128)
    m["wa"] = bd(I["lru_wa"]); m["wx"] = bd(I["lru_wx"])
    m["ba"] = pl(I["lru_ba"].reshape(L, 256), 2); m["bx"] = pl(I["lru_bx"].reshape(L, 256), 2)
    m["llam"] = pl(I["lru_lambda"], 2)
    m["bfr"] = f(np.broadcast_to(np.tile(I["fox_fgate_b"], (1, 4))[None], (128, L, 16)))
    m["ident"] = np.eye(128, dtype=np.float32)
    m["tri"] = np.triu(np.ones((128, 128), np.float32))
    kk = np.arange(128)[:, None, None] + 128 * np.arange(4)[None, :, None]
    m["amask"] = np.where(kk <= np.arange(512)[None, None, :], 0.0, NEG).astype(np.float32)
    m["iota"] = f(np.broadcast_to(np.arange(512, dtype=np.float32)[None], (128, 512)))
    return m


def kernel(**inputs):
    nc = build(L)
    in_maps = [_prep(inputs, b) for b in range(NCORE)]
    res = run_bass_kernel_spmd(nc, in_maps, core_ids=list(range(NCORE)))
    out = np.stack([np.asarray(res.results[b]["outT"], dtype=np.float32).T for b in range(NCORE)], 0)
    return np.ascontiguousarray(out)
```

```python
import math
from contextlib import ExitStack
import numpy as np
import concourse.bass as bass
import concourse.mybir as mybir
from concourse.bass_utils import run_bass_kernel_spmd

F32 = mybir.dt.float32
BF16 = mybir.dt.bfloat16
AF = mybir.ActivationFunctionType
ALU = mybir.AluOpType

D = 1024
S = 4096
L = 4
NCORE = 4
TB = 512
NB = S // TB
TF = 1024
NBF = S // TF
DIN = 2052
EPS = 1e-6
PI = math.pi
NEG = -30000.0
NDS = 24
GELU = AF.Gelu
NOSYNC_SAME = ()


class K:
    def __init__(s, nc, es):
        s.nc = nc
        s.es = es
        s.epoch = 0
        s.engs = {'pe': nc.tensor, 'act': nc.scalar, 'dve': nc.vector, 'pool': nc.gpsimd, 'sp': nc.sync}
        s.sem = {e: es.enter_context(nc.semaphore('s_' + e)) for e in s.engs}
        s.cnt = {e: 0 for e in s.engs}
        s.dsem = [es.enter_context(nc.semaphore('d%d' % i)) for i in range(NDS)]
        s.dcnt = [0] * NDS
        s.dnext = 0
        s.seen = {e: {} for e in s.engs}
        s.lastw = {}
        s.readers = {}
        s.psn = 0
        s.xsem = {}
        s.xcnt = {}
        s.xn = 0

    def _wait(s, e, sk, val):
        if e == sk and (e == 'pe' or e in NOSYNC_SAME):
            return
        if s.seen[e].get(sk, 0) >= val:
            return
        if isinstance(sk, str):
            sem = s.sem[sk]
        elif isinstance(sk, int):
            sem = s.dsem[sk]
        else:
            sem = s.xsem[sk]
        s.engs[e].wait_ge(sem, val)
        s.seen[e][sk] = val

    def deps(s, e, reads, writes):
        for k in reads:
            if k in s.lastw:
                s._wait(e, *s.lastw[k])
        for k in writes:
            if k in s.lastw:
                s._wait(e, *s.lastw[k])
            for sk, v in s.readers.get(k, {}).items():
                s._wait(e, sk, v)

    def _record(s, tok, reads, writes):
        for k in reads:
            d = s.readers.setdefault(k, {})
            d[tok[0]] = max(d.get(tok[0], 0), tok[1])
        for k in writes:
            s.lastw[k] = tok
            s.readers[k] = {}

    def op(s, e, reads, writes, fn):
        s.deps(e, reads, writes)
        ins = fn(s.engs[e])
        s.cnt[e] += 1
        ins.then_inc(s.sem[e], 1)
        s._record((e, s.cnt[e]), reads, writes)

    def dma(s, q, out, in_, reads, writes):
        i = s.dnext
        s.dnext = (i + 1) % NDS
        if s.dcnt[i] > 0:
            s._wait(q, i, s.dcnt[i])
        s.deps(q, reads, writes)
        s.dcnt[i] += 16
        s.engs[q].dma_start(out=out, in_=in_).then_inc(s.dsem[i], 16)
        s._record((i, s.dcnt[i]), reads, writes)

    def dma_sw(s, out, in_, reads, writes, **kw):
        key = ('x', s.xn % 4)
        s.xn += 1
        if key not in s.xsem:
            s.xsem[key] = s.es.enter_context(s.nc.semaphore('x%d' % key[1]))
            s.xcnt[key] = 0
        s.deps('pool', reads, writes)
        s.xcnt[key] += 16
        s.engs['pool'].dma_start(out=out, in_=in_, **kw).then_inc(s.xsem[key], 16)
        s._record((key, s.xcnt[key]), reads, writes)
        return (key, s.xcnt[key])

    def barrier(s):
        for e in s.engs:
            for f in s.engs:
                if f != e and s.cnt[f] > 0:
                    s._wait(e, f, s.cnt[f])
            for i in range(NDS):
                if s.dcnt[i] > 0:
                    s._wait(e, i, s.dcnt[i])
        s.lastw = {k_: v_ for k_, v_ in s.lastw.items() if isinstance(v_[0], tuple)}
        s.readers = {}
        s.epoch += 1
        for e in s.engs:
            s.sem[e] = s.es.enter_context(s.nc.semaphore('s_%s_%d' % (e, s.epoch)))
            s.cnt[e] = 0
        for e in s.engs:
            for f in s.engs:
                s.seen[e].pop(f, None)


def build(nl=L):
    nc = bass.Bass("TRN2", target_bir_lowering=False)

    def din(name, shape):
        return nc.dram_tensor(name, list(shape), F32, kind="ExternalInput").ap()

    xT_in = din("xT", [D, S])
    w_in_t = din("w_in_t", [L, 8, 128, 2048])
    wif_t = din("wif_t", [L, 128, 32])
    woa_t = din("woa_t", [L, 8, 128, 768])
    wod_t = din("wod_t", [L, 8, 64, 512])
    w1_t = din("w1_t", [L, 8, 128, 4096])
    w2_t = din("w2_t", [L, 8, 128, 4096])
    wbin = nc.dram_tensor("wbin", [L, 8, 128, 2048], BF16).ap()
    wbif = nc.dram_tensor("wbif", [L, 128, 32], BF16).ap()
    wboa = nc.dram_tensor("wboa", [L, 8, 128, 768], BF16).ap()
    wbod = nc.dram_tensor("wbod", [L, 8, 64, 512], BF16).ap()
    wb1 = nc.dram_tensor("wb1", [L, 8, 128, 4096], BF16).ap()
    wb2 = nc.dram_tensor("wb2", [L, 8, 128, 4096], BF16).ap()
    g1_d = din("g1", [128, L, 8])
    g2_d = din("g2", [128, L, 8])
    gma_d = din("gma", [128, L, 6])
    gmd_d = din("gmd", [64, L, 4])
    gf_d = din("gf", [128, 8])
    sgug_d = din("sgug", [128, L, 256])
    sguw_d = din("sguw", [L, 128, 4, 128])
    sgub_d = din("sgub", [128, L, 2, 128])
    lre_d = din("lre", [128, L, 8])
    lim_d = din("lim", [128, L, 8])
    ldt_d = din("ldt", [128, L, 8])
    bre_d = din("bre", [L, 128, 8, 128])
    bim_d = din("bim", [L, 128, 8, 128])
    cre_d = din("cre", [L, 128, 8, 128])
    cim_d = din("cim", [L, 128, 8, 128])
    s5d_d = din("s5d", [128, L, 2])
    gluw_d = din("gluw", [L, 256, 256])
    glub_d = din("glub", [128, L, 2])
    cw_d = din("cw", [128, L, 2, 4])
    cb_d = din("cb", [128, L, 2])
    wa_d = din("wa", [L, 128, 2, 128])
    wx_d = din("wx", [L, 128, 2, 128])
    ba_d = din("ba", [128, L, 2])
    bx_d = din("bx", [128, L, 2])
    llam_d = din("llam", [128, L, 2])
    bf_d = din("bfr", [128, L, 16])
    ident_d = din("ident", [128, 128])
    tri_d = din("tri", [128, 128])
    amask_d = din("amask", [128, 4, 512])
    iota_d = din("iota", [128, 512])
    outT = nc.dram_tensor("outT", [D, S], F32, kind="ExternalOutput").ap()
    xs = nc.dram_tensor("xs", [D, S], F32).ap()
    xs1 = nc.dram_tensor("xs1", [D, S], F32).ap()

    def fm(ap2d, t0, n):
        return ap2d[:, t0:t0 + n].rearrange("(c p) t -> p c t", p=128)

    with ExitStack() as es:
        E = es.enter_context
        k = K(nc, es)
        PS = [E(nc.psum_tensor("ps%d" % i, [128, 512], F32)) for i in range(8)]

        def psum():
            i = k.psn
            k.psn = (i + 1) % 6
            return PS[i], 'ps%d' % i

        def psfix(i):
            return PS[i], 'ps%d' % i

        uid = [0]

        def sb(stack, name, shape, dt=F32):
            uid[0] += 1
            return stack.enter_context(nc.sbuf_tensor("sb_%s_%d" % (name, uid[0]), list(shape), dt))

        g1 = sb(es, "g1", [128, L, 8]); g2 = sb(es, "g2", [128, L, 8])
        gma = sb(es, "gma", [128, L, 6]); gmd = sb(es, "gmd", [64, L, 4]); gf = sb(es, "gf", [128, 8])
        lre = sb(es, "lre", [128, L, 8]); lim = sb(es, "lim", [128, L, 8]); ldt = sb(es, "ldt", [128, L, 8])
        s5d = sb(es, "s5d", [128, L, 2]); glub = sb(es, "glub", [128, L, 2])
        cw = sb(es, "cw", [128, L, 2, 4]); cb = sb(es, "cb", [128, L, 2])
        ba = sb(es, "ba", [128, L, 2]); bx = sb(es, "bx", [128, L, 2]); llam = sb(es, "llam", [128, L, 2])
        bfr = sb(es, "bfr", [128, L, 16])
        ident_f = sb(es, "ident_f", [128, 128]); ident_b = sb(es, "ident_b", [128, 128], BF16)
        tri_f = sb(es, "tri_f", [128, 128])
        ones_f = sb(es, "ones_f", [128, 128]); ones_b = sb(es, "ones_b", [128, 128], BF16)
        for t, d_, nm in [(g1, g1_d, 'g1'), (g2, g2_d, 'g2'), (gma, gma_d, 'gma'), (gmd, gmd_d, 'gmd'),
                          (gf, gf_d, 'gf'), (lre, lre_d, 'lre'), (lim, lim_d, 'lim'), (ldt, ldt_d, 'ldt'),
                          (s5d, s5d_d, 's5d'), (glub, glub_d, 'glub'), (cw, cw_d, 'cw'), (cb, cb_d, 'cb'),
                          (ba, ba_d, 'ba'), (bx, bx_d, 'bx'), (llam, llam_d, 'llam'), (bfr, bf_d, 'bfr'),
                          (ident_f, ident_d, 'ident_f'), (tri_f, tri_d, 'tri_f')]:
            k.dma('sp', t[:], d_, [], [nm])
        conv_hist = []
        conv_hist.append(k.dma_sw(ident_b[:], ident_d, [], ['ident_b']))

        def f2k(ap):
            names = " ".join("d%d" % i for i in range(len(ap.shape)))
            return ap.rearrange("%s -> (%s)" % (names, names)).rearrange("(r c) -> r c", c=2048)
        def conv1(out, in_, key):
            if len(conv_hist) >= 2:
                k._wait('pool', *conv_hist[-2])
            conv_hist.append(k.dma_sw(out, in_, [], [key]))

        def conv(l_):
            conv1(f2k(wbin[l_]), f2k(w_in_t[l_]), ('wbin', l_))
            conv1(wbif[l_], wif_t[l_], ('wbif', l_))
            conv1(f2k(wboa[l_]), f2k(woa_t[l_]), ('wboa', l_))
            conv1(f2k(wbod[l_]), f2k(wod_t[l_]), ('wbod', l_))
            for hh_ in range(2):
                conv1(f2k(wb1[l_, 4 * hh_:4 * hh_ + 4]), f2k(w1_t[l_, 4 * hh_:4 * hh_ + 4]), ('wb1', l_, hh_))
            for hh_ in range(2):
                conv1(f2k(wb2[l_, 4 * hh_:4 * hh_ + 4]), f2k(w2_t[l_, 4 * hh_:4 * hh_ + 4]), ('wb2', l_, hh_))
        conv(0)
        epsc = sb(es, "epsc", [128, 1])
        k.op('dve', [], ['epsc'], lambda e: e.memset(epsc[:], EPS))
        k.op('dve', [], ['ones_f'], lambda e: e.memset(ones_f[:], 1.0))
        k.op('dve', [], ['ones_b'], lambda e: e.memset(ones_b[:], 1.0))

        def rmsnorm(stack_tmp, keys, src, srck, nch, ntok, gain, dst, dstk, dim, eng2='dve'):
            sq, rs = stack_tmp
            sqks, rsk = keys
            skf = srck if callable(srck) else (lambda c_: srck)
            dkf = dstk if callable(dstk) else (lambda c_: dstk)
            for s0 in range(0, ntok, 512):
                pss, pk = psum()
                for c in range(nch):
                    sqt, sqk = sq[c % 2].bitcast(BF16)[:, 0:512], sqks[c % 2]
                    if c % 2 == 0:
                        k.op('act', [skf(c)], [sqk], lambda e, c=c, sqt=sqt: e.activation(out=sqt[:], in_=src[:, c, s0:s0 + 512], func=AF.Square))
                    else:
                        k.op('dve', [skf(c)], [sqk], lambda e, c=c, sqt=sqt: e.tensor_tensor(out=sqt[:], in0=src[:, c, s0:s0 + 512], in1=src[:, c, s0:s0 + 512], op=ALU.mult))
                    k.op('pe', [sqk, 'ones_b'], [pk], lambda e, c=c, sqt=sqt: e.matmul(pss[:], lhsT=ones_b[:], rhs=sqt[:], start=(c == 0), stop=(c == nch - 1)))
                k.op('act', [pk], [rsk], lambda e: e.activation(out=rs[:], in_=pss[:], func=AF.Sqrt, scale=1.0 / dim, bias=epsc[:, 0:1]))
                k.op('dve', [rsk], [rsk], lambda e: e.reciprocal(out=rs[:], in_=rs[:]))
                for c in range(nch):
                    en = eng2 if c % 2 else 'dve'
                    k.op(en, [skf(c), rsk], [dkf(c)], lambda e, c=c: e.scalar_tensor_tensor(out=dst[:, c, s0:s0 + 512], in0=src[:, c, s0:s0 + 512], scalar=gain[:, c:c + 1], in1=rs[:], op0=ALU.mult, op1=ALU.mult))

        cosT = sb(es, "cosT", [128, 8, 512]); sinT = sb(es, "sinT", [128, 8, 512])
        bT = sb(es, "bT", [128, 2, 8, 128], BF16)
        cre = sb(es, "cre", [128, 8, 128], BF16); cimn = sb(es, "cimn", [128, 8, 128], BF16)
        wm = sb(es, "wm", [128, 4, 128], BF16); sgug = sb(es, "sgug", [128, 256]); sgub = sb(es, "sgub", [128, 2, 128])
        gluw = sb(es, "gluw", [128, 2, 256], BF16); gls = sb(es, "gls", [128, 2, 256])
        wab = sb(es, "wab", [128, 2, 128], BF16); wxb = sb(es, "wxb", [128, 2, 128], BF16)
        s5s = sb(es, "s5s", [128, 16, 8]); s5i = sb(es, "s5i", [128, 1, 8], mybir.dt.int32)
        tI = sb(es, "tI", [128, 512], mybir.dt.int32); lsc = sb(es, "lsc", [128, 8])
        tSall = sb(es, "tSall", [128, 4, 512])
        tS = [tSall[:, i, :] for i in range(4)]
        iota = tS[3]
        stg = tSall[:, 0:2, :].rearrange("p a (b c) -> p (a b) c", b=4)
        c_ = lambda i: s5s[:, i, :]
        DT, MAG, TH, SN, CS, ARE, AIM, DEN, FRE, FIM, T1, T2, C5, S5_, RR = range(15)

        def setup(l):
            k.dma('sp', gls[:], gluw_d[l].rearrange("(c p) n -> p c n", p=128), [], ['rl0', 'rl1'])
            k.op('dve', ['rl0', 'rl1'], ['gluw'], lambda e: e.tensor_copy(out=gluw[:], in_=gls[:]))
            k.dma('sp', stg[:, 0:2, :], wa_d[l], [], ['tS0', 'tS1'])
            k.op('dve', ['tS0', 'tS1'], ['wab'], lambda e: e.tensor_copy(out=wab[:], in_=stg[:, 0:2, :]))
            k.dma('sp', stg[:, 2:4, :], wx_d[l], [], ['tS0', 'tS1'])
            k.op('dve', ['tS0', 'tS1'], ['wxb'], lambda e: e.tensor_copy(out=wxb[:], in_=stg[:, 2:4, :]))
            k.dma('sp', stg[:], cre_d[l], ['tS0', 'tS1'], ['tS0', 'tS1'])
            k.op('dve', ['tS0', 'tS1'], ['cre'], lambda e: e.tensor_copy(out=cre[:], in_=stg[:]))
            k.dma('sp', iota[:], iota_d, [], ['tS3'])
            k.dma('sp', sgug[:], sgug_d[:, l, :], [], ['sgug'])
            k.dma('sp', sgub[:], sgub_d[:, l, :, :], [], ['sgub'])
            k.dma('sp', stg[:], cim_d[l], [], ['tS0', 'tS1'])
            k.op('act', ['tS0', 'tS1'], ['cimn'], lambda e: e.activation(out=cimn[:], in_=stg[:], func=AF.Copy, scale=-1.0))
            k.dma('sp', stg[:, 0:4, :], sguw_d[l], ['tS0', 'tS1'], ['tS0', 'tS1'])
            for h in range(4):
                k.op('dve', ['tS0', 'tS1', 'tri_f'], ['wm'], lambda e, h=h: e.tensor_tensor(out=wm[:, h, :], in0=stg[:, h, :], in1=tri_f[:], op=ALU.mult))
            c_ = lambda i: s5s[:, i, :]
            lr, li, ld = lre[:, l, :], lim[:, l, :], ldt[:, l, :]
            DT, MAG, TH, SN, CS, ARE, AIM, DEN, FRE, FIM, T1, T2, C5, S5_, RR = range(15)
            v = lambda fn: k.op('dve', ['s5s', 'lre', 'lim', 'ldt'], ['s5s'], fn)
            a_ = lambda fn: k.op('act', ['s5s', 'lre', 'lim', 'ldt'], ['s5s'], fn)
            a_(lambda e: e.activation(out=c_(DT), in_=ld, func=AF.Exp))
            v(lambda e: e.tensor_tensor(out=c_(T1), in0=lr, in1=c_(DT), op=ALU.mult))
            a_(lambda e: e.activation(out=c_(MAG), in_=c_(T1), func=AF.Exp))
            v(lambda e: e.tensor_tensor(out=c_(TH), in0=li, in1=c_(DT), op=ALU.mult))

            C1 = 6.28125
            C2 = 2 * PI - C1

            def red_sin(dst, X, KI, KF, M, rk, wk):
                o = lambda en, fn: k.op(en, rk + wk, wk, fn)
                o('dve', lambda e: e.tensor_scalar(out=KF, in0=X, scalar1=1.0 / (2 * PI), scalar2=None, op0=ALU.mult))
                o('dve', lambda e: e.tensor_copy(out=KI, in_=KF))
                o('dve', lambda e: e.tensor_copy(out=KF, in_=KI))
                o('dve', lambda e: e.scalar_tensor_tensor(out=X, in0=KF, scalar=-C1, in1=X, op0=ALU.mult, op1=ALU.add))
                o('dve', lambda e: e.scalar_tensor_tensor(out=X, in0=KF, scalar=-C2, in1=X, op0=ALU.mult, op1=ALU.add))
                o('dve', lambda e: e.tensor_scalar(out=M, in0=X, scalar1=PI, scalar2=2 * PI, op0=ALU.is_gt, op1=ALU.mult))
                o('dve', lambda e: e.tensor_tensor(out=X, in0=X, in1=M, op=ALU.subtract))
                o('dve', lambda e: e.tensor_scalar(out=M, in0=X, scalar1=-PI, scalar2=2 * PI, op0=ALU.is_lt, op1=ALU.mult))
                o('dve', lambda e: e.tensor_tensor(out=X, in0=X, in1=M, op=ALU.add))
                o('act', lambda e: e.activation(out=dst, in_=X, func=AF.Sin))

            def sincos(dst_s, dst_c, src, mul):
                kk_ = ['s5s', 'lre', 'lim', 'ldt']
                v(lambda e: e.tensor_scalar(out=c_(T1), in0=src, scalar1=float(mul), scalar2=16 * PI, op0=ALU.mult, op1=ALU.add))
                red_sin(dst_s, c_(T1), s5i[:, 0, :], c_(T2), c_(RR), kk_, ['s5s', 's5i'])
                v(lambda e: e.tensor_scalar(out=c_(T1), in0=src, scalar1=float(mul), scalar2=16.5 * PI, op0=ALU.mult, op1=ALU.add))
                red_sin(dst_c, c_(T1), s5i[:, 0, :], c_(T2), c_(RR), kk_, ['s5s', 's5i'])
            sincos(c_(SN), c_(CS), c_(TH), 1.0)
            sincos(c_(S5_), c_(C5), c_(TH), float(TB))
            v(lambda e: e.tensor_tensor(out=c_(ARE), in0=c_(MAG), in1=c_(CS), op=ALU.mult))
            v(lambda e: e.tensor_tensor(out=c_(AIM), in0=c_(MAG), in1=c_(SN), op=ALU.mult))
            v(lambda e: e.tensor_tensor(out=c_(T1), in0=lr, in1=lr, op=ALU.mult))
            v(lambda e: e.tensor_tensor(out=c_(T2), in0=li, in1=li, op=ALU.mult))
            v(lambda e: e.tensor_tensor(out=c_(DEN), in0=c_(T1), in1=c_(T2), op=ALU.add))
            v(lambda e: e.reciprocal(out=c_(DEN), in_=c_(DEN)))
            v(lambda e: e.tensor_scalar(out=c_(RR), in0=c_(ARE), scalar1=-1.0, scalar2=None, op0=ALU.add))
            v(lambda e: e.tensor_tensor(out=c_(T1), in0=c_(RR), in1=lr, op=ALU.mult))
            v(lambda e: e.tensor_tensor(out=c_(T2), in0=c_(AIM), in1=li, op=ALU.mult))
            v(lambda e: e.tensor_tensor(out=c_(T1), in0=c_(T1), in1=c_(T2), op=ALU.add))
            v(lambda e: e.tensor_tensor(out=c_(FRE), in0=c_(T1), in1=c_(DEN), op=ALU.mult))
            v(lambda e: e.tensor_tensor(out=c_(T1), in0=c_(AIM), in1=lr, op=ALU.mult))
            v(lambda e: e.tensor_tensor(out=c_(T2), in0=c_(RR), in1=li, op=ALU.mult))
            v(lambda e: e.tensor_tensor(out=c_(T1), in0=c_(T1), in1=c_(T2), op=ALU.subtract))
            v(lambda e: e.tensor_tensor(out=c_(FIM), in0=c_(T1), in1=c_(DEN), op=ALU.mult))
            Bre = cosT
            Bim = sinT
            k.dma('sp', Bre[:, :, 0:128], bre_d[l], [], ['cosT'])
            k.dma('sp', Bim[:, :, 0:128], bim_d[l], [], ['sinT'])
            for j in range(8):
                fr, fi = s5s[:, FRE, j:j + 1], s5s[:, FIM, j:j + 1]
                k.op('dve', ['sinT', 's5s'], ['tS0'], lambda e, j=j, fi=fi: e.tensor_scalar(out=tS[0][:, 0:128], in0=Bim[:, j, 0:128], scalar1=fi, scalar2=None, op0=ALU.mult))
                k.op('dve', ['cosT', 'tS0', 's5s'], ['tS1'], lambda e, j=j, fr=fr: e.scalar_tensor_tensor(out=tS[1][:, 0:128], in0=Bre[:, j, 0:128], scalar=fr, in1=tS[0][:, 0:128], op0=ALU.mult, op1=ALU.subtract))
                k.op('dve', ['cosT', 's5s'], ['tS0'], lambda e, j=j, fi=fi: e.tensor_scalar(out=tS[0][:, 0:128], in0=Bre[:, j, 0:128], scalar1=fi, scalar2=None, op0=ALU.mult))
                k.op('dve', ['sinT', 'tS0', 's5s'], ['tS2'], lambda e, j=j, fr=fr: e.scalar_tensor_tensor(out=tS[2][:, 0:128], in0=Bim[:, j, 0:128], scalar=fr, in1=tS[0][:, 0:128], op0=ALU.mult, op1=ALU.add))
                for ri, tt, tk in ((0, tS[1], 'tS1'), (1, tS[2], 'tS2')):
                    pst, pk = psum()
                    k.op('pe', [tk, 'ident_f'], [pk], lambda e, tt=tt, pst=pst: e.transpose(out=pst[:, 0:128], in_=tt[:, 0:128], identity=ident_f[:]))
                    k.op('act', [pk], ['bT'], lambda e, ri=ri, j=j, pst=pst: e.activation(out=bT[:, ri, j, :], in_=pst[:, 0:128], func=AF.Copy))
            for j in range(8):
                th = s5s[:, TH, j:j + 1]
                for tab, tkk, off in ((sinT, 'sinT', 16 * PI), (cosT, 'cosT', 16.5 * PI)):
                    k.op('dve', ['tS3', 's5s'], ['tS0'], lambda e, th=th, off=off: e.tensor_scalar(out=tS[0][:], in0=iota[:], scalar1=th, scalar2=off, op0=ALU.mult, op1=ALU.add))
                    red_sin(tab[:, j, :], tS[0][:], tI[:], tS[1][:], tS[2][:], [], ['tS0', 'tS1', 'tS2', 'tI', tkk])
            k.op('act', ['llam'], ['lsc'], lambda e: e.activation(out=lsc[:, 0:2], in_=llam[:, l, :], func=AF.Exp, scale=-1.0))
            k.op('act', ['lsc'], ['lsc'], lambda e: e.activation(out=lsc[:, 2:4], in_=lsc[:, 0:2], func=AF.Ln, bias=1.0))
            k.op('dve', ['lsc'], ['lsc'], lambda e: e.tensor_scalar(out=lsc[:, 4:6], in0=lsc[:, 2:4], scalar1=-8.0, scalar2=None, op0=ALU.mult))


        setup(0)
        k.barrier()
        for l in range(nl):
            x_src = xT_in if l == 0 else xs
            with ExitStack() as ea:
                A = lambda name, shape, dt=F32: sb(ea, name, shape, dt)
                wr = [A("wr%d" % i, [128, 8, 256], BF16) for i in range(3)]
                wif = A("wif", [128, 8, 4], BF16)
                woa = [A("woa%d" % i, [128, 6, 128], BF16) for i in range(2)]
                wod = [A("wod%d" % i, [64, 4, 128], BF16) for i in range(2)]
                Kc = A("Kc", [65, 4, S], BF16); Vc = A("Vc", [128, S // 128, 4, 65], BF16)
                pcK = A("pcK", [128, S // 128, 4])
                amask = A("amask", [128, 128], BF16)
                vA = A("vA", [128, 2, 2, 128], BF16)
                xb = A("xb", [128, 8, TB]); hT = A("hT", [128, 8, TB], BF16)
                ya = A("ya", [128, 6, TB], BF16); yd = A("yd", [64, 4, TB], BF16)
                yn = ya; ynd = yd
                ug = A("m2a", [128, 2, TB]); uf = A("uf", [128, 2, TB]); xc = ug; ub = A("ub", [128, 2, TB], BF16); xcb = A("b2a", [128, 2, TB], BF16); qT = A("qT", [65, 4, TB], BF16); pcx = A("pcx", [128, 4, 4, 65])
                fT = tSall[:, 2:4, :]
                tAall = A("tAall", [128, 6, 512])
                tA = [tAall[:, i, :] for i in range(6)]
                sq0, sq1, rs = tA[3], tA[4], tA[5]
                gv = tA[1][:, 0:256]; rd = tA[1]
                sS = A("sS", [128, 4, 2, TB], BF16)
                pT = [tI[:].bitcast(BF16)[:, 0:512], tI[:].bitcast(BF16)[:, 512:1024], A("pT2", [128, 512], BF16)]
                sm = A("sm", [128, 12])
                wend = A("wend", [128, 2, 8]); winit = A("winit", [128, 2, 8])
                hcar = A("hcar", [128, 2])
                xbuf = A("xbuf", [128, 2, TB + 3])
                lft = A("lft", [128, 16]); car = [A("car%d" % i, [128, 4]) for i in range(2)]
                ygf = tSall[:, 0:2, :]; ygb = gls[:].bitcast(BF16)

                k.dma('sp', wif[:].rearrange("p a b -> p (a b)"), wbif[l], [('wbif', l)], ['wif'])
                k.dma('sp', tA[0][:, 0:128], amask_d[:, 0, 0:128], [], ['tA0'])
                k.op('pool', ['tA0'], ['amask'], lambda e: e.tensor_copy(out=amask[:], in_=tA[0][:, 0:128]))
                wslots = [None, None, None]
                wuse = [0, 0, 0]
                wtick = [0]

                def wget(grp):
                    wtick[0] += 1
                    for i in (0, 1, 2):
                        if wslots[i] == grp:
                            wuse[i] = wtick[0]
                            return wr[i], 'wr%d' % i
                    i = min((0, 1, 2), key=lambda i_: wuse[i_])
                    wslots[i] = grp
                    wuse[i] = wtick[0]
                    k.dma('sp', wr[i][:].rearrange("p a b -> p (a b)"), wbin[l, grp], [('wbin', l)], ['wr%d' % i])
                    return wr[i], 'wr%d' % i
                k.op('dve', [], ['vA'], lambda e: e.memset(vA[:], 0.0))
                k.op('pool', [], ['Vc'], lambda e: e.memset(Vc[:], 1.0))
                k.op('dve', [], ['xbuf'], lambda e: e.memset(xbuf[:], 0.0))
                k.op('dve', [], ['pcx'], lambda e: e.memset(pcx[:], 0.0))
                k.op('pool', [], ['Kc'], lambda e: e.memset(Kc[64:65, :, :], 1.0))
                k.op('dve', [], ['hcar'], lambda e: e.memset(hcar[:], 0.0))
                k.op('dve', [], ['winit'], lambda e: e.memset(winit[:], 0.0))
                k.op('dve', [], ['car0'], lambda e: e.memset(car[0][:], 0.0))

                wcur = {}

                def inproj_fm(col0, ncol, evac):
                    base = (col0 // 256) * 256
                    wt, wk = wget(base // 256)
                    o = col0 - base
                    ps, pk = psum()
                    for kc in range(8):
                        k.op('pe', [wk, 'hT%d' % kc], [pk], lambda e, kc=kc: e.matmul(ps[0:ncol, :], lhsT=wt[:, kc, o:o + ncol], rhs=hT[:, kc, :], start=(kc == 0), stop=(kc == 7)))
                    evac(ps, pk)

                def inproj_tm(q, col0, ncol):
                    wt, wk = wget(col0 // 256)
                    ps, pk = psum()
                    for kc in range(8):
                        k.op('pe', [wk, 'hT%d' % kc], [pk], lambda e, kc=kc: e.matmul(ps[:, 0:ncol], lhsT=hT[:, kc, q * 128:(q + 1) * 128], rhs=wt[:, kc, 0:ncol], start=(kc == 0), stop=(kc == 7)))
                    return ps, pk

                for tb in range(NB):
                    t0 = tb * TB
                    for c8 in range(8):
                        k.dma('sp', xb[:, c8, :], x_src[c8 * 128:(c8 + 1) * 128, t0:t0 + TB], [], ['xb%d' % c8])
                    rmsnorm(((sq0, sq1), rs), (('tA3', 'tA4'), 'tA5'), xb, (lambda c_: 'xb%d' % c_), 8, TB, g1[:, l, :], hT, (lambda c_: 'hT%d' % c_), D, eng2='dve')

                    fox_st = {}
                    def fox_f1():
                        pf, pfk = psfix(7)
                        for q in range(4):
                            for kc in range(8):
                                k.op('pe', ['wif', 'hT%d' % kc], [pfk], lambda e, kc=kc, q=q: e.matmul(pf[:, 4 * q:4 * q + 4], lhsT=hT[:, kc, q * 128:(q + 1) * 128], rhs=wif[:, kc, 0:4], start=(kc == 0), stop=(kc == 7)))
                        k.op('dve', [pfk, 'bfr'], ['lft'], lambda e: e.tensor_tensor(out=lft[:], in0=pf[:, 0:16], in1=bfr[:, l, :], op=ALU.add))
                        k.op('act', ['lft'], ['lft'], lambda e: e.activation(out=lft[:], in_=lft[:], func=AF.Exp, scale=-1.0))
                        k.op('act', ['lft'], ['lft'], lambda e: e.activation(out=lft[:], in_=lft[:], func=AF.Ln, bias=1.0))

                    def fox_f2():
                        pcs, pcsk = psum()
                        fox_st['pcs'] = (pcs, pcsk)
                        for q in range(4):
                            k.op('pe', ['tri_f', 'lft'], [pcsk], lambda e, q=q: e.matmul(pcs[:, 4 * q:4 * q + 4], lhsT=tri_f[:], rhs=lft[:, 4 * q:4 * q + 4], start=True, stop=True))
                        k.op('pe', ['ones_f', 'lft'], [pcsk], lambda e: e.matmul(pcs[:, 16:32], lhsT=ones_f[:], rhs=lft[:, 0:16], start=True, stop=True))
                        for q in range(4):
                            ci, co = car[(tb * 4 + q) % 2], car[(tb * 4 + q + 1) % 2]
                            cik, cok = 'car%d' % ((tb * 4 + q) % 2), 'car%d' % ((tb * 4 + q + 1) % 2)
                            k.op('dve', [pcsk, cik], ['pcK'], lambda e, q=q, ci=ci: e.tensor_tensor(out=pcK[:, tb * 4 + q, :], in0=pcs[:, 4 * q:4 * q + 4], in1=ci[:], op=ALU.add))
                            k.op('dve', ['pcK'], ['pcx'], lambda e, q=q: e.tensor_copy(out=pcx[:, q, :, 64], in_=pcK[:, tb * 4 + q, :]))
                            k.op('dve', [pcsk, cik], [cok], lambda e, q=q, ci=ci, co=co: e.tensor_tensor(out=co[:], in0=pcs[:, 16 + 4 * q:20 + 4 * q], in1=ci[:], op=ALU.add))

                    def fox_f3():
                        for h in range(4):
                            pcq, pcqk = psum()
                            for q in range(4):
                                k.op('pe', ['pcx', 'ident_f'], [pcqk], lambda e, q=q, h=h, pcq=pcq: e.matmul(pcq[0:65, q * 128:(q + 1) * 128], lhsT=pcx[:, q, h, :], rhs=ident_f[:], start=True, stop=True))
                            k.op('act', [pcqk], ['qT'], lambda e, h=h, pcq=pcq: e.activation(out=qT[64:65, h, :], in_=pcq[64:65, :], func=AF.Copy, scale=-1.0))


                    def s5_inproj(c):
                        def ev(ps, pk, c=c):
                            k.op('act', [pk], ['uf'], lambda e: e.activation(out=uf[:, c, :], in_=ps[:], func=AF.Copy))
                            k.op('pool', ['uf'], ['ub'], lambda e: e.tensor_copy(out=ub[:, c, :], in_=uf[:, c, :]))
                        inproj_fm(512 + c * 128, 128, ev)

                    def lru_cx(c):
                        inproj_fm(768 + c * 128, 128, lambda ps, pk, c=c: k.op('act', [pk], ['xbuf'], lambda e: e.activation(out=xbuf[:, c, 3:3 + TB], in_=ps[:], func=AF.Copy)))
                    fillers = [lambda: s5_inproj(0), lambda: (s5_inproj(1), fox_f3()), lambda: lru_cx(0), lambda: lru_cx(1)]

                    fox_f1()
                    for c in range(2):
                        inproj_fm(c * 128, 128, lambda ps, pk, c=c: k.op('act', [pk], ['m2a'], lambda e: e.activation(out=ug[:, c, :], in_=ps[:], func=GELU)))
                    fox_f2()
                    def sguA(q):
                            ps, pk = inproj_tm(q, 256, 256)
                            k.op('act', [pk], ['tA1'], lambda e: e.activation(out=gv[:], in_=ps[:, 0:256], func=GELU))
                            k.op('act', ['tA1'], ['tA3', 'sm'], lambda e: e.activation(out=sq0[:, 0:256], in_=gv[:], func=AF.Square, accum_out=sm[:, 0:1]))
                            k.op('dve', ['sm'], ['sm'], lambda e: e.tensor_scalar(out=sm[:, 1:2], in0=sm[:, 0:1], scalar1=1.0 / 256, scalar2=EPS, op0=ALU.mult, op1=ALU.add))
                            k.op('act', ['sm'], ['sm'], lambda e: e.activation(out=sm[:, 3:4], in_=sm[:, 1:2], func=AF.Sqrt))
                            k.op('dve', ['sm'], ['sm'], lambda e: e.reciprocal(out=sm[:, 2:3], in_=sm[:, 3:4]))
                    def sguB(q):
                            gv4 = gv[:].rearrange("p (a b d) -> p a b d", a=2, b=2)
                            gg4 = sgug[:].rearrange("p (a b d) -> p a b d", a=2, b=2)
                            for hh in range(2):
                                k.op('dve', ['tA1', 'sm', 'sgug'], ['vA'], lambda e, hh=hh: e.scalar_tensor_tensor(out=vA[:, :, hh, hh * 64:(hh + 1) * 64], in0=gv4[:, :, hh, :], scalar=sm[:, 2:3], in1=gg4[:, :, hh, :], op0=ALU.mult, op1=ALU.mult))
                    def sguC(q):
                            for pr in range(2):
                                pm, pmk = psum()
                                for hh in range(2):
                                    k.op('pe', ['vA', 'wm'], [pmk], lambda e, hh=hh, pr=pr, pm=pm: e.matmul(pm[:, 0:128], lhsT=vA[:, pr, hh, :], rhs=wm[:, 2 * pr + hh, :], start=(hh == 0), stop=(hh == 1)))
                                k.op('dve', [pmk, 'sgub'], ['tA0'], lambda e, pr=pr, pm=pm: e.tensor_tensor(out=tA[0][:, 0:128], in0=pm[:, 0:128], in1=sgub[:, pr, :], op=ALU.add))
                                k.op('pool', ['tA0', 'm2a'], ['ya%d' % pr], lambda e, pr=pr, q=q: e.tensor_tensor(out=ya[:, pr, q * 128:(q + 1) * 128], in0=tA[0][:, 0:128], in1=ug[:, pr, q * 128:(q + 1) * 128], op=ALU.mult))

                    sguA(0)
                    sguB(0)
                    for q in range(1, 4):
                        sguA(q)
                        fillers[q - 1]()
                        sguC(q - 1)
                        sguB(q)
                    fillers[3]()
                    sguC(3)

                    def s5_gen():
                        if tb > 0:
                            k.op('dve', ['wend', 's5s'], ['tA0'], lambda e: e.tensor_tensor(out=tA[0][:, 0:8], in0=wend[:, 0, :], in1=c_(C5), op=ALU.mult))
                            k.op('dve', ['wend', 's5s'], ['tA1'], lambda e: e.tensor_tensor(out=tA[1][:, 0:8], in0=wend[:, 1, :], in1=c_(S5_), op=ALU.mult))
                            k.op('dve', ['tA0', 'tA1'], ['winit'], lambda e: e.tensor_tensor(out=winit[:, 0, :], in0=tA[0][:, 0:8], in1=tA[1][:, 0:8], op=ALU.subtract))
                            k.op('dve', ['wend', 's5s'], ['tA0'], lambda e: e.tensor_tensor(out=tA[0][:, 0:8], in0=wend[:, 0, :], in1=c_(S5_), op=ALU.mult))
                            k.op('dve', ['wend', 's5s'], ['tA1'], lambda e: e.tensor_tensor(out=tA[1][:, 0:8], in0=wend[:, 1, :], in1=c_(C5), op=ALU.mult))
                            k.op('dve', ['tA0', 'tA1'], ['winit'], lambda e: e.tensor_tensor(out=winit[:, 1, :], in0=tA[0][:, 0:8], in1=tA[1][:, 0:8], op=ALU.add))
                        def s5_out(cc):
                            py, pyk = psum()
                            for jj in range(4):
                                j = cc * 4 + jj
                                k.op('pe', ['cre', 'sSr%d' % jj], [pyk], lambda e, j=j, jj=jj, py=py: e.matmul(py[:], lhsT=cre[:, j, :], rhs=sS[:, jj, 0, :], start=(jj == 0), stop=False))
                                k.op('pe', ['cimn', 'sSi%d' % jj], [pyk], lambda e, j=j, jj=jj, py=py: e.matmul(py[:], lhsT=cimn[:, j, :], rhs=sS[:, jj, 1, :], start=False, stop=(jj == 3)))
                            k.op('dve', [pyk, 'uf', 's5d'], ['tA0'], lambda e, cc=cc, py=py: e.scalar_tensor_tensor(out=tA[0][:], in0=uf[:, cc, :], scalar=s5d[:, l, cc:cc + 1], in1=py[:], op0=ALU.mult, op1=ALU.add))
                            k.op('act', ['tA0'], ['ygf'], lambda e, cc=cc: e.activation(out=ygf[:, cc, :], in_=tA[0][:], func=GELU))
                            k.op('pool', ['ygf'], ['ygb'], lambda e, cc=cc: e.tensor_copy(out=ygb[:, cc, :], in_=ygf[:, cc, :]))
                        for j in range(8):
                            cc = j // 4
                            jj = j % 4
                            pre, prk = psum()
                            pim, pik = psum()
                            k.op('pe', ['bT', 'ub'], [prk], lambda e, j=j, cc=cc, pre=pre: e.matmul(pre[:], lhsT=bT[:, 0, j, :], rhs=ub[:, cc, :], start=True, stop=True))
                            k.op('pe', ['bT', 'ub'], [pik], lambda e, j=j, cc=cc, pim=pim: e.matmul(pim[:], lhsT=bT[:, 1, j, :], rhs=ub[:, cc, :], start=True, stop=True))
                            cj, sj = cosT[:, j, :], sinT[:, j, :]
                            TT = lambda o, ok, a, ak, b, bk, op, en='dve': k.op(en, [ak, bk], [ok], lambda e: e.tensor_tensor(out=o, in0=a, in1=b, op=op))
                            TT(tA[0][:], 'tA0', pre[:], prk, cj, 'cosT', ALU.mult)
                            TT(tA[1][:], 'tA1', pim[:], pik, sj, 'sinT', ALU.mult)
                            TT(tA[0][:], 'tA0', tA[0][:], 'tA0', tA[1][:], 'tA1', ALU.add)
                            TT(tA[1][:], 'tA1', pim[:], pik, cj, 'cosT', ALU.mult)
                            TT(ug[:, 0, :], 'm2a', pre[:], prk, sj, 'sinT', ALU.mult)
                            TT(tA[1][:], 'tA1', tA[1][:], 'tA1', ug[:, 0, :], 'm2a', ALU.subtract)
                            rbc = s5s[:, MAG, j:j + 1].to_broadcast([128, TB])
                            k.op('dve', ['tA0', 's5s', 'winit'], ['tA2'], lambda e, j=j, rbc=rbc: e.tensor_tensor_scan(out=tA[2][:], data0=rbc, data1=tA[0][:], initial=winit[:, 0, j:j + 1], op0=ALU.mult, op1=ALU.add))
                            k.op('dve', ['tA1', 's5s', 'winit'], ['tA3'], lambda e, j=j, rbc=rbc: e.tensor_tensor_scan(out=tA[3][:], data0=rbc, data1=tA[1][:], initial=winit[:, 1, j:j + 1], op0=ALU.mult, op1=ALU.add))
                            k.op('pool', ['tA2'], ['wend'], lambda e, j=j: e.tensor_copy(out=wend[:, 0, j:j + 1], in_=tA[2][:, TB - 1:TB]))
                            k.op('pool', ['tA3'], ['wend'], lambda e, j=j: e.tensor_copy(out=wend[:, 1, j:j + 1], in_=tA[3][:, TB - 1:TB]))
                            TT(tA[4][:], 'tA4', tA[2][:], 'tA2', cj, 'cosT', ALU.mult, 'pool')
                            TT(tA[5][:], 'tA5', tA[3][:], 'tA3', sj, 'sinT', ALU.mult, 'pool')
                            TT(sS[:, jj, 0, :], 'sSr%d' % jj, tA[4][:], 'tA4', tA[5][:], 'tA5', ALU.subtract, 'pool')
                            TT(tA[0][:], 'tA0', tA[2][:], 'tA2', sj, 'sinT', ALU.mult)
                            TT(tA[1][:], 'tA1', tA[3][:], 'tA3', cj, 'cosT', ALU.mult)
                            TT(sS[:, jj, 1, :], 'sSi%d' % jj, tA[0][:], 'tA0', tA[1][:], 'tA1', ALU.add)
                            yield 10.0
                            if jj == 3:
                                s5_out(cc)
                        for oc in range(2):
                            pg, pgk = psum()
                            for kc in range(2):
                                k.op('pe', ['gluw', 'ygb'], [pgk], lambda e, kc=kc, oc=oc, pg=pg: e.matmul(pg[:], lhsT=gluw[:, kc, oc * 128:(oc + 1) * 128], rhs=ygb[:, kc, :], start=(kc == 0), stop=(kc == 1)))
                            k.op('act', [pgk, 'glub'], ['tA1'], lambda e, oc=oc, pg=pg: e.activation(out=tA[1][:], in_=pg[:], func=AF.Sigmoid, bias=glub[:, l, oc:oc + 1]))
                            k.op('dve', ['tA1', 'ygf'], ['ya%d' % (2 + oc)], lambda e, oc=oc: e.tensor_tensor(out=ya[:, 2 + oc, :], in0=ygf[:, oc, :], in1=tA[1][:], op=ALU.mult))
                        yield 2.0

                    for c in range(2):
                        k.op('dve', ['xbuf', 'cw', 'cb'], ['m2a'], lambda e, c=c: e.tensor_scalar(out=xc[:, c, :], in0=xbuf[:, c, 0:TB], scalar1=cw[:, l, c, 0:1], scalar2=cb[:, l, c:c + 1], op0=ALU.mult, op1=ALU.add))
                        for kk in range(1, 4):
                            k.op('dve', ['xbuf', 'cw', 'm2a'], ['m2a'], lambda e, c=c, kk=kk: e.scalar_tensor_tensor(out=xc[:, c, :], in0=xbuf[:, c, kk:kk + TB], scalar=cw[:, l, c, kk:kk + 1], in1=xc[:, c, :], op0=ALU.mult, op1=ALU.add))
                        k.op('pool', ['xbuf'], ['sm'], lambda e, c=c: e.tensor_copy(out=sm[:, 8:11], in_=xbuf[:, c, TB:TB + 3]))
                        k.op('pool', ['sm'], ['xbuf'], lambda e, c=c: e.tensor_copy(out=xbuf[:, c, 0:3], in_=sm[:, 8:11]))
                        k.op('pool', ['m2a'], ['b2a'], lambda e, c=c: e.tensor_copy(out=xcb[:, c, :], in_=xc[:, c, :]))
                        def evg(ps, pk, c=c):
                            k.op('act', [pk], ['tA5'], lambda e: e.activation(out=tA[5][:], in_=ps[:], func=GELU))
                        inproj_fm(1024 + c * 128, 128, evg)
                        if c == 0:
                            for h in range(4):
                                inproj_fm(1280 + h * 64, 64, lambda ps, pk, h=h: k.op('act', [pk], ['qT'], lambda e: e.activation(out=qT[0:64, h, :], in_=ps[0:64, :], func=AF.Copy, scale=0.125)))
                        else:
                            for h in range(4):
                                inproj_fm(1536 + h * 64, 64, lambda ps, pk, h=h: k.op('act', [pk], ['Kc'], lambda e: e.activation(out=Kc[0:64, h, t0:t0 + TB], in_=ps[0:64, :], func=AF.Copy)))
                        pr_, prk = psum()
                        pi_, pik = psum()
                        k.op('pe', ['wab', 'b2a'], [prk], lambda e, c=c, pr_=pr_: e.matmul(pr_[:], lhsT=wab[:, c, :], rhs=xcb[:, c, :], start=True, stop=True))
                        k.op('pe', ['wxb', 'b2a'], [pik], lambda e, c=c, pi_=pi_: e.matmul(pi_[:], lhsT=wxb[:, c, :], rhs=xcb[:, c, :], start=True, stop=True))
                        k.op('act', [prk, 'ba'], ['tA0'], lambda e, c=c, pr_=pr_: e.activation(out=tA[0][:], in_=pr_[:], func=AF.Sigmoid, bias=ba[:, l, c:c + 1]))
                        k.op('act', [pik, 'bx'], ['tA1'], lambda e, c=c, pi_=pi_: e.activation(out=tA[1][:], in_=pi_[:], func=AF.Sigmoid, bias=bx[:, l, c:c + 1]))
                        k.op('act', ['tA0', 'lsc'], ['tA2'], lambda e, c=c: e.activation(out=tA[2][:], in_=tA[0][:], func=AF.Exp, scale=lsc[:, 4 + c:5 + c]))
                        k.op('pool', ['tA2'], ['tA3'], lambda e: e.tensor_tensor(out=tA[3][:], in0=tA[2][:], in1=tA[2][:], op=ALU.mult))
                        k.op('pool', ['tA3'], ['tA3'], lambda e: e.tensor_scalar(out=tA[3][:], in0=tA[3][:], scalar1=-1.0, scalar2=1.0, op0=ALU.mult, op1=ALU.add))
                        k.op('act', ['tA3'], ['tA3'], lambda e: e.activation(out=tA[3][:], in_=tA[3][:], func=AF.Sqrt))
                        k.op('dve', ['tA1', 'm2a'], ['tA1'], lambda e, c=c: e.tensor_tensor(out=tA[1][:], in0=tA[1][:], in1=xc[:, c, :], op=ALU.mult))
                        k.op('dve', ['tA1', 'tA3'], ['tA1'], lambda e: e.tensor_tensor(out=tA[1][:], in0=tA[1][:], in1=tA[3][:], op=ALU.mult))
                        k.op('dve', ['tA2', 'tA1', 'hcar'], ['tA4'], lambda e, c=c: e.tensor_tensor_scan(out=tA[4][:], data0=tA[2][:], data1=tA[1][:], initial=hcar[:, c:c + 1], op0=ALU.mult, op1=ALU.add))
                        k.op('pool', ['tA4'], ['hcar'], lambda e, c=c: e.tensor_copy(out=hcar[:, c:c + 1], in_=tA[4][:, TB - 1:TB]))
                        k.op('dve', ['tA4', 'tA5'], ['ya%d' % (4 + c)], lambda e, c=c: e.tensor_tensor(out=ya[:, 4 + c, :], in0=tA[4][:], in1=tA[5][:], op=ALU.mult))

                    for q in range(4):
                        ps, pk = inproj_tm(q, 1792, 256)
                        k.op('act', [pk], ['Vc'], lambda e, q=q, ps=ps: e.activation(out=Vc[:, tb * 4 + q, :, 0:64], in_=ps[:, 0:256].rearrange("p (h d) -> p h d", h=4), func=AF.Copy))

                    def fox_gen():
                        pend = []
                        for h in range(4):
                            po, pok = psfix(6 + (h % 2))
                            nkb = 4 * tb + 4

                            def emitS(kb, h=h):
                                pss, psk = psum()
                                diag = kb >= 4 * tb
                                c0 = 128 * (kb - 4 * tb) if diag else 0
                                k.op('pe', ['Kc', 'qT'], [psk], lambda e: e.matmul(pss[:, c0:512], lhsT=Kc[0:65, h, kb * 128:(kb + 1) * 128], rhs=qT[0:65, h, c0:512], start=True, stop=(not diag)))
                                if diag:
                                    k.op('pe', ['ident_b', 'amask'], [psk], lambda e: e.matmul(pss[:, c0:c0 + 128], lhsT=ident_b[:], rhs=amask[:], start=False, stop=True, skip_group_check=True))
                                return pss, psk
                            sq_ = [emitS(0), emitS(1)]
                            for kb in range(nkb):
                                pss, psk = sq_.pop(0)
                                if kb + 2 < nkb:
                                    sq_.append(emitS(kb + 2))
                                if kb == 1 and pend:
                                    pend.pop()()
                                pt, ptk = pT[kb % 3], 'pT%d' % (kb % 3)
                                c0 = 128 * (kb - 4 * tb) if kb >= 4 * tb else 0
                                k.op('act', [psk, 'pcK'], [ptk], lambda e, kb=kb, h=h, pss=pss, pt=pt, c0=c0: e.activation(out=pt[:, c0:512], in_=pss[:, c0:512], func=AF.Exp, bias=pcK[:, kb, h:h + 1]))
                                k.op('pe', ['Vc', ptk], [pok], lambda e, kb=kb, h=h, pt=pt, po=po, nkb=nkb, c0=c0: e.matmul(po[0:65, c0:512], lhsT=Vc[:, kb, h, :], rhs=pt[:, c0:512], start=(kb == 0), stop=(kb == nkb - 1), skip_group_check=True))
                                yield 1.0
                            def finish(h=h, po=po, pok=pok):
                                pb, pbk = psum()
                                k.op('pe', ['ones_f', 'fT1'], [pbk], lambda e, pb=pb: e.matmul(pb[0:64, :], lhsT=ones_f[64:65, 0:64], rhs=fT[64:65, 1, :], start=True, stop=True))
                                k.op('act', [pbk], ['fT0'], lambda e, pb=pb: e.activation(out=fT[0:64, 0, :], in_=pb[0:64, :], func=AF.Copy))
                                k.op('dve', [pok, 'fT0'], ['yd%d' % h], lambda e, h=h, po=po: e.tensor_tensor(out=yd[:, h, :], in0=po[0:64, :], in1=fT[0:64, 0, :], op=ALU.mult))
                            k.op('act', [pok], ['fT1'], lambda e, po=po: e.activation(out=fT[64:65, 1, :], in_=po[64:65, :], func=AF.Ln))
                            k.op('act', ['fT1'], ['fT1'], lambda e: e.activation(out=fT[64:65, 1, :], in_=fT[64:65, 1, :], func=AF.Exp, scale=-1.0))
                            pend.append(finish)
                            yield 1.0
                        while pend:
                            pend.pop()()
                        yield 1.0

                    gens = [s5_gen(), fox_gen()]
                    tot = [82.0, 4.0 * (4 * tb + 5)]
                    prog = [0.0, 0.0]
                    alive = [True, True]
                    while any(alive):
                        i = min((i_ for i_ in range(2) if alive[i_]), key=lambda i_: prog[i_])
                        try:
                            prog[i] += next(gens[i]) / tot[i]
                        except StopIteration:
                            alive[i] = False

                    tsets = [((tA[3], 'tA3'), (tA[4], 'tA4'), (tA[5], 'tA5')), ((tA[0], 'tA0'), (tA[1], 'tA1'), (tA[2], 'tA2'))]
                    for g in range(3):
                        (s0_, s0k), (s1_, s1k), (rs_, rsk_) = tsets[g % 2]
                        pss, pk = psum()
                        for cc in range(2):
                            sqt, sqk = (s0_.bitcast(BF16)[:, 0:512], s0k) if cc == 0 else (s1_.bitcast(BF16)[:, 0:512], s1k)
                            k.op('act', ['ya%d' % (2 * g + cc)], [sqk], lambda e, g=g, cc=cc, sqt=sqt: e.activation(out=sqt[:], in_=ya[:, 2 * g + cc, :], func=AF.Square))
                            k.op('pe', [sqk, 'ones_b'], [pk], lambda e, cc=cc, sqt=sqt, pss=pss: e.matmul(pss[:], lhsT=ones_b[:], rhs=sqt[:], start=(cc == 0), stop=(cc == 1)))
                        k.op('act', [pk], [rsk_], lambda e, pss=pss, rs_=rs_: e.activation(out=rs_[:], in_=pss[:], func=AF.Sqrt, scale=1.0 / 256, bias=epsc[:, 0:1]))
                        k.op('dve', [rsk_], [rsk_], lambda e, rs_=rs_: e.reciprocal(out=rs_[:], in_=rs_[:]))
                        for cc in range(2):
                            ch = 2 * g + cc
                            k.op('dve', ['ya%d' % ch, rsk_, 'gma'], ['ya%d' % ch], lambda e, ch=ch, rs_=rs_: e.scalar_tensor_tensor(out=yn[:, ch, :], in0=ya[:, ch, :], scalar=gma[:, l, ch:ch + 1], in1=rs_[:], op0=ALU.mult, op1=ALU.mult))
                    (s0_, s0k), (s1_, s1k), (rs_, rsk_) = tsets[1]
                    pss, pk = psum()
                    for h in range(4):
                        sqt, sqk = (s0_.bitcast(BF16)[:, 0:512], s0k) if h % 2 == 0 else (s1_.bitcast(BF16)[:, 0:512], s1k)
                        k.op('act', ['yd%d' % h], [sqk], lambda e, h=h, sqt=sqt: e.activation(out=sqt[0:64, :], in_=yd[:, h, :], func=AF.Square))
                        k.op('pe', [sqk, 'ones_b'], [pk], lambda e, h=h, sqt=sqt, pss=pss: e.matmul(pss[:], lhsT=ones_b[0:64, :], rhs=sqt[0:64, :], start=(h == 0), stop=(h == 3)))
                    k.op('act', [pk], [rsk_], lambda e, pss=pss, rs_=rs_: e.activation(out=rs_[:], in_=pss[:], func=AF.Sqrt, scale=1.0 / 256, bias=epsc[:, 0:1]))
                    k.op('dve', [rsk_], [rsk_], lambda e, rs_=rs_: e.reciprocal(out=rs_[:], in_=rs_[:]))
                    for h in range(4):
                        k.op('dve', ['yd%d' % h, rsk_, 'gmd'], ['yd%d' % h], lambda e, h=h, rs_=rs_: e.scalar_tensor_tensor(out=ynd[:, h, :], in0=yd[:, h, :], scalar=gmd[:, l, h:h + 1], in1=rs_[0:64, :], op0=ALU.mult, op1=ALU.mult))
                    for dc in range(8):
                        wa_, wak = woa[dc % 2], 'woa%d' % (dc % 2)
                        wd_, wdk = wod[dc % 2], 'wod%d' % (dc % 2)
                        k.dma('sp', wa_[:].rearrange("p a b -> p (a b)"), wboa[l, dc], [('wboa', l)], [wak])
                        k.dma('sp', wd_[:].rearrange("p a b -> p (a b)"), wbod[l, dc], [('wbod', l)], [wdk])
                        po, pok = psum()
                        for kc in range(6):
                            k.op('pe', [wak, 'ya%d' % kc], [pok], lambda e, kc=kc, po=po, wa_=wa_: e.matmul(po[:], lhsT=wa_[:, kc, :], rhs=yn[:, kc, :], start=(kc == 0), stop=False))
                        for h in range(4):
                            k.op('pe', [wdk, 'yd%d' % h], [pok], lambda e, h=h, po=po, wd_=wd_: e.matmul(po[:], lhsT=wd_[:, h, :], rhs=ynd[:, h, :], start=False, stop=(h == 3)))
                        k.op('dve', [pok, 'xb%d' % dc], ['xb%d' % dc], lambda e, dc=dc, po=po: e.tensor_tensor(out=xb[:, dc, :], in0=xb[:, dc, :], in1=po[:], op=ALU.add))
                        k.dma('pool', xs1[dc * 128:(dc + 1) * 128, t0:t0 + TB], xb[:, dc, :], ['xb%d' % dc], [])
                k.barrier()

            with ExitStack() as eb:
                B = lambda name, shape, dt=F32: sb(eb, name, shape, dt)
                x1 = B("x1", [128, 9, TF]); h2 = B("h2", [128, 8, TF], BF16)
                hid = B("hid", [128, 32, TF], BF16)
                w1t = [B("w1t%d" % i, [128, 8, 512], BF16) for i in range(2)]
                w2t = [B("w2t%d" % i, [128, 16, 128], BF16) for i in range(3)]
                rl = [gls[:].bitcast(BF16)[:, i, :] for i in range(2)]
                sq0, sq1, rs = tS[0], tS[1], tS[2]
                for tf in range(NBF):
                    t0 = tf * TF
                    bi = lambda c_, tf=tf: (c_ - tf) % 9
                    for c8 in range(8):
                        k.dma('sp', x1[:, bi(c8), :], xs1[c8 * 128:(c8 + 1) * 128, t0:t0 + TF], [], ['x1_%d' % bi(c8)])

                    class XV:
                        def __getitem__(s_, idx):
                            return x1[idx[0], bi(idx[1]), idx[2]]
                    rmsnorm(((sq0, sq1), rs), (('tS0', 'tS1'), 'tS2'), XV(), (lambda c_: 'x1_%d' % bi(c_)), 8, TF, g2[:, l, :], h2, (lambda c_: 'h2_%d' % c_), D, eng2='dve')
                    for fg in range(8):
                        wt, wk = w1t[fg % 2], 'w1t%d' % (fg % 2)
                        k.dma('sp', wt[:].rearrange("p a b -> p (a b)"), wb1[l, fg], [('wb1', l, fg // 4)], [wk])
                        for fc in range(4):
                            for sbk in range(TF // 512):
                                ps, pk = psum()
                                for kc in range(8):
                                    k.op('pe', [wk, 'h2_%d' % kc], [pk], lambda e, kc=kc, fc=fc, sbk=sbk, wt=wt, ps=ps: e.matmul(ps[:], lhsT=wt[:, kc, fc * 128:(fc + 1) * 128], rhs=h2[:, kc, sbk * 512:(sbk + 1) * 512], start=(kc == 0), stop=(kc == 7)))
                                f = fg * 4 + fc
                                rt, rk_ = rl[(f * 2 + sbk) % 2], 'rl%d' % ((f * 2 + sbk) % 2)
                                k.op('act', [pk], [rk_], lambda e, ps=ps, rt=rt: e.activation(out=rt[:], in_=ps[:], func=AF.Relu))
                                k.op('dve', [rk_], ['hid%d' % f], lambda e, f=f, sbk=sbk, rt=rt: e.tensor_tensor(out=hid[:, f, sbk * 512:(sbk + 1) * 512], in0=rt[:], in1=rt[:], op=ALU.mult))
                    if tf == 0 and l + 1 < nl:
                        conv(l + 1)
                        setup(l + 1)
                    hk = ['hid%d' % f for f in range(32)]
                    for dc in range(8):
                        pss2 = [psum() for _ in range(TF // 512)]
                        for hf in range(2):
                            wi_ = (dc * 2 + hf) % 3
                            wt, wk = w2t[wi_], 'w2t%d' % wi_
                            k.dma('sp', wt[:].rearrange("p a b -> p (a b)"), wb2[l, dc][:, hf * 2048:(hf + 1) * 2048], [('wb2', l, dc // 4)], [wk])
                            for sbk in range(TF // 512):
                                ps, pk = pss2[sbk]
                                for f16 in range(16):
                                    f = hf * 16 + f16
                                    k.op('pe', [wk] + (hk if f == 0 else []), [pk], lambda e, f=f, f16=f16, sbk=sbk, wt=wt, ps=ps: e.matmul(ps[:], lhsT=wt[:, f16, :], rhs=hid[:, f, sbk * 512:(sbk + 1) * 512], start=(f == 0), stop=(f == 31)))
                        for sbk in range(TF // 512):
                            ps, pk = pss2[sbk]
                            k.op('dve', [pk, 'x1_%d' % bi(dc)], ['x1_%d' % bi(dc)], lambda e, dc=dc, sbk=sbk, ps=ps: e.tensor_tensor(out=x1[:, bi(dc), sbk * 512:(sbk + 1) * 512], in0=x1[:, bi(dc), sbk * 512:(sbk + 1) * 512], in1=ps[:], op=ALU.add))
                        if l != nl - 1:
                            k.dma('pool', xs[dc * 128:(dc + 1) * 128, t0:t0 + TF], x1[:, bi(dc), :], ['x1_%d' % bi(dc)], [])
                    if l == nl - 1:
                        for sbk in range(TF // 512):
                            class _V:
                                def __getitem__(s_, idx):
                                    return x1[idx[0], bi(idx[1]), sbk * 512 + idx[2].start: sbk * 512 + idx[2].stop]
                            rmsnorm(((sq0, sq1), rs), (('tS0', 'tS1'), 'tS2'), _V(), (lambda c_: 'x1_%d' % bi(c_)), 8, 512, gf, _V(), (lambda c_: 'x1_%d' % bi(c_)), D, eng2='dve')
                            for c8 in range(8):
                                k.dma('sp', outT[c8 * 128:(c8 + 1) * 128, t0 + sbk * 512:t0 + (sbk + 1) * 512], x1[:, bi(c8), sbk * 512:(sbk + 1) * 512], ['x1_%d' % bi(c8)], [])
                    else:
                        pass
                k.barrier()
        for i in range(NDS):
            if k.dcnt[i] > 0:
                k._wait('sp', i, k.dcnt[i])
    return nc


def _prep(inputs, b):
    f = lambda a: np.ascontiguousarray(np.asarray(a, dtype=np.float32))
    I = {k_: np.asarray(v, dtype=np.float32) for k_, v in inputs.items()}
    m = {}
    m["xT"] = f(I["x"][b].T)
    wi_ = I["w_in"]
    m["w_in_t"] = f(wi_[:, :, :2048].reshape(L, 8, 128, 8, 256).transpose(0, 3, 2, 1, 4).reshape(L, 8, 128, 2048))
    m["wif_t"] = f(wi_[:, :, 2048:2052].reshape(L, 8, 128, 4).transpose(0, 2, 1, 3).reshape(L, 128, 32))
    wo_ = I["w_out"]
    m["woa_t"] = f(wo_[:, 0:768, :].reshape(L, 6, 128, 8, 128).transpose(0, 3, 2, 1, 4).reshape(L, 8, 128, 768))
    m["wod_t"] = f(wo_[:, 768:1024, :].reshape(L, 4, 64, 8, 128).transpose(0, 3, 2, 1, 4).reshape(L, 8, 64, 512))
    m["w1_t"] = f(I["w_mlp_in"].reshape(L, 8, 128, 8, 512).transpose(0, 3, 2, 1, 4).reshape(L, 8, 128, 4096))
    m["w2_t"] = f(I["w_mlp_out"].reshape(L, 32, 128, 8, 128).transpose(0, 3, 2, 1, 4).reshape(L, 8, 128, 4096))
    pl = lambda a, nchunk, p=128: f(a.reshape(L, nchunk, p).transpose(2, 0, 1))
    m["g1"] = pl(I["norm1_g"], 8); m["g2"] = pl(I["norm2_g"], 8)
    m["gma"] = pl(I["mix_norm_g"][:, :768], 6); m["gmd"] = pl(I["mix_norm_g"][:, 768:], 4, 64)
    m["gf"] = f(I["final_g"].reshape(8, 128).T)
    m["sgug"] = f(np.broadcast_to(I["sgu_norm_g"][None], (128, L, 256)))
    m["sguw"] = f(I["sgu_w"].transpose(0, 3, 1, 2))
    sb_ = I["sgu_b"].reshape(L, 2, 2, 128)
    m["sgub"] = f(np.broadcast_to(sb_.transpose(2, 0, 1, 3)[:, None], (2, 64, L, 2, 128)).reshape(128, L, 2, 128))
    st = lambda a: f(a.reshape(L, 8, 2, 64).transpose(2, 3, 0, 1).reshape(128, L, 8))
    m["lre"] = st(I["s5_lambda_re"]); m["lim"] = st(I["s5_lambda_im"])
    m["ldt"] = st(np.broadcast_to(I["s5_log_dt"][:, :, None], (L, 16, 64)))

    def padb(a):
        o = np.zeros((L, 2, 64, 8, 128), np.float32)
        for j in range(8):
            for gl in range(2):
                c0 = 32 * (j % 4) + 16 * gl
                o[:, gl, :, j, c0:c0 + 16] = a[:, 2 * j + gl]
        return o.reshape(L, 128, 8, 128)
    m["bre"] = padb(I["s5_b_re"]); m["bim"] = padb(I["s5_b_im"])
    m["cre"] = padb(I["s5_c_re"].transpose(0, 1, 3, 2)); m["cim"] = padb(I["s5_c_im"].transpose(0, 1, 3, 2))
    m["s5d"] = pl(I["s5_d"], 2); m["gluw"] = f(I["s5_glu_w"]); m["glub"] = pl(I["s5_glu_b"], 2)
    m["cw"] = f(I["lru_conv_w"].reshape(L, 4, 2, 128).transpose(3, 0, 2, 1))
    m["cb"] = pl(I["lru_conv_b"], 2)

    def bd(a):
        o = np.zeros((L, 2, 64, 2, 2, 64), np.float32)
        for c in range(2):
            for hl in range(2):
                o[:, hl, :, c, hl, :] = a[:, 2 * c + hl]
        return o.reshape(L, 128, 2, 128)
    m["wa"] = bd(I["lru_wa"]); m["wx"] = bd(I["lru_wx"])
    m["ba"] = pl(I["lru_ba"].reshape(L, 256), 2); m["bx"] = pl(I["lru_bx"].reshape(L, 256), 2)
    m["llam"] = pl(I["lru_lambda"], 2)
    m["bfr"] = f(np.broadcast_to(np.tile(I["fox_fgate_b"], (1, 4))[None], (128, L, 16)))
    m["ident"] = np.eye(128, dtype=np.float32)
    m["tri"] = np.triu(np.ones((128, 128), np.float32))
    kk = np.arange(128)[:, None, None] + 128 * np.arange(4)[None, :, None]
    m["amask"] = np.where(kk <= np.arange(512)[None, None, :], 0.0, NEG).astype(np.float32)
    m["iota"] = f(np.broadcast_to(np.arange(512, dtype=np.float32)[None], (128, 512)))
    return m


def kernel(**inputs):
    nc = build(L)
    in_maps = [_prep(inputs, b) for b in range(NCORE)]
    res = run_bass_kernel_spmd(nc, in_maps, core_ids=list(range(NCORE)))
    out = np.stack([np.asarray(res.results[b]["outT"], dtype=np.float32).T for b in range(NCORE)], 0)
    return np.ascontiguousarray(out)
```

```python
import math
from contextlib import ExitStack
import numpy as np
import concourse.bass as bass
import concourse.mybir as mybir
from concourse.bass_utils import run_bass_kernel_spmd

F32 = mybir.dt.float32
BF16 = mybir.dt.bfloat16
AF = mybir.ActivationFunctionType
ALU = mybir.AluOpType

D = 1024
S = 4096
L = 4
NCORE = 4
TB = 512
NB = S // TB
TF = 1024
NBF = S // TF
DIN = 2052
EPS = 1e-6
PI = math.pi
NEG = -30000.0
NDS = 24
GELU = AF.Gelu
NOSYNC_SAME = ()


class K:
    def __init__(s, nc, es):
        s.nc = nc
        s.es = es
        s.epoch = 0
        s.engs = {'pe': nc.tensor, 'act': nc.scalar, 'dve': nc.vector, 'pool': nc.gpsimd, 'sp': nc.sync}
        s.sem = {e: es.enter_context(nc.semaphore('s_' + e)) for e in s.engs}
        s.cnt = {e: 0 for e in s.engs}
        s.dsem = [es.enter_context(nc.semaphore('d%d' % i)) for i in range(NDS)]
        s.dcnt = [0] * NDS
        s.dnext = 0
        s.seen = {e: {} for e in s.engs}
        s.lastw = {}
        s.readers = {}
        s.psn = 0
        s.xsem = {}
        s.xcnt = {}
        s.xn = 0

    def _wait(s, e, sk, val):
        if e == sk and (e == 'pe' or e in NOSYNC_SAME):
            return
        if s.seen[e].get(sk, 0) >= val:
            return
        if isinstance(sk, str):
            sem = s.sem[sk]
        elif isinstance(sk, int):
            sem = s.dsem[sk]
        else:
            sem = s.xsem[sk]
        s.engs[e].wait_ge(sem, val)
        s.seen[e][sk] = val

    def deps(s, e, reads, writes):
        for k in reads:
            if k in s.lastw:
                s._wait(e, *s.lastw[k])
        for k in writes:
            if k in s.lastw:
                s._wait(e, *s.lastw[k])
            for sk, v in s.readers.get(k, {}).items():
                s._wait(e, sk, v)

    def _record(s, tok, reads, writes):
        for k in reads:
            d = s.readers.setdefault(k, {})
            d[tok[0]] = max(d.get(tok[0], 0), tok[1])
        for k in writes:
            s.lastw[k] = tok
            s.readers[k] = {}

    def op(s, e, reads, writes, fn):
        s.deps(e, reads, writes)
        ins = fn(s.engs[e])
        s.cnt[e] += 1
        ins.then_inc(s.sem[e], 1)
        s._record((e, s.cnt[e]), reads, writes)

    def dma(s, q, out, in_, reads, writes):
        i = s.dnext
        s.dnext = (i + 1) % NDS
        if s.dcnt[i] > 0:
            s._wait(q, i, s.dcnt[i])
        s.deps(q, reads, writes)
        s.dcnt[i] += 16
        s.engs[q].dma_start(out=out, in_=in_).then_inc(s.dsem[i], 16)
        s._record((i, s.dcnt[i]), reads, writes)

    def dma_sw(s, out, in_, reads, writes, **kw):
        key = ('x', s.xn % 4)
        s.xn += 1
        if key not in s.xsem:
            s.xsem[key] = s.es.enter_context(s.nc.semaphore('x%d' % key[1]))
            s.xcnt[key] = 0
        s.deps('pool', reads, writes)
        s.xcnt[key] += 16
        s.engs['pool'].dma_start(out=out, in_=in_, **kw).then_inc(s.xsem[key], 16)
        s._record((key, s.xcnt[key]), reads, writes)
        return (key, s.xcnt[key])

    def barrier(s):
        for e in s.engs:
            for f in s.engs:
                if f != e and s.cnt[f] > 0:
                    s._wait(e, f, s.cnt[f])
            for i in range(NDS):
                if s.dcnt[i] > 0:
                    s._wait(e, i, s.dcnt[i])
        s.lastw = {k_: v_ for k_, v_ in s.lastw.items() if isinstance(v_[0], tuple)}
        s.readers = {}
        s.epoch += 1
        for e in s.engs:
            s.sem[e] = s.es.enter_context(s.nc.semaphore('s_%s_%d' % (e, s.epoch)))
            s.cnt[e] = 0
        for e in s.engs:
            for f in s.engs:
                s.seen[e].pop(f, None)


def build(nl=L):
    nc = bass.Bass("TRN2", target_bir_lowering=False)

    def din(name, shape):
        return nc.dram_tensor(name, list(shape), F32, kind="ExternalInput").ap()

    xT_in = din("xT", [D, S])
    w_in_t = din("w_in_t", [L, 8, 128, 2048])
    wif_t = din("wif_t", [L, 128, 32])
    woa_t = din("woa_t", [L, 8, 128, 768])
    wod_t = din("wod_t", [L, 8, 64, 512])
    w1_t = din("w1_t", [L, 8, 128, 4096])
    w2_t = din("w2_t", [L, 8, 128, 4096])
    wbin = nc.dram_tensor("wbin", [L, 8, 128, 2048], BF16).ap()
    wbif = nc.dram_tensor("wbif", [L, 128, 32], BF16).ap()
    wboa = nc.dram_tensor("wboa", [L, 8, 128, 768], BF16).ap()
    wbod = nc.dram_tensor("wbod", [L, 8, 64, 512], BF16).ap()
    wb1 = nc.dram_tensor("wb1", [L, 8, 128, 4096], BF16).ap()
    wb2 = nc.dram_tensor("wb2", [L, 8, 128, 4096], BF16).ap()
    g1_d = din("g1", [128, L, 8])
    g2_d = din("g2", [128, L, 8])
    gma_d = din("gma", [128, L, 6])
    gmd_d = din("gmd", [64, L, 4])
    gf_d = din("gf", [128, 8])
    sgug_d = din("sgug", [128, L, 256])
    sguw_d = din("sguw", [L, 128, 4, 128])
    sgub_d = din("sgub", [128, L, 2, 128])
    lre_d = din("lre", [128, L, 8])
    lim_d = din("lim", [128, L, 8])
    ldt_d = din("ldt", [128, L, 8])
    bre_d = din("bre", [L, 128, 8, 128])
    bim_d = din("bim", [L, 128, 8, 128])
    cre_d = din("cre", [L, 128, 8, 128])
    cim_d = din("cim", [L, 128, 8, 128])
    s5d_d = din("s5d", [128, L, 2])
    gluw_d = din("gluw", [L, 256, 256])
    glub_d = din("glub", [128, L, 2])
    cw_d = din("cw", [128, L, 2, 4])
    cb_d = din("cb", [128, L, 2])
    wa_d = din("wa", [L, 128, 2, 128])
    wx_d = din("wx", [L, 128, 2, 128])
    ba_d = din("ba", [128, L, 2])
    bx_d = din("bx", [128, L, 2])
    llam_d = din("llam", [128, L, 2])
    bf_d = din("bfr", [128, L, 16])
    ident_d = din("ident", [128, 128])
    tri_d = din("tri", [128, 128])
    amask_d = din("amask", [128, 4, 512])
    iota_d = din("iota", [128, 512])
    outT = nc.dram_tensor("outT", [D, S], F32, kind="ExternalOutput").ap()
    xs = nc.dram_tensor("xs", [D, S], F32).ap()
    xs1 = nc.dram_tensor("xs1", [D, S], F32).ap()

    def fm(ap2d, t0, n):
        return ap2d[:, t0:t0 + n].rearrange("(c p) t -> p c t", p=128)

    with ExitStack() as es:
        E = es.enter_context
        k = K(nc, es)
        PS = [E(nc.psum_tensor("ps%d" % i, [128, 512], F32)) for i in range(8)]

        def psum():
            i = k.psn
            k.psn = (i + 1) % 6
            return PS[i], 'ps%d' % i

        def psfix(i):
            return PS[i], 'ps%d' % i

        uid = [0]

        def sb(stack, name, shape, dt=F32):
            uid[0] += 1
            return stack.enter_context(nc.sbuf_tensor("sb_%s_%d" % (name, uid[0]), list(shape), dt))

        g1 = sb(es, "g1", [128, L, 8]); g2 = sb(es, "g2", [128, L, 8])
        gma = sb(es, "gma", [128, L, 6]); gmd = sb(es, "gmd", [64, L, 4]); gf = sb(es, "gf", [128, 8])
        lre = sb(es, "lre", [128, L, 8]); lim = sb(es, "lim", [128, L, 8]); ldt = sb(es, "ldt", [128, L, 8])
        s5d = sb(es, "s5d", [128, L, 2]); glub = sb(es, "glub", [128, L, 2])
        cw = sb(es, "cw", [128, L, 2, 4]); cb = sb(es, "cb", [128, L, 2])
        ba = sb(es, "ba", [128, L, 2]); bx = sb(es, "bx", [128, L, 2]); llam = sb(es, "llam", [128, L, 2])
        bfr = sb(es, "bfr", [128, L, 16])
        ident_f = sb(es, "ident_f", [128, 128]); ident_b = sb(es, "ident_b", [128, 128], BF16)
        tri_f = sb(es, "tri_f", [128, 128])
        ones_f = sb(es, "ones_f", [128, 128]); ones_b = sb(es, "ones_b", [128, 128], BF16)
        for t, d_, nm in [(g1, g1_d, 'g1'), (g2, g2_d, 'g2'), (gma, gma_d, 'gma'), (gmd, gmd_d, 'gmd'),
                          (gf, gf_d, 'gf'), (lre, lre_d, 'lre'), (lim, lim_d, 'lim'), (ldt, ldt_d, 'ldt'),
                          (s5d, s5d_d, 's5d'), (glub, glub_d, 'glub'), (cw, cw_d, 'cw'), (cb, cb_d, 'cb'),
                          (ba, ba_d, 'ba'), (bx, bx_d, 'bx'), (llam, llam_d, 'llam'), (bfr, bf_d, 'bfr'),
                          (ident_f, ident_d, 'ident_f'), (tri_f, tri_d, 'tri_f')]:
            k.dma('sp', t[:], d_, [], [nm])
        conv_hist = []
        conv_hist.append(k.dma_sw(ident_b[:], ident_d, [], ['ident_b']))

        def f2k(ap):
            names = " ".join("d%d" % i for i in range(len(ap.shape)))
            return ap.rearrange("%s -> (%s)" % (names, names)).rearrange("(r c) -> r c", c=2048)
        def conv1(out, in_, key):
            if len(conv_hist) >= 2:
                k._wait('pool', *conv_hist[-2])
            conv_hist.append(k.dma_sw(out, in_, [], [key]))

        def conv(l_):
            conv1(f2k(wbin[l_]), f2k(w_in_t[l_]), ('wbin', l_))
            conv1(wbif[l_], wif_t[l_], ('wbif', l_))
            conv1(f2k(wboa[l_]), f2k(woa_t[l_]), ('wboa', l_))
            conv1(f2k(wbod[l_]), f2k(wod_t[l_]), ('wbod', l_))
            for hh_ in range(2):
                conv1(f2k(wb1[l_, 4 * hh_:4 * hh_ + 4]), f2k(w1_t[l_, 4 * hh_:4 * hh_ + 4]), ('wb1', l_, hh_))
            for hh_ in range(2):
                conv1(f2k(wb2[l_, 4 * hh_:4 * hh_ + 4]), f2k(w2_t[l_, 4 * hh_:4 * hh_ + 4]), ('wb2', l_, hh_))
        conv(0)
        epsc = sb(es, "epsc", [128, 1])
        k.op('dve', [], ['epsc'], lambda e: e.memset(epsc[:], EPS))
        k.op('dve', [], ['ones_f'], lambda e: e.memset(ones_f[:], 1.0))
        k.op('dve', [], ['ones_b'], lambda e: e.memset(ones_b[:], 1.0))

        def rmsnorm(stack_tmp, keys, src, srck, nch, ntok, gain, dst, dstk, dim, eng2='dve'):
            sq, rs = stack_tmp
            sqks, rsk = keys
            skf = srck if callable(srck) else (lambda c_: srck)
            dkf = dstk if callable(dstk) else (lambda c_: dstk)
            for s0 in range(0, ntok, 512):
                pss, pk = psum()
                for c in range(nch):
                    sqt, sqk = sq[c % 2].bitcast(BF16)[:, 0:512], sqks[c % 2]
                    if c % 2 == 0:
                        k.op('act', [skf(c)], [sqk], lambda e, c=c, sqt=sqt: e.activation(out=sqt[:], in_=src[:, c, s0:s0 + 512], func=AF.Square))
                    else:
                        k.op('dve', [skf(c)], [sqk], lambda e, c=c, sqt=sqt: e.tensor_tensor(out=sqt[:], in0=src[:, c, s0:s0 + 512], in1=src[:, c, s0:s0 + 512], op=ALU.mult))
                    k.op('pe', [sqk, 'ones_b'], [pk], lambda e, c=c, sqt=sqt: e.matmul(pss[:], lhsT=ones_b[:], rhs=sqt[:], start=(c == 0), stop=(c == nch - 1)))
                k.op('act', [pk], [rsk], lambda e: e.activation(out=rs[:], in_=pss[:], func=AF.Sqrt, scale=1.0 / dim, bias=epsc[:, 0:1]))
                k.op('dve', [rsk], [rsk], lambda e: e.reciprocal(out=rs[:], in_=rs[:]))
                for c in range(nch):
                    en = eng2 if c % 2 else 'dve'
                    k.op(en, [skf(c), rsk], [dkf(c)], lambda e, c=c: e.scalar_tensor_tensor(out=dst[:, c, s0:s0 + 512], in0=src[:, c, s0:s0 + 512], scalar=gain[:, c:c + 1], in1=rs[:], op0=ALU.mult, op1=ALU.mult))

        cosT = sb(es, "cosT", [128, 8, 512]); sinT = sb(es, "sinT", [128, 8, 512])
        bT = sb(es, "bT", [128, 2, 8, 128], BF16)
        cre = sb(es, "cre", [128, 8, 128], BF16); cimn = sb(es, "cimn", [128, 8, 128], BF16)
        wm = sb(es, "wm", [128, 4, 128], BF16); sgug = sb(es, "sgug", [128, 256]); sgub = sb(es, "sgub", [128, 2, 128])
        gluw = sb(es, "gluw", [128, 2, 256], BF16); gls = sb(es, "gls", [128, 2, 256])
        wab = sb(es, "wab", [128, 2, 128], BF16); wxb = sb(es, "wxb", [128, 2, 128], BF16)
        s5s = sb(es, "s5s", [128, 16, 8]); s5i = sb(es, "s5i", [128, 1, 8], mybir.dt.int32)
        tI = sb(es, "tI", [128, 512], mybir.dt.int32); lsc = sb(es, "lsc", [128, 8])
        tSall = sb(es, "tSall", [128, 4, 512])
        tS = [tSall[:, i, :] for i in range(4)]
        iota = tS[3]
        stg = tSall[:, 0:2, :].rearrange("p a (b c) -> p (a b) c", b=4)
        c_ = lambda i: s5s[:, i, :]
        DT, MAG, TH, SN, CS, ARE, AIM, DEN, FRE, FIM, T1, T2, C5, S5_, RR = range(15)

        def setup(l):
            k.dma('sp', gls[:], gluw_d[l].rearrange("(c p) n -> p c n", p=128), [], ['rl0', 'rl1'])
            k.op('dve', ['rl0', 'rl1'], ['gluw'], lambda e: e.tensor_copy(out=gluw[:], in_=gls[:]))
            k.dma('sp', stg[:, 0:2, :], wa_d[l], [], ['tS0', 'tS1'])
            k.op('dve', ['tS0', 'tS1'], ['wab'], lambda e: e.tensor_copy(out=wab[:], in_=stg[:, 0:2, :]))
            k.dma('sp', stg[:, 2:4, :], wx_d[l], [], ['tS0', 'tS1'])
            k.op('dve', ['tS0', 'tS1'], ['wxb'], lambda e: e.tensor_copy(out=wxb[:], in_=stg[:, 2:4, :]))
            k.dma('sp', stg[:], cre_d[l], ['tS0', 'tS1'], ['tS0', 'tS1'])
            k.op('dve', ['tS0', 'tS1'], ['cre'], lambda e: e.tensor_copy(out=cre[:], in_=stg[:]))
            k.dma('sp', iota[:], iota_d, [], ['tS3'])
            k.dma('sp', sgug[:], sgug_d[:, l, :], [], ['sgug'])
            k.dma('sp', sgub[:], sgub_d[:, l, :, :], [], ['sgub'])
            k.dma('sp', stg[:], cim_d[l], [], ['tS0', 'tS1'])
            k.op('act', ['tS0', 'tS1'], ['cimn'], lambda e: e.activation(out=cimn[:], in_=stg[:], func=AF.Copy, scale=-1.0))
            k.dma('sp', stg[:, 0:4, :], sguw_d[l], ['tS0', 'tS1'], ['tS0', 'tS1'])
            for h in range(4):
                k.op('dve', ['tS0', 'tS1', 'tri_f'], ['wm'], lambda e, h=h: e.tensor_tensor(out=wm[:, h, :], in0=stg[:, h, :], in1=tri_f[:], op=ALU.mult))
            c_ = lambda i: s5s[:, i, :]
            lr, li, ld = lre[:, l, :], lim[:, l, :], ldt[:, l, :]
            DT, MAG, TH, SN, CS, ARE, AIM, DEN, FRE, FIM, T1, T2, C5, S5_, RR = range(15)
            v = lambda fn: k.op('dve', ['s5s', 'lre', 'lim', 'ldt'], ['s5s'], fn)
            a_ = lambda fn: k.op('act', ['s5s', 'lre', 'lim', 'ldt'], ['s5s'], fn)
            a_(lambda e: e.activation(out=c_(DT), in_=ld, func=AF.Exp))
            v(lambda e: e.tensor_tensor(out=c_(T1), in0=lr, in1=c_(DT), op=ALU.mult))
            a_(lambda e: e.activation(out=c_(MAG), in_=c_(T1), func=AF.Exp))
            v(lambda e: e.tensor_tensor(out=c_(TH), in0=li, in1=c_(DT), op=ALU.mult))

            C1 = 6.28125
            C2 = 2 * PI - C1

            def red_sin(dst, X, KI, KF, M, rk, wk):
                o = lambda en, fn: k.op(en, rk + wk, wk, fn)
                o('dve', lambda e: e.tensor_scalar(out=KF, in0=X, scalar1=1.0 / (2 * PI), scalar2=None, op0=ALU.mult))
                o('dve', lambda e: e.tensor_copy(out=KI, in_=KF))
                o('dve', lambda e: e.tensor_copy(out=KF, in_=KI))
                o('dve', lambda e: e.scalar_tensor_tensor(out=X, in0=KF, scalar=-C1, in1=X, op0=ALU.mult, op1=ALU.add))
                o('dve', lambda e: e.scalar_tensor_tensor(out=X, in0=KF, scalar=-C2, in1=X, op0=ALU.mult, op1=ALU.add))
                o('dve', lambda e: e.tensor_scalar(out=M, in0=X, scalar1=PI, scalar2=2 * PI, op0=ALU.is_gt, op1=ALU.mult))
                o('dve', lambda e: e.tensor_tensor(out=X, in0=X, in1=M, op=ALU.subtract))
                o('dve', lambda e: e.tensor_scalar(out=M, in0=X, scalar1=-PI, scalar2=2 * PI, op0=ALU.is_lt, op1=ALU.mult))
                o('dve', lambda e: e.tensor_tensor(out=X, in0=X, in1=M, op=ALU.add))
                o('act', lambda e: e.activation(out=dst, in_=X, func=AF.Sin))

            def sincos(dst_s, dst_c, src, mul):
                kk_ = ['s5s', 'lre', 'lim', 'ldt']
                v(lambda e: e.tensor_scalar(out=c_(T1), in0=src, scalar1=float(mul), scalar2=16 * PI, op0=ALU.mult, op1=ALU.add))
                red_sin(dst_s, c_(T1), s5i[:, 0, :], c_(T2), c_(RR), kk_, ['s5s', 's5i'])
                v(lambda e: e.tensor_scalar(out=c_(T1), in0=src, scalar1=float(mul), scalar2=16.5 * PI, op0=ALU.mult, op1=ALU.add))
                red_sin(dst_c, c_(T1), s5i[:, 0, :], c_(T2), c_(RR), kk_, ['s5s', 's5i'])
            sincos(c_(SN), c_(CS), c_(TH), 1.0)
            sincos(c_(S5_), c_(C5), c_(TH), float(TB))
            v(lambda e: e.tensor_tensor(out=c_(ARE), in0=c_(MAG), in1=c_(CS), op=ALU.mult))
            v(lambda e: e.tensor_tensor(out=c_(AIM), in0=c_(MAG), in1=c_(SN), op=ALU.mult))
            v(lambda e: e.tensor_tensor(out=c_(T1), in0=lr, in1=lr, op=ALU.mult))
            v(lambda e: e.tensor_tensor(out=c_(T2), in0=li, in1=li, op=ALU.mult))
            v(lambda e: e.tensor_tensor(out=c_(DEN), in0=c_(T1), in1=c_(T2), op=ALU.add))
            v(lambda e: e.reciprocal(out=c_(DEN), in_=c_(DEN)))
            v(lambda e: e.tensor_scalar(out=c_(RR), in0=c_(ARE), scalar1=-1.0, scalar2=None, op0=ALU.add))
            v(lambda e: e.tensor_tensor(out=c_(T1), in0=c_(RR), in1=lr, op=ALU.mult))
            v(lambda e: e.tensor_tensor(out=c_(T2), in0=c_(AIM), in1=li, op=ALU.mult))
            v(lambda e: e.tensor_tensor(out=c_(T1), in0=c_(T1), in1=c_(T2), op=ALU.add))
            v(lambda e: e.tensor_tensor(out=c_(FRE), in0=c_(T1), in1=c_(DEN), op=ALU.mult))
            v(lambda e: e.tensor_tensor(out=c_(T1), in0=c_(AIM), in1=lr, op=ALU.mult))
            v(lambda e: e.tensor_tensor(out=c_(T2), in0=c_(RR), in1=li, op=ALU.mult))
            v(lambda e: e.tensor_tensor(out=c_(T1), in0=c_(T1), in1=c_(T2), op=ALU.subtract))
            v(lambda e: e.tensor_tensor(out=c_(FIM), in0=c_(T1), in1=c_(DEN), op=ALU.mult))
            Bre = cosT
            Bim = sinT
            k.dma('sp', Bre[:, :, 0:128], bre_d[l], [], ['cosT'])
            k.dma('sp', Bim[:, :, 0:128], bim_d[l], [], ['sinT'])
            for j in range(8):
                fr, fi = s5s[:, FRE, j:j + 1], s5s[:, FIM, j:j + 1]
                k.op('dve', ['sinT', 's5s'], ['tS0'], lambda e, j=j, fi=fi: e.tensor_scalar(out=tS[0][:, 0:128], in0=Bim[:, j, 0:128], scalar1=fi, scalar2=None, op0=ALU.mult))
                k.op('dve', ['cosT', 'tS0', 's5s'], ['tS1'], lambda e, j=j, fr=fr: e.scalar_tensor_tensor(out=tS[1][:, 0:128], in0=Bre[:, j, 0:128], scalar=fr, in1=tS[0][:, 0:128], op0=ALU.mult, op1=ALU.subtract))
                k.op('dve', ['cosT', 's5s'], ['tS0'], lambda e, j=j, fi=fi: e.tensor_scalar(out=tS[0][:, 0:128], in0=Bre[:, j, 0:128], scalar1=fi, scalar2=None, op0=ALU.mult))
                k.op('dve', ['sinT', 'tS0', 's5s'], ['tS2'], lambda e, j=j, fr=fr: e.scalar_tensor_tensor(out=tS[2][:, 0:128], in0=Bim[:, j, 0:128], scalar=fr, in1=tS[0][:, 0:128], op0=ALU.mult, op1=ALU.add))
                for ri, tt, tk in ((0, tS[1], 'tS1'), (1, tS[2], 'tS2')):
                    pst, pk = psum()
                    k.op('pe', [tk, 'ident_f'], [pk], lambda e, tt=tt, pst=pst: e.transpose(out=pst[:, 0:128], in_=tt[:, 0:128], identity=ident_f[:]))
                    k.op('act', [pk], ['bT'], lambda e, ri=ri, j=j, pst=pst: e.activation(out=bT[:, ri, j, :], in_=pst[:, 0:128], func=AF.Copy))
            for j in range(8):
                th = s5s[:, TH, j:j + 1]
                for tab, tkk, off in ((sinT, 'sinT', 16 * PI), (cosT, 'cosT', 16.5 * PI)):
                    k.op('dve', ['tS3', 's5s'], ['tS0'], lambda e, th=th, off=off: e.tensor_scalar(out=tS[0][:], in0=iota[:], scalar1=th, scalar2=off, op0=ALU.mult, op1=ALU.add))
                    red_sin(tab[:, j, :], tS[0][:], tI[:], tS[1][:], tS[2][:], [], ['tS0', 'tS1', 'tS2', 'tI', tkk])
            k.op('act', ['llam'], ['lsc'], lambda e: e.activation(out=lsc[:, 0:2], in_=llam[:, l, :], func=AF.Exp, scale=-1.0))
            k.op('act', ['lsc'], ['lsc'], lambda e: e.activation(out=lsc[:, 2:4], in_=lsc[:, 0:2], func=AF.Ln, bias=1.0))
            k.op('dve', ['lsc'], ['lsc'], lambda e: e.tensor_scalar(out=lsc[:, 4:6], in0=lsc[:, 2:4], scalar1=-8.0, scalar2=None, op0=ALU.mult))


        setup(0)
        k.barrier()
        for l in range(nl):
            x_src = xT_in if l == 0 else xs
            with ExitStack() as ea:
                A = lambda name, shape, dt=F32: sb(ea, name, shape, dt)
                wr = [A("wr%d" % i, [128, 8, 256], BF16) for i in range(3)]
                wif = A("wif", [128, 8, 4], BF16)
                woa = [A("woa%d" % i, [128, 6, 128], BF16) for i in range(2)]
                wod = [A("wod%d" % i, [64, 4, 128], BF16) for i in range(2)]
                Kc = A("Kc", [65, 4, S], BF16); Vc = A("Vc", [128, S // 128, 4, 65], BF16)
                pcK = A("pcK", [128, S // 128, 4])
                amask = A("amask", [128, 128], BF16)
                vA = A("vA", [128, 2, 2, 128], BF16)
                xb = A("xb", [128, 8, TB]); hT = A("hT", [128, 8, TB], BF16)
                ya = A("ya", [128, 6, TB], BF16); yd = A("yd", [64, 4, TB], BF16)
                yn = ya; ynd = yd
                ug = A("m2a", [128, 2, TB]); uf = A("uf", [128, 2, TB]); xc = ug; ub = A("ub", [128, 2, TB], BF16); xcb = A("b2a", [128, 2, TB], BF16); qT = A("qT", [65, 4, TB], BF16); pcx = A("pcx", [128, 4, 4, 65])
                fT = tSall[:, 2:4, :]
                tAall = A("tAall", [128, 6, 512])
                tA = [tAall[:, i, :] for i in range(6)]
                sq0, sq1, rs = tA[3], tA[4], tA[5]
                gv = tA[1][:, 0:256]; rd = tA[1]
                sS = A("sS", [128, 4, 2, TB], BF16)
                pT = [tI[:].bitcast(BF16)[:, 0:512], tI[:].bitcast(BF16)[:, 512:1024], A("pT2", [128, 512], BF16)]
                sm = A("sm", [128, 12])
                wend = A("wend", [128, 2, 8]); winit = A("winit", [128, 2, 8])
                hcar = A("hcar", [128, 2])
                xbuf = A("xbuf", [128, 2, TB + 3])
                lft = A("lft", [128, 16]); car = [A("car%d" % i, [128, 4]) for i in range(2)]
                ygf = tSall[:, 0:2, :]; ygb = gls[:].bitcast(BF16)

                k.dma('sp', wif[:].rearrange("p a b -> p (a b)"), wbif[l], [('wbif', l)], ['wif'])
                k.dma('sp', tA[0][:, 0:128], amask_d[:, 0, 0:128], [], ['tA0'])
                k.op('pool', ['tA0'], ['amask'], lambda e: e.tensor_copy(out=amask[:], in_=tA[0][:, 0:128]))
                wslots = [None, None, None]
                wuse = [0, 0, 0]
                wtick = [0]

                def wget(grp):
                    wtick[0] += 1
                    for i in (0, 1, 2):
                        if wslots[i] == grp:
                            wuse[i] = wtick[0]
                            return wr[i], 'wr%d' % i
                    i = min((0, 1, 2), key=lambda i_: wuse[i_])
                    wslots[i] = grp
                    wuse[i] = wtick[0]
                    k.dma('sp', wr[i][:].rearrange("p a b -> p (a b)"), wbin[l, grp], [('wbin', l)], ['wr%d' % i])
                    return wr[i], 'wr%d' % i
                k.op('dve', [], ['vA'], lambda e: e.memset(vA[:], 0.0))
                k.op('pool', [], ['Vc'], lambda e: e.memset(Vc[:], 1.0))
                k.op('dve', [], ['xbuf'], lambda e: e.memset(xbuf[:], 0.0))
                k.op('dve', [], ['pcx'], lambda e: e.memset(pcx[:], 0.0))
                k.op('pool', [], ['Kc'], lambda e: e.memset(Kc[64:65, :, :], 1.0))
                k.op('dve', [], ['hcar'], lambda e: e.memset(hcar[:], 0.0))
                k.op('dve', [], ['winit'], lambda e: e.memset(winit[:], 0.0))
                k.op('dve', [], ['car0'], lambda e: e.memset(car[0][:], 0.0))

                wcur = {}

                def inproj_fm(col0, ncol, evac):
                    base = (col0 // 256) * 256
                    wt, wk = wget(base // 256)
                    o = col0 - base
                    ps, pk = psum()
                    for kc in range(8):
                        k.op('pe', [wk, 'hT%d' % kc], [pk], lambda e, kc=kc: e.matmul(ps[0:ncol, :], lhsT=wt[:, kc, o:o + ncol], rhs=hT[:, kc, :], start=(kc == 0), stop=(kc == 7)))
                    evac(ps, pk)

                def inproj_tm(q, col0, ncol):
                    wt, wk = wget(col0 // 256)
                    ps, pk = psum()
                    for kc in range(8):
                        k.op('pe', [wk, 'hT%d' % kc], [pk], lambda e, kc=kc: e.matmul(ps[:, 0:ncol], lhsT=hT[:, kc, q * 128:(q + 1) * 128], rhs=wt[:, kc, 0:ncol], start=(kc == 0), stop=(kc == 7)))
                    return ps, pk

                for tb in range(NB):
                    t0 = tb * TB
                    for c8 in range(8):
                        k.dma('sp', xb[:, c8, :], x_src[c8 * 128:(c8 + 1) * 128, t0:t0 + TB], [], ['xb%d' % c8])
                    rmsnorm(((sq0, sq1), rs), (('tA3', 'tA4'), 'tA5'), xb, (lambda c_: 'xb%d' % c_), 8, TB, g1[:, l, :], hT, (lambda c_: 'hT%d' % c_), D, eng2='dve')

                    fox_st = {}
                    def fox_f1():
                        pf, pfk = psfix(7)
                        for q in range(4):
                            for kc in range(8):
                                k.op('pe', ['wif', 'hT%d' % kc], [pfk], lambda e, kc=kc, q=q: e.matmul(pf[:, 4 * q:4 * q + 4], lhsT=hT[:, kc, q * 128:(q + 1) * 128], rhs=wif[:, kc, 0:4], start=(kc == 0), stop=(kc == 7)))
                        k.op('dve', [pfk, 'bfr'], ['lft'], lambda e: e.tensor_tensor(out=lft[:], in0=pf[:, 0:16], in1=bfr[:, l, :], op=ALU.add))
                        k.op('act', ['lft'], ['lft'], lambda e: e.activation(out=lft[:], in_=lft[:], func=AF.Exp, scale=-1.0))
                        k.op('act', ['lft'], ['lft'], lambda e: e.activation(out=lft[:], in_=lft[:], func=AF.Ln, bias=1.0))

                    def fox_f2():
                        pcs, pcsk = psum()
                        fox_st['pcs'] = (pcs, pcsk)
                        for q in range(4):
                            k.op('pe', ['tri_f', 'lft'], [pcsk], lambda e, q=q: e.matmul(pcs[:, 4 * q:4 * q + 4], lhsT=tri_f[:], rhs=lft[:, 4 * q:4 * q + 4], start=True, stop=True))
                        k.op('pe', ['ones_f', 'lft'], [pcsk], lambda e: e.matmul(pcs[:, 16:32], lhsT=ones_f[:], rhs=lft[:, 0:16], start=True, stop=True))
                        for q in range(4):
                            ci, co = car[(tb * 4 + q) % 2], car[(tb * 4 + q + 1) % 2]
                            cik, cok = 'car%d' % ((tb * 4 + q) % 2), 'car%d' % ((tb * 4 + q + 1) % 2)
                            k.op('dve', [pcsk, cik], ['pcK'], lambda e, q=q, ci=ci: e.tensor_tensor(out=pcK[:, tb * 4 + q, :], in0=pcs[:, 4 * q:4 * q + 4], in1=ci[:], op=ALU.add))
                            k.op('dve', ['pcK'], ['pcx'], lambda e, q=q: e.tensor_copy(out=pcx[:, q, :, 64], in_=pcK[:, tb * 4 + q, :]))
                            k.op('dve', [pcsk, cik], [cok], lambda e, q=q, ci=ci, co=co: e.tensor_tensor(out=co[:], in0=pcs[:, 16 + 4 * q:20 + 4 * q], in1=ci[:], op=ALU.add))

                    def fox_f3():
                        for h in range(4):
                            pcq, pcqk = psum()
                            for q in range(4):
                                k.op('pe', ['pcx', 'ident_f'], [pcqk], lambda e, q=q, h=h, pcq=pcq: e.matmul(pcq[0:65, q * 128:(q + 1) * 128], lhsT=pcx[:, q, h, :], rhs=ident_f[:], start=True, stop=True))
                            k.op('act', [pcqk], ['qT'], lambda e, h=h, pcq=pcq: e.activation(out=qT[64:65, h, :], in_=pcq[64:65, :], func=AF.Copy, scale=-1.0))


                    def s5_inproj(c):
                        def ev(ps, pk, c=c):
                            k.op('act', [pk], ['uf'], lambda e: e.activation(out=uf[:, c, :], in_=ps[:], func=AF.Copy))
                            k.op('pool', ['uf'], ['ub'], lambda e: e.tensor_copy(out=ub[:, c, :], in_=uf[:, c, :]))
                        inproj_fm(512 + c * 128, 128, ev)

                    def lru_cx(c):
                        inproj_fm(768 + c * 128, 128, lambda ps, pk, c=c: k.op('act', [pk], ['xbuf'], lambda e: e.activation(out=xbuf[:, c, 3:3 + TB], in_=ps[:], func=AF.Copy)))
                    fillers = [lambda: s5_inproj(0), lambda: (s5_inproj(1), fox_f3()), lambda: lru_cx(0), lambda: lru_cx(1)]

                    fox_f1()
                    for c in range(2):
                        inproj_fm(c * 128, 128, lambda ps, pk, c=c: k.op('act', [pk], ['m2a'], lambda e: e.activation(out=ug[:, c, :], in_=ps[:], func=GELU)))
                    fox_f2()
                    def sguA(q):
                            ps, pk = inproj_tm(q, 256, 256)
                            k.op('act', [pk], ['tA1'], lambda e: e.activation(out=gv[:], in_=ps[:, 0:256], func=GELU))
                            k.op('act', ['tA1'], ['tA3', 'sm'], lambda e: e.activation(out=sq0[:, 0:256], in_=gv[:], func=AF.Square, accum_out=sm[:, 0:1]))
                            k.op('dve', ['sm'], ['sm'], lambda e: e.tensor_scalar(out=sm[:, 1:2], in0=sm[:, 0:1], scalar1=1.0 / 256, scalar2=EPS, op0=ALU.mult, op1=ALU.add))
                            k.op('act', ['sm'], ['sm'], lambda e: e.activation(out=sm[:, 3:4], in_=sm[:, 1:2], func=AF.Sqrt))
                            k.op('dve', ['sm'], ['sm'], lambda e: e.reciprocal(out=sm[:, 2:3], in_=sm[:, 3:4]))
                    def sguB(q):
                            gv4 = gv[:].rearrange("p (a b d) -> p a b d", a=2, b=2)
                            gg4 = sgug[:].rearrange("p (a b d) -> p a b d", a=2, b=2)
                            for hh in range(2):
                                k.op('dve', ['tA1', 'sm', 'sgug'], ['vA'], lambda e, hh=hh: e.scalar_tensor_tensor(out=vA[:, :, hh, hh * 64:(hh + 1) * 64], in0=gv4[:, :, hh, :], scalar=sm[:, 2:3], in1=gg4[:, :, hh, :], op0=ALU.mult, op1=ALU.mult))
                    def sguC(q):
                            for pr in range(2):
                                pm, pmk = psum()
                                for hh in range(2):
                                    k.op('pe', ['vA', 'wm'], [pmk], lambda e, hh=hh, pr=pr, pm=pm: e.matmul(pm[:, 0:128], lhsT=vA[:, pr, hh, :], rhs=wm[:, 2 * pr + hh, :], start=(hh == 0), stop=(hh == 1)))
                                k.op('dve', [pmk, 'sgub'], ['tA0'], lambda e, pr=pr, pm=pm: e.tensor_tensor(out=tA[0][:, 0:128], in0=pm[:, 0:128], in1=sgub[:, pr, :], op=ALU.add))
                                k.op('pool', ['tA0', 'm2a'], ['ya%d' % pr], lambda e, pr=pr, q=q: e.tensor_tensor(out=ya[:, pr, q * 128:(q + 1) * 128], in0=tA[0][:, 0:128], in1=ug[:, pr, q * 128:(q + 1) * 128], op=ALU.mult))

                    sguA(0)
                    sguB(0)
                    for q in range(1, 4):
                        sguA(q)
                        fillers[q - 1]()
                        sguC(q - 1)
                        sguB(q)
                    fillers[3]()
                    sguC(3)

                    def s5_gen():
                        if tb > 0:
                            k.op('dve', ['wend', 's5s'], ['tA0'], lambda e: e.tensor_tensor(out=tA[0][:, 0:8], in0=wend[:, 0, :], in1=c_(C5), op=ALU.mult))
                            k.op('dve', ['wend', 's5s'], ['tA1'], lambda e: e.tensor_tensor(out=tA[1][:, 0:8], in0=wend[:, 1, :], in1=c_(S5_), op=ALU.mult))
                            k.op('dve', ['tA0', 'tA1'], ['winit'], lambda e: e.tensor_tensor(out=winit[:, 0, :], in0=tA[0][:, 0:8], in1=tA[1][:, 0:8], op=ALU.subtract))
                            k.op('dve', ['wend', 's5s'], ['tA0'], lambda e: e.tensor_tensor(out=tA[0][:, 0:8], in0=wend[:, 0, :], in1=c_(S5_), op=ALU.mult))
                            k.op('dve', ['wend', 's5s'], ['tA1'], lambda e: e.tensor_tensor(out=tA[1][:, 0:8], in0=wend[:, 1, :], in1=c_(C5), op=ALU.mult))
                            k.op('dve', ['tA0', 'tA1'], ['winit'], lambda e: e.tensor_tensor(out=winit[:, 1, :], in0=tA[0][:, 0:8], in1=tA[1][:, 0:8], op=ALU.add))
                        def s5_out(cc):
                            py, pyk = psum()
                            for jj in range(4):
                                j = cc * 4 + jj
                                k.op('pe', ['cre', 'sSr%d' % jj], [pyk], lambda e, j=j, jj=jj, py=py: e.matmul(py[:], lhsT=cre[:, j, :], rhs=sS[:, jj, 0, :], start=(jj == 0), stop=False))
                                k.op('pe', ['cimn', 'sSi%d' % jj], [pyk], lambda e, j=j, jj=jj, py=py: e.matmul(py[:], lhsT=cimn[:, j, :], rhs=sS[:, jj, 1, :], start=False, stop=(jj == 3)))
                            k.op('dve', [pyk, 'uf', 's5d'], ['tA0'], lambda e, cc=cc, py=py: e.scalar_tensor_tensor(out=tA[0][:], in0=uf[:, cc, :], scalar=s5d[:, l, cc:cc + 1], in1=py[:], op0=ALU.mult, op1=ALU.add))
                            k.op('act', ['tA0'], ['ygf'], lambda e, cc=cc: e.activation(out=ygf[:, cc, :], in_=tA[0][:], func=GELU))
                            k.op('pool', ['ygf'], ['ygb'], lambda e, cc=cc: e.tensor_copy(out=ygb[:, cc, :], in_=ygf[:, cc, :]))
                        for j in range(8):
                            cc = j // 4
                            jj = j % 4
                            pre, prk = psum()
                            pim, pik = psum()
                            k.op('pe', ['bT', 'ub'], [prk], lambda e, j=j, cc=cc, pre=pre: e.matmul(pre[:], lhsT=bT[:, 0, j, :], rhs=ub[:, cc, :], start=True, stop=True))
                            k.op('pe', ['bT', 'ub'], [pik], lambda e, j=j, cc=cc, pim=pim: e.matmul(pim[:], lhsT=bT[:, 1, j, :], rhs=ub[:, cc, :], start=True, stop=True))
                            cj, sj = cosT[:, j, :], sinT[:, j, :]
                            TT = lambda o, ok, a, ak, b, bk, op, en='dve': k.op(en, [ak, bk], [ok], lambda e: e.tensor_tensor(out=o, in0=a, in1=b, op=op))
                            TT(tA[0][:], 'tA0', pre[:], prk, cj, 'cosT', ALU.mult)
                            TT(tA[1][:], 'tA1', pim[:], pik, sj, 'sinT', ALU.mult)
                            TT(tA[0][:], 'tA0', tA[0][:], 'tA0', tA[1][:], 'tA1', ALU.add)
                            TT(tA[1][:], 'tA1', pim[:], pik, cj, 'cosT', ALU.mult)
                            TT(ug[:, 0, :], 'm2a', pre[:], prk, sj, 'sinT', ALU.mult)
                            TT(tA[1][:], 'tA1', tA[1][:], 'tA1', ug[:, 0, :], 'm2a', ALU.subtract)
                            rbc = s5s[:, MAG, j:j + 1].to_broadcast([128, TB])
                            k.op('dve', ['tA0', 's5s', 'winit'], ['tA2'], lambda e, j=j, rbc=rbc: e.tensor_tensor_scan(out=tA[2][:], data0=rbc, data1=tA[0][:], initial=winit[:, 0, j:j + 1], op0=ALU.mult, op1=ALU.add))
                            k.op('dve', ['tA1', 's5s', 'winit'], ['tA3'], lambda e, j=j, rbc=rbc: e.tensor_tensor_scan(out=tA[3][:], data0=rbc, data1=tA[1][:], initial=winit[:, 1, j:j + 1], op0=ALU.mult, op1=ALU.add))
                            k.op('pool', ['tA2'], ['wend'], lambda e, j=j: e.tensor_copy(out=wend[:, 0, j:j + 1], in_=tA[2][:, TB - 1:TB]))
                            k.op('pool', ['tA3'], ['wend'], lambda e, j=j: e.tensor_copy(out=wend[:, 1, j:j + 1], in_=tA[3][:, TB - 1:TB]))
                            TT(tA[4][:], 'tA4', tA[2][:], 'tA2', cj, 'cosT', ALU.mult, 'pool')
                            TT(tA[5][:], 'tA5', tA[3][:], 'tA3', sj, 'sinT', ALU.mult, 'pool')
                            TT(sS[:, jj, 0, :], 'sSr%d' % jj, tA[4][:], 'tA4', tA[5][:], 'tA5', ALU.subtract, 'pool')
                            TT(tA[0][:], 'tA0', tA[2][:], 'tA2', sj, 'sinT', ALU.mult)
                            TT(tA[1][:], 'tA1', tA[3][:], 'tA3', cj, 'cosT', ALU.mult)
                            TT(sS[:, jj, 1, :], 'sSi%d' % jj, tA[0][:], 'tA0', tA[1][:], 'tA1', ALU.add)
                            yield 10.0
                            if jj == 3:
                                s5_out(cc)
                        for oc in range(2):
                            pg, pgk = psum()
                            for kc in range(2):
                                k.op('pe', ['gluw', 'ygb'], [pgk], lambda e, kc=kc, oc=oc, pg=pg: e.matmul(pg[:], lhsT=gluw[:, kc, oc * 128:(oc + 1) * 128], rhs=ygb[:, kc, :], start=(kc == 0), stop=(kc == 1)))
                            k.op('act', [pgk, 'glub'], ['tA1'], lambda e, oc=oc, pg=pg: e.activation(out=tA[1][:], in_=pg[:], func=AF.Sigmoid, bias=glub[:, l, oc:oc + 1]))
                            k.op('dve', ['tA1', 'ygf'], ['ya%d' % (2 + oc)], lambda e, oc=oc: e.tensor_tensor(out=ya[:, 2 + oc, :], in0=ygf[:, oc, :], in1=tA[1][:], op=ALU.mult))
                        yield 2.0

                    for c in range(2):
                        k.op('dve', ['xbuf', 'cw', 'cb'], ['m2a'], lambda e, c=c: e.tensor_scalar(out=xc[:, c, :], in0=xbuf[:, c, 0:TB], scalar1=cw[:, l, c, 0:1], scalar2=cb[:, l, c:c + 1], op0=ALU.mult, op1=ALU.add))
                        for kk in range(1, 4):
                            k.op('dve', ['xbuf', 'cw', 'm2a'], ['m2a'], lambda e, c=c, kk=kk: e.scalar_tensor_tensor(out=xc[:, c, :], in0=xbuf[:, c, kk:kk + TB], scalar=cw[:, l, c, kk:kk + 1], in1=xc[:, c, :], op0=ALU.mult, op1=ALU.add))
                        k.op('pool', ['xbuf'], ['sm'], lambda e, c=c: e.tensor_copy(out=sm[:, 8:11], in_=xbuf[:, c, TB:TB + 3]))
                        k.op('pool', ['sm'], ['xbuf'], lambda e, c=c: e.tensor_copy(out=xbuf[:, c, 0:3], in_=sm[:, 8:11]))
                        k.op('pool', ['m2a'], ['b2a'], lambda e, c=c: e.tensor_copy(out=xcb[:, c, :], in_=xc[:, c, :]))
                        def evg(ps, pk, c=c):
                            k.op('act', [pk], ['tA5'], lambda e: e.activation(out=tA[5][:], in_=ps[:], func=GELU))
                        inproj_fm(1024 + c * 128, 128, evg)
                        if c == 0:
                            for h in range(4):
                                inproj_fm(1280 + h * 64, 64, lambda ps, pk, h=h: k.op('act', [pk], ['qT'], lambda e: e.activation(out=qT[0:64, h, :], in_=ps[0:64, :], func=AF.Copy, scale=0.125)))
                        else:
                            for h in range(4):
                                inproj_fm(1536 + h * 64, 64, lambda ps, pk, h=h: k.op('act', [pk], ['Kc'], lambda e: e.activation(out=Kc[0:64, h, t0:t0 + TB], in_=ps[0:64, :], func=AF.Copy)))
                        pr_, prk = psum()
                        pi_, pik = psum()
                        k.op('pe', ['wab', 'b2a'], [prk], lambda e, c=c, pr_=pr_: e.matmul(pr_[:], lhsT=wab[:, c, :], rhs=xcb[:, c, :], start=True, stop=True))
                        k.op('pe', ['wxb', 'b2a'], [pik], lambda e, c=c, pi_=pi_: e.matmul(pi_[:], lhsT=wxb[:, c, :], rhs=xcb[:, c, :], start=True, stop=True))
                        k.op('act', [prk, 'ba'], ['tA0'], lambda e, c=c, pr_=pr_: e.activation(out=tA[0][:], in_=pr_[:], func=AF.Sigmoid, bias=ba[:, l, c:c + 1]))
                        k.op('act', [pik, 'bx'], ['tA1'], lambda e, c=c, pi_=pi_: e.activation(out=tA[1][:], in_=pi_[:], func=AF.Sigmoid, bias=bx[:, l, c:c + 1]))
                        k.op('act', ['tA0', 'lsc'], ['tA2'], lambda e, c=c: e.activation(out=tA[2][:], in_=tA[0][:], func=AF.Exp, scale=lsc[:, 4 + c:5 + c]))
                        k.op('pool', ['tA2'], ['tA3'], lambda e: e.tensor_tensor(out=tA[3][:], in0=tA[2][:], in1=tA[2][:], op=ALU.mult))
                        k.op('pool', ['tA3'], ['tA3'], lambda e: e.tensor_scalar(out=tA[3][:], in0=tA[3][:], scalar1=-1.0, scalar2=1.0, op0=ALU.mult, op1=ALU.add))
                        k.op('act', ['tA3'], ['tA3'], lambda e: e.activation(out=tA[3][:], in_=tA[3][:], func=AF.Sqrt))
                        k.op('dve', ['tA1', 'm2a'], ['tA1'], lambda e, c=c: e.tensor_tensor(out=tA[1][:], in0=tA[1][:], in1=xc[:, c, :], op=ALU.mult))
                        k.op('dve', ['tA1', 'tA3'], ['tA1'], lambda e: e.tensor_tensor(out=tA[1][:], in0=tA[1][:], in1=tA[3][:], op=ALU.mult))
                        k.op('dve', ['tA2', 'tA1', 'hcar'], ['tA4'], lambda e, c=c: e.tensor_tensor_scan(out=tA[4][:], data0=tA[2][:], data1=tA[1][:], initial=hcar[:, c:c + 1], op0=ALU.mult, op1=ALU.add))
                        k.op('pool', ['tA4'], ['hcar'], lambda e, c=c: e.tensor_copy(out=hcar[:, c:c + 1], in_=tA[4][:, TB - 1:TB]))
                        k.op('dve', ['tA4', 'tA5'], ['ya%d' % (4 + c)], lambda e, c=c: e.tensor_tensor(out=ya[:, 4 + c, :], in0=tA[4][:], in1=tA[5][:], op=ALU.mult))

                    for q in range(4):
                        ps, pk = inproj_tm(q, 1792, 256)
                        k.op('act', [pk], ['Vc'], lambda e, q=q, ps=ps: e.activation(out=Vc[:, tb * 4 + q, :, 0:64], in_=ps[:, 0:256].rearrange("p (h d) -> p h d", h=4), func=AF.Copy))

                    def fox_gen():
                        pend = []
                        for h in range(4):
                            po, pok = psfix(6 + (h % 2))
                            nkb = 4 * tb + 4

                            def emitS(kb, h=h):
                                pss, psk = psum()
                                diag = kb >= 4 * tb
                                c0 = 128 * (kb - 4 * tb) if diag else 0
                                k.op('pe', ['Kc', 'qT'], [psk], lambda e: e.matmul(pss[:, c0:512], lhsT=Kc[0:65, h, kb * 128:(kb + 1) * 128], rhs=qT[0:65, h, c0:512], start=True, stop=(not diag)))
                                if diag:
                                    k.op('pe', ['ident_b', 'amask'], [psk], lambda e: e.matmul(pss[:, c0:c0 + 128], lhsT=ident_b[:], rhs=amask[:], start=False, stop=True, skip_group_check=True))
                                return pss, psk
                            sq_ = [emitS(0), emitS(1)]
                            for kb in range(nkb):
                                pss, psk = sq_.pop(0)
                                if kb + 2 < nkb:
                                    sq_.append(emitS(kb + 2))
                                if kb == 1 and pend:
                                    pend.pop()()
                                pt, ptk = pT[kb % 3], 'pT%d' % (kb % 3)
                                c0 = 128 * (kb - 4 * tb) if kb >= 4 * tb else 0
                                k.op('act', [psk, 'pcK'], [ptk], lambda e, kb=kb, h=h, pss=pss, pt=pt, c0=c0: e.activation(out=pt[:, c0:512], in_=pss[:, c0:512], func=AF.Exp, bias=pcK[:, kb, h:h + 1]))
                                k.op('pe', ['Vc', ptk], [pok], lambda e, kb=kb, h=h, pt=pt, po=po, nkb=nkb, c0=c0: e.matmul(po[0:65, c0:512], lhsT=Vc[:, kb, h, :], rhs=pt[:, c0:512], start=(kb == 0), stop=(kb == nkb - 1), skip_group_check=True))
                                yield 1.0
                            def finish(h=h, po=po, pok=pok):
                                pb, pbk = psum()
                                k.op('pe', ['ones_f', 'fT1'], [pbk], lambda e, pb=pb: e.matmul(pb[0:64, :], lhsT=ones_f[64:65, 0:64], rhs=fT[64:65, 1, :], start=True, stop=True))
                                k.op('act', [pbk], ['fT0'], lambda e, pb=pb: e.activation(out=fT[0:64, 0, :], in_=pb[0:64, :], func=AF.Copy))
                                k.op('dve', [pok, 'fT0'], ['yd%d' % h], lambda e, h=h, po=po: e.tensor_tensor(out=yd[:, h, :], in0=po[0:64, :], in1=fT[0:64, 0, :], op=ALU.mult))
                            k.op('act', [pok], ['fT1'], lambda e, po=po: e.activation(out=fT[64:65, 1, :], in_=po[64:65, :], func=AF.Ln))
                            k.op('act', ['fT1'], ['fT1'], lambda e: e.activation(out=fT[64:65, 1, :], in_=fT[64:65, 1, :], func=AF.Exp, scale=-1.0))
                            pend.append(finish)
                            yield 1.0
                        while pend:
                            pend.pop()()
                        yield 1.0

                    gens = [s5_gen(), fox_gen()]
                    tot = [82.0, 4.0 * (4 * tb + 5)]
                    prog = [0.0, 0.0]
                    alive = [True, True]
                    while any(alive):
                        i = min((i_ for i_ in range(2) if alive[i_]), key=lambda i_: prog[i_])
                        try:
                            prog[i] += next(gens[i]) / tot[i]
                        except StopIteration:
                            alive[i] = False

                    tsets = [((tA[3], 'tA3'), (tA[4], 'tA4'), (tA[5], 'tA5')), ((tA[0], 'tA0'), (tA[1], 'tA1'), (tA[2], 'tA2'))]
                    for g in range(3):
                        (s0_, s0k), (s1_, s1k), (rs_, rsk_) = tsets[g % 2]
                        pss, pk = psum()
                        for cc in range(2):
                            sqt, sqk = (s0_.bitcast(BF16)[:, 0:512], s0k) if cc == 0 else (s1_.bitcast(BF16)[:, 0:512], s1k)
                            k.op('act', ['ya%d' % (2 * g + cc)], [sqk], lambda e, g=g, cc=cc, sqt=sqt: e.activation(out=sqt[:], in_=ya[:, 2 * g + cc, :], func=AF.Square))
                            k.op('pe', [sqk, 'ones_b'], [pk], lambda e, cc=cc, sqt=sqt, pss=pss: e.matmul(pss[:], lhsT=ones_b[:], rhs=sqt[:], start=(cc == 0), stop=(cc == 1)))
                        k.op('act', [pk], [rsk_], lambda e, pss=pss, rs_=rs_: e.activation(out=rs_[:], in_=pss[:], func=AF.Sqrt, scale=1.0 / 256, bias=epsc[:, 0:1]))
                        k.op('dve', [rsk_], [rsk_], lambda e, rs_=rs_: e.reciprocal(out=rs_[:], in_=rs_[:]))
                        for cc in range(2):
                            ch = 2 * g + cc
                            k.op('dve', ['ya%d' % ch, rsk_, 'gma'], ['ya%d' % ch], lambda e, ch=ch, rs_=rs_: e.scalar_tensor_tensor(out=yn[:, ch, :], in0=ya[:, ch, :], scalar=gma[:, l, ch:ch + 1], in1=rs_[:], op0=ALU.mult, op1=ALU.mult))
                    (s0_, s0k), (s1_, s1k), (rs_, rsk_) = tsets[1]
                    pss, pk = psum()
                    for h in range(4):
                        sqt, sqk = (s0_.bitcast(BF16)[:, 0:512], s0k) if h % 2 == 0 else (s1_.bitcast(BF16)[:, 0:512], s1k)
                        k.op('act', ['yd%d' % h], [sqk], lambda e, h=h, sqt=sqt: e.activation(out=sqt[0:64, :], in_=yd[:, h, :], func=AF.Square))
                        k.op('pe', [sqk, 'ones_b'], [pk], lambda e, h=h, sqt=sqt, pss=pss: e.matmul(pss[:], lhsT=ones_b[0:64, :], rhs=sqt[0:64, :], start=(h == 0), stop=(h == 3)))
                    k.op('act', [pk], [rsk_], lambda e, pss=pss, rs_=rs_: e.activation(out=rs_[:], in_=pss[:], func=AF.Sqrt, scale=1.0 / 256, bias=epsc[:, 0:1]))
                    k.op('dve', [rsk_], [rsk_], lambda e, rs_=rs_: e.reciprocal(out=rs_[:], in_=rs_[:]))
                    for h in range(4):
                        k.op('dve', ['yd%d' % h, rsk_, 'gmd'], ['yd%d' % h], lambda e, h=h, rs_=rs_: e.scalar_tensor_tensor(out=ynd[:, h, :], in0=yd[:, h, :], scalar=gmd[:, l, h:h + 1], in1=rs_[0:64, :], op0=ALU.mult, op1=ALU.mult))
                    for dc in range(8):
                        wa_, wak = woa[dc % 2], 'woa%d' % (dc % 2)
                        wd_, wdk = wod[dc % 2], 'wod%d' % (dc % 2)
                        k.dma('sp', wa_[:].rearrange("p a b -> p (a b)"), wboa[l, dc], [('wboa', l)], [wak])
                        k.dma('sp', wd_[:].rearrange("p a b -> p (a b)"), wbod[l, dc], [('wbod', l)], [wdk])
                        po, pok = psum()
                        for kc in range(6):
                            k.op('pe', [wak, 'ya%d' % kc], [pok], lambda e, kc=kc, po=po, wa_=wa_: e.matmul(po[:], lhsT=wa_[:, kc, :], rhs=yn[:, kc, :], start=(kc == 0), stop=False))
                        for h in range(4):
                            k.op('pe', [wdk, 'yd%d' % h], [pok], lambda e, h=h, po=po, wd_=wd_: e.matmul(po[:], lhsT=wd_[:, h, :], rhs=ynd[:, h, :], start=False, stop=(h == 3)))
                        k.op('dve', [pok, 'xb%d' % dc], ['xb%d' % dc], lambda e, dc=dc, po=po: e.tensor_tensor(out=xb[:, dc, :], in0=xb[:, dc, :], in1=po[:], op=ALU.add))
                        k.dma('pool', xs1[dc * 128:(dc + 1) * 128, t0:t0 + TB], xb[:, dc, :], ['xb%d' % dc], [])
                k.barrier()

            with ExitStack() as eb:
                B = lambda name, shape, dt=F32: sb(eb, name, shape, dt)
                x1 = B("x1", [128, 9, TF]); h2 = B("h2", [128, 8, TF], BF16)
                hid = B("hid", [128, 32, TF], BF16)
                w1t = [B("w1t%d" % i, [128, 8, 512], BF16) for i in range(2)]
                w2t = [B("w2t%d" % i, [128, 16, 128], BF16) for i in range(3)]
                rl = [gls[:].bitcast(BF16)[:, i, :] for i in range(2)]
                sq0, sq1, rs = tS[0], tS[1], tS[2]
                for tf in range(NBF):
                    t0 = tf * TF
                    bi = lambda c_, tf=tf: (c_ - tf) % 9
                    bin_ = lambda c_, tf=tf: (c_ - tf - 1) % 9
                    prefetch = (l != nl - 1) and (tf + 1 < NBF)
                    if not (l != nl - 1 and tf > 0):
                        for c8 in range(8):
                            k.dma('sp', x1[:, bi(c8), :], xs1[c8 * 128:(c8 + 1) * 128, t0:t0 + TF], [], ['x1_%d' % bi(c8)])

                    class XV:
                        def __getitem__(s_, idx):
                            return x1[idx[0], bi(idx[1]), idx[2]]
                    rmsnorm(((sq0, sq1), rs), (('tS0', 'tS1'), 'tS2'), XV(), (lambda c_: 'x1_%d' % bi(c_)), 8, TF, g2[:, l, :], h2, (lambda c_: 'h2_%d' % c_), D, eng2='dve')
                    for fg in range(8):
                        wt, wk = w1t[fg % 2], 'w1t%d' % (fg % 2)
                        k.dma('sp', wt[:].rearrange("p a b -> p (a b)"), wb1[l, fg], [('wb1', l, fg // 4)], [wk])
                        for fc in range(4):
                            for sbk in range(TF // 512):
                                ps, pk = psum()
                                for kc in range(8):
                                    k.op('pe', [wk, 'h2_%d' % kc], [pk], lambda e, kc=kc, fc=fc, sbk=sbk, wt=wt, ps=ps: e.matmul(ps[:], lhsT=wt[:, kc, fc * 128:(fc + 1) * 128], rhs=h2[:, kc, sbk * 512:(sbk + 1) * 512], start=(kc == 0), stop=(kc == 7)))
                                f = fg * 4 + fc
                                rt, rk_ = rl[(f * 2 + sbk) % 2], 'rl%d' % ((f * 2 + sbk) % 2)
                                k.op('act', [pk], [rk_], lambda e, ps=ps, rt=rt: e.activation(out=rt[:], in_=ps[:], func=AF.Relu))
                                k.op('dve', [rk_], ['hid%d' % f], lambda e, f=f, sbk=sbk, rt=rt: e.tensor_tensor(out=hid[:, f, sbk * 512:(sbk + 1) * 512], in0=rt[:], in1=rt[:], op=ALU.mult))
                    if tf == 0 and l + 1 < nl:
                        conv(l + 1)
                        setup(l + 1)
                    hk = ['hid%d' % f for f in range(32)]
                    if prefetch:
                        k.dma('act', x1[:, bin_(0), :], xs1[0:128, t0 + TF:t0 + 2 * TF], [], ['x1_%d' % bin_(0)])
                    for dc in range(8):
                        pss2 = [psum() for _ in range(TF // 512)]
                        for hf in range(2):
                            wi_ = (dc * 2 + hf) % 3
                            wt, wk = w2t[wi_], 'w2t%d' % wi_
                            k.dma('sp', wt[:].rearrange("p a b -> p (a b)"), wb2[l, dc][:, hf * 2048:(hf + 1) * 2048], [('wb2', l, dc // 4)], [wk])
                            for sbk in range(TF // 512):
                                ps, pk = pss2[sbk]
                                for f16 in range(16):
                                    f = hf * 16 + f16
                                    k.op('pe', [wk] + (hk if f == 0 else []), [pk], lambda e, f=f, f16=f16, sbk=sbk, wt=wt, ps=ps: e.matmul(ps[:], lhsT=wt[:, f16, :], rhs=hid[:, f, sbk * 512:(sbk + 1) * 512], start=(f == 0), stop=(f == 31)))
                        for sbk in range(TF // 512):
                            ps, pk = pss2[sbk]
                            k.op('dve', [pk, 'x1_%d' % bi(dc)], ['x1_%d' % bi(dc)], lambda e, dc=dc, sbk=sbk, ps=ps: e.tensor_tensor(out=x1[:, bi(dc), sbk * 512:(sbk + 1) * 512], in0=x1[:, bi(dc), sbk * 512:(sbk + 1) * 512], in1=ps[:], op=ALU.add))
                        if l != nl - 1:
                            k.dma('pool', xs[dc * 128:(dc + 1) * 128, t0:t0 + TF], x1[:, bi(dc), :], ['x1_%d' % bi(dc)], [])
                            if prefetch and dc + 1 < 8:
                                k.dma('act', x1[:, bin_(dc + 1), :], xs1[(dc + 1) * 128:(dc + 2) * 128, t0 + TF:t0 + 2 * TF], [], ['x1_%d' % bin_(dc + 1)])
                    if l == nl - 1:
                        for sbk in range(TF // 512):
                            class _V:
                                def __getitem__(s_, idx):
                                    return x1[idx[0], bi(idx[1]), sbk * 512 + idx[2].start: sbk * 512 + idx[2].stop]
                            rmsnorm(((sq0, sq1), rs), (('tS0', 'tS1'), 'tS2'), _V(), (lambda c_: 'x1_%d' % bi(c_)), 8, 512, gf, _V(), (lambda c_: 'x1_%d' % bi(c_)), D, eng2='dve')
                            for c8 in range(8):
                                k.dma('sp', outT[c8 * 128:(c8 + 1) * 128, t0 + sbk * 512:t0 + (sbk + 1) * 512], x1[:, bi(c8), sbk * 512:(sbk + 1) * 512], ['x1_%d' % bi(c8)], [])
                    else:
                        pass
                k.barrier()
        for i in range(NDS):
            if k.dcnt[i] > 0:
                k._wait('sp', i, k.dcnt[i])
    return nc


def _prep(inputs, b):
    f = lambda a: np.ascontiguousarray(np.asarray(a, dtype=np.float32))
    I = {k_: np.asarray(v, dtype=np.float32) for k_, v in inputs.items()}
    m = {}
    m["xT"] = f(I["x"][b].T)
    wi_ = I["w_in"]
    m["w_in_t"] = f(wi_[:, :, :2048].reshape(L, 8, 128, 8, 256).transpose(0, 3, 2, 1, 4).reshape(L, 8, 128, 2048))
    m["wif_t"] = f(wi_[:, :, 2048:2052].reshape(L, 8, 128, 4).transpose(0, 2, 1, 3).reshape(L, 128, 32))
    wo_ = I["w_out"]
    m["woa_t"] = f(wo_[:, 0:768, :].reshape(L, 6, 128, 8, 128).transpose(0, 3, 2, 1, 4).reshape(L, 8, 128, 768))
    m["wod_t"] = f(wo_[:, 768:1024, :].reshape(L, 4, 64, 8, 128).transpose(0, 3, 2, 1, 4).reshape(L, 8, 64, 512))
    m["w1_t"] = f(I["w_mlp_in"].reshape(L, 8, 128, 8, 512).transpose(0, 3, 2, 1, 4).reshape(L, 8, 128, 4096))
    m["w2_t"] = f(I["w_mlp_out"].reshape(L, 32, 128, 8, 128).transpose(0, 3, 2, 1, 4).reshape(L, 8, 128, 4096))
    pl = lambda a, nchunk, p=128: f(a.reshape(L, nchunk, p).transpose(2, 0, 1))
    m["g1"] = pl(I["norm1_g"], 8); m["g2"] = pl(I["norm2_g"], 8)
    m["gma"] = pl(I["mix_norm_g"][:, :768], 6); m["gmd"] = pl(I["mix_norm_g"][:, 768:], 4, 64)
    m["gf"] = f(I["final_g"].reshape(8, 128).T)
    m["sgug"] = f(np.broadcast_to(I["sgu_norm_g"][None], (128, L, 256)))
    m["sguw"] = f(I["sgu_w"].transpose(0, 3, 1, 2))
    sb_ = I["sgu_b"].reshape(L, 2, 2, 128)
    m["sgub"] = f(np.broadcast_to(sb_.transpose(2, 0, 1, 3)[:, None], (2, 64, L, 2, 128)).reshape(128, L, 2, 128))
    st = lambda a: f(a.reshape(L, 8, 2, 64).transpose(2, 3, 0, 1).reshape(128, L, 8))
    m["lre"] = st(I["s5_lambda_re"]); m["lim"] = st(I["s5_lambda_im"])
    m["ldt"] = st(np.broadcast_to(I["s5_log_dt"][:, :, None], (L, 16, 64)))

    def padb(a):
        o = np.zeros((L, 2, 64, 8, 128), np.float32)
        for j in range(8):
            for gl in range(2):
                c0 = 32 * (j % 4) + 16 * gl
                o[:, gl, :, j, c0:c0 + 16] = a[:, 2 * j + gl]
        return o.reshape(L, 128, 8, 128)
    m["bre"] = padb(I["s5_b_re"]); m["bim"] = padb(I["s5_b_im"])
    m["cre"] = padb(I["s5_c_re"].transpose(0, 1, 3, 2)); m["cim"] = padb(I["s5_c_im"].transpose(0, 1, 3, 2))
    m["s5d"] = pl(I["s5_d"], 2); m["gluw"] = f(I["s5_glu_w"]); m["glub"] = pl(I["s5_glu_b"], 2)
    m["cw"] = f(I["lru_conv_w"].reshape(L, 4, 2, 128).transpose(3, 0, 2, 1))
    m["cb"] = pl(I["lru_conv_b"], 2)

    def bd(a):
        o = np.zeros((L, 2, 64, 2, 2, 64), np.float32)
        for c in range(2):
            for hl in range(2):
                o[:, hl, :, c, hl, :] = a[:, 2 * c + hl]
        return o.reshape(L, 128, 2, 128)
    m["wa"] = bd(I["lru_wa"]); m["wx"] = bd(I["lru_wx"])
    m["ba"] = pl(I["lru_ba"].reshape(L, 256), 2); m["bx"] = pl(I["lru_bx"].reshape(L, 256), 2)
    m["llam"] = pl(I["lru_lambda"], 2)
    m["bfr"] = f(np.broadcast_to(np.tile(I["fox_fgate_b"], (1, 4))[None], (128, L, 16)))
    m["ident"] = np.eye(128, dtype=np.float32)
    m["tri"] = np.triu(np.ones((128, 128), np.float32))
    kk = np.arange(128)[:, None, None] + 128 * np.arange(4)[None, :, None]
    m["amask"] = np.where(kk <= np.arange(512)[None, None, :], 0.0, NEG).astype(np.float32)
    m["iota"] = f(np.broadcast_to(np.arange(512, dtype=np.float32)[None], (128, 512)))
    return m


def kernel(**inputs):
    nc = build(L)
    in_maps = [_prep(inputs, b) for b in range(NCORE)]
    res = run_bass_kernel_spmd(nc, in_maps, core_ids=list(range(NCORE)))
    out = np.stack([np.asarray(res.results[b]["outT"], dtype=np.float32).T for b in range(NCORE)], 0)
    return np.ascontiguousarray(out)
```
